# Optimizing a Trainium2 kernel written in Bass

```python
import math
import jax
import jax.numpy as jnp
from jax import lax
import numpy as np

D_MODEL = 2048
BATCH = 2
SEQ = 4096
DEPTH = 2

GRID_W = 64
CTX_LEN = 256
N_BRANCH = 3
DA_HEADS = 8
DA_HEAD_DIM = 64
DA_V_DIM = 2 * DA_HEAD_DIM
DA_WIDTH = DA_HEADS * DA_V_DIM
Q_BLOCK = 128
ROPE_THETA = 10000.0
SSD_HEADS = 16
SSD_HEAD_DIM = 64
SSD_GROUPS = 2
SSD_HPG = SSD_HEADS // SSD_GROUPS
SSD_STATE = 128
SSD_WIDTH = SSD_HEADS * SSD_HEAD_DIM
SSD_XBC = SSD_WIDTH + 2 * SSD_GROUPS * SSD_STATE
SSD_CONV = 5
SSD_CHUNK = 128
S5_GROUP = 16
S5_GROUPS = 64
S5_WIDTH = S5_GROUPS * S5_GROUP
S5_STATE = 64
D_FF = int(math.ceil(8 * D_MODEL / 3 / 256)) * 256
Q_OFF = 0
K_OFF = Q_OFF + DA_WIDTH
V_OFF = K_OFF + DA_WIDTH
Z_OFF = V_OFF + DA_WIDTH
XBC_OFF = Z_OFF + SSD_WIDTH
DT_OFF = XBC_OFF + SSD_XBC
U_OFF = DT_OFF + 2 * SSD_HEADS
GATE_OFF = U_OFF + S5_WIDTH
N_IN = GATE_OFF + N_BRANCH * D_MODEL
RMS_EPS = 1e-6

kernel_name = 'hybrid_diffattn_ssd_s5_block'


def rms_norm(x, g):
    xf = x.astype(jnp.float32)
    y = xf * lax.rsqrt(jnp.mean(xf * xf, axis=-1, keepdims=True) + RMS_EPS)
    return (y * g.astype(jnp.float32)).astype(x.dtype)


def modulate(h, shift, scale):
    return h * (1 + scale) + shift


def swiglu(h, w_gate, w_up, w_down):
    return (jax.nn.silu(h @ w_gate) * (h @ w_up)) @ w_down


def flip_seq(t):
    return jnp.flip(t, axis=1)


def axial_rope_tables(n_tokens):
    n_rows = n_tokens // GRID_W
    row = jnp.repeat(jnp.arange(n_rows, dtype=jnp.float32), GRID_W)
    col = jnp.tile(jnp.arange(GRID_W, dtype=jnp.float32), n_rows)
    half = DA_HEAD_DIM // 2
    inv_freq = ROPE_THETA ** (-jnp.arange(0, half, 2, dtype=jnp.float32) / half)

    def axis_angles(pos):
        a = pos[:, None] * inv_freq[None, :]
        return jnp.concatenate([a, a], axis=-1)

    ang = jnp.concatenate([axis_angles(row), axis_angles(col)], axis=-1)
    return jnp.cos(ang), jnp.sin(ang)


def rotate_half(x):
    x1, x2 = jnp.split(x, 2, axis=-1)
    return jnp.concatenate([-x2, x1], axis=-1)


def apply_axial_rope(x, cos, sin):
    half = DA_HEAD_DIM // 2
    rot = jnp.concatenate([rotate_half(x[..., :half]), rotate_half(x[..., half:])], axis=-1)
    c = cos[None, :, None, None, :]
    s = sin[None, :, None, None, :]
    return (x.astype(jnp.float32) * c + rot.astype(jnp.float32) * s).astype(x.dtype)


def diff_attn_block(q, k, v, lam):
    s = jnp.einsum('bqhcd,bkhcd->bhcqk', q, k).astype(jnp.float32) * (DA_HEAD_DIM ** -0.5)
    p = jax.nn.softmax(s, axis=-1)
    a = p[:, :, 0] - lam * p[:, :, 1]
    return jnp.einsum('bhqk,bkhe->bqhe', a.astype(v.dtype), v)


def diff_attention_sweep(q, k, v, lam):
    b, n_q = q.shape[:2]
    n_blk = n_q // Q_BLOCK
    qb = jnp.moveaxis(q.reshape(b, n_blk, Q_BLOCK, DA_HEADS, 2, DA_HEAD_DIM), 1, 0)
    ob = lax.map(lambda qq: diff_attn_block(qq, k, v, lam), qb)
    return jnp.moveaxis(ob, 0, 1).reshape(b, n_q, DA_HEADS, DA_V_DIM)


def depthwise_conv_centred(x, w, b):
    pad = SSD_CONV // 2
    y = lax.conv_general_dilated(x, w[:, None, :], window_strides=(1,), padding=[(pad, pad)],
                                 dimension_numbers=('NWC', 'WIO', 'NWC'),
                                 feature_group_count=x.shape[-1])
    return y + b


def ssd_inputs(xbc_raw, dt_raw, conv_w, conv_b, dt_bias):
    b, n = xbc_raw.shape[:2]
    xbc = jax.nn.silu(depthwise_conv_centred(xbc_raw, conv_w, conv_b)).astype(jnp.float32)
    gn = SSD_GROUPS * SSD_STATE
    xs = xbc[..., :SSD_WIDTH].reshape(b, n, SSD_GROUPS, SSD_HPG, SSD_HEAD_DIM)
    bm = xbc[..., SSD_WIDTH:SSD_WIDTH + gn].reshape(b, n, SSD_GROUPS, SSD_STATE)
    cm = xbc[..., SSD_WIDTH + gn:].reshape(b, n, SSD_GROUPS, SSD_STATE)
    dt = jax.nn.softplus(dt_raw.astype(jnp.float32).reshape(b, n, 2, SSD_HEADS)
                         + dt_bias.astype(jnp.float32))
    return xs, bm, cm, dt.reshape(b, n, 2, SSD_GROUPS, SSD_HPG)


def ssd_chunked_scan(xs, dt, a, bm, cm, h0):
    b, n = xs.shape[:2]
    nc = n // SSD_CHUNK
    q = SSD_CHUNK
    xs = xs.reshape(b, nc, q, SSD_GROUPS, SSD_HPG, SSD_HEAD_DIM)
    dt = dt.reshape(b, nc, q, SSD_GROUPS, SSD_HPG)
    bm = bm.reshape(b, nc, q, SSD_GROUPS, SSD_STATE)
    cm = cm.reshape(b, nc, q, SSD_GROUPS, SSD_STATE)
    acum = jnp.cumsum(dt * a, axis=2)
    xdt = xs * dt[..., None]
    ac = jnp.moveaxis(acum, 2, -1)
    seg = ac[..., :, None] - ac[..., None, :]
    lower = jnp.tril(jnp.ones((q, q), dtype=bool))
    decay = jnp.exp(jnp.where(lower, seg, -jnp.inf))
    cb = jnp.einsum('bcign,bcjgn->bcgij', cm, bm)
    y_diag = jnp.einsum('bcgkij,bcjgkp->bcigkp', cb[:, :, :, None] * decay, xdt)
    decay_end = jnp.exp(acum[:, :, -1:] - acum)
    states = jnp.einsum('bcjgn,bcjgkp->bcgkpn', bm, xdt * decay_end[..., None])
    chunk_decay = jnp.exp(acum[:, :, -1])

    def step(h, inp):
        st, dec = inp
        return h * dec[..., None, None] + st, h

    h_final, h_prev = lax.scan(step, h0, (jnp.moveaxis(states, 1, 0), jnp.moveaxis(chunk_decay, 1, 0)))
    h_prev = jnp.moveaxis(h_prev, 0, 1)
    y_off = jnp.einsum('bcign,bcgkpn->bcigkp', cm, h_prev) * jnp.exp(acum)[..., None]
    return (y_diag + y_off).reshape(b, n, SSD_GROUPS, SSD_HPG, SSD_HEAD_DIM), h_final


def ssd_bidir(xs, bm, cm, dt, a, h0_f, h0_b):
    y_f, h_f = ssd_chunked_scan(xs, dt[:, :, 0], a[0], bm, cm, h0_f)
    y_b, h_b = ssd_chunked_scan(flip_seq(xs), flip_seq(dt[:, :, 1]), a[1],
                                flip_seq(bm), flip_seq(cm), h0_b)
    return y_f + flip_seq(y_b), h_f, h_b


def s5_discretise(lam_re, lam_im, log_step, b_re, b_im):
    lam = lax.complex(lam_re.astype(jnp.float32), lam_im.astype(jnp.float32))
    delta = jnp.exp(log_step.astype(jnp.float32))[..., None]
    lam_bar = jnp.exp(lam * delta)
    bmat = lax.complex(b_re.astype(jnp.float32), b_im.astype(jnp.float32))
    b_bar = ((lam_bar - 1.0) / lam)[..., None] * bmat[None]
    return lam_bar, b_bar


def _linear_recurrence(left, right):
    a_l, b_l = left
    a_r, b_r = right
    return a_l * a_r, a_r * b_l + b_r


def s5_scan(u, lam_bar, b_bar, h0):
    bu = jnp.einsum('gpe,blge->blgp', b_bar, u.astype(jnp.float32).astype(jnp.complex64))
    bu = bu.at[:, 0].add(lam_bar * h0)
    a = jnp.broadcast_to(lam_bar, bu.shape)
    _, h = lax.associative_scan(_linear_recurrence, (a, bu), axis=1)
    return h, h[:, -1]


def s5_bidir(u, lam_bar, b_bar, h0_f, h0_b):
    h_f, last_f = s5_scan(u, lam_bar[0], b_bar[0], h0_f)
    h_b, last_b = s5_scan(flip_seq(u), lam_bar[1], b_bar[1], h0_b)
    return h_f + flip_seq(h_b), last_f, last_b


def token_mixing(h_lat, h_ctx, need_ctx, lam_init, w_in, da_lambda, da_subln,
                 conv_w, conv_b, dt_bias, a_log, ssd_d, ssd_norm,
                 lam_re, lam_im, log_step, b_re, b_im, c_re, c_im, s5_d, glu_w, glu_b,
                 w_branch, w_out):
    dtype = h_lat.dtype
    b, n_lat = h_lat.shape[:2]
    p_lat = h_lat @ w_in
    p_ctx = h_ctx @ w_in

    lf = da_lambda.astype(jnp.float32)
    lam = jnp.exp(jnp.sum(lf[0] * lf[1])) - jnp.exp(jnp.sum(lf[2] * lf[3])) + lam_init

    def qkv(p):
        m = p.shape[1]
        q = p[..., Q_OFF:K_OFF].reshape(b, m, DA_HEADS, 2, DA_HEAD_DIM)
        k = p[..., K_OFF:V_OFF].reshape(b, m, DA_HEADS, 2, DA_HEAD_DIM)
        v = p[..., V_OFF:Z_OFF].reshape(b, m, DA_HEADS, DA_V_DIM)
        return q, k, v

    def attn_post(o):
        return (rms_norm(o, da_subln) * (1.0 - lam_init)).reshape(o.shape[0], o.shape[1], DA_WIDTH)

    cos, sin = axial_rope_tables(n_lat)
    q_l, k_l, v_l = qkv(p_lat)
    q_l = apply_axial_rope(q_l, cos, sin)
    k_l = apply_axial_rope(k_l, cos, sin)
    q_c, k_c, v_c = qkv(p_ctx)
    y_attn_lat = attn_post(diff_attention_sweep(q_l, jnp.concatenate([k_c, k_l], axis=1),
                                                jnp.concatenate([v_c, v_l], axis=1), lam))

    a = -jnp.exp(a_log.astype(jnp.float32)).reshape(2, SSD_GROUPS, SSD_HPG)
    d_ssd = ssd_d.astype(jnp.float32).reshape(SSD_GROUPS, SSD_HPG, 1)

    def ssd_post(y, xs, p):
        y = (y + d_ssd * xs).reshape(b, y.shape[1], SSD_WIDTH).astype(dtype)
        return rms_norm(y * jax.nn.silu(p[..., Z_OFF:XBC_OFF]), ssd_norm)

    xs_c, bm_c, cm_c, dt_c = ssd_inputs(p_ctx[..., XBC_OFF:DT_OFF], p_ctx[..., DT_OFF:U_OFF],
                                        conv_w, conv_b, dt_bias)
    h_zero = jnp.zeros((b, SSD_GROUPS, SSD_HPG, SSD_HEAD_DIM, SSD_STATE), jnp.float32)
    y_ssd_c, hf_c, hb_c = ssd_bidir(xs_c, bm_c, cm_c, dt_c, a, h_zero, h_zero)
    xs_l, bm_l, cm_l, dt_l = ssd_inputs(p_lat[..., XBC_OFF:DT_OFF], p_lat[..., DT_OFF:U_OFF],
                                        conv_w, conv_b, dt_bias)
    y_ssd_l, _, _ = ssd_bidir(xs_l, bm_l, cm_l, dt_l, a, hf_c, hb_c)
    y_ssd_lat = ssd_post(y_ssd_l, xs_l, p_lat)

    lam_bar, b_bar = s5_discretise(lam_re, lam_im, log_step, b_re, b_im)
    cmat = lax.complex(c_re.astype(jnp.float32), c_im.astype(jnp.float32))
    d_s5 = s5_d.astype(jnp.float32).reshape(S5_GROUPS, S5_GROUP)

    def s5_post(hs, u):
        y = jnp.real(jnp.einsum('gep,blgp->blge', cmat, hs)) + d_s5 * u.astype(jnp.float32)
        y = jax.nn.gelu(y.reshape(b, u.shape[1], S5_WIDTH).astype(dtype))
        ab = y @ glu_w + glu_b
        return ab[..., :S5_WIDTH] * jax.nn.sigmoid(ab[..., S5_WIDTH:])

    u_c = p_ctx[..., U_OFF:GATE_OFF].reshape(b, h_ctx.shape[1], S5_GROUPS, S5_GROUP)
    s_zero = jnp.zeros((b, S5_GROUPS, S5_STATE), jnp.complex64)
    hs_c, lf_c, lb_c = s5_bidir(u_c, lam_bar, b_bar, s_zero, s_zero)
    u_l = p_lat[..., U_OFF:GATE_OFF].reshape(b, n_lat, S5_GROUPS, S5_GROUP)
    hs_l, _, _ = s5_bidir(u_l, lam_bar, b_bar, lf_c, lb_c)
    y_s5_lat = s5_post(hs_l, u_l)

    def merge(p, y_a, y_b, y_c):
        m = p.shape[1]
        ys = jnp.stack([y_a, y_b, y_c], axis=2)
        branches = jnp.einsum('blnw,nwd->blnd', ys, w_branch)
        gates = jax.nn.sigmoid(p[..., GATE_OFF:].reshape(b, m, N_BRANCH, D_MODEL))
        return jnp.sum(gates * branches, axis=2) @ w_out

    out_lat = merge(p_lat, y_attn_lat, y_ssd_lat, y_s5_lat)
    out_ctx = None
    if need_ctx:
        y_attn_c = attn_post(diff_attn_block(q_c, k_c, v_c, lam))
        out_ctx = merge(p_ctx, y_attn_c, ssd_post(y_ssd_c, xs_c, p_ctx), s5_post(hs_c, u_c))
    return out_lat, out_ctx


def setup_inputs(seed: int = 0) -> dict:
    key = jax.random.key(seed)
    ks = iter(jax.random.split(key, 40))

    def nrm(shape, scale):
        return jax.random.normal(next(ks), shape, jnp.float32) * scale

    def gain(shape):
        return 1.0 + nrm(shape, 0.02)

    def unif(shape, lo, hi):
        return jax.random.uniform(next(ks), shape, jnp.float32, lo, hi)

    L = DEPTH
    x = nrm((BATCH, SEQ, D_MODEL), 1.0)
    c = nrm((BATCH, D_MODEL), 1.0)
    ctx = nrm((BATCH, CTX_LEN, D_MODEL), 1.0)
    c_ctx = nrm((D_MODEL,), 1.0)
    ada_w = nrm((L, D_MODEL, 6 * D_MODEL), 0.5 * D_MODEL ** -0.5)
    ada_b = nrm((L, 6 * D_MODEL), 0.02)
    norm_mix_pre = gain((L, D_MODEL))
    norm_mix_post = gain((L, D_MODEL))
    norm_ffn_pre = gain((L, D_MODEL))
    norm_ffn_post = gain((L, D_MODEL))
    w_in = nrm((L, D_MODEL, N_IN), D_MODEL ** -0.5)
    da_lambda = nrm((L, 4, DA_HEAD_DIM), 0.1)
    da_subln = gain((L, DA_V_DIM))
    ssd_conv_w = nrm((L, SSD_CONV, SSD_XBC), SSD_CONV ** -0.5)
    ssd_conv_b = nrm((L, SSD_XBC), 0.01)
    dt0 = jnp.exp(unif((L, 2, SSD_HEADS), math.log(1e-3), math.log(1e-1)))
    ssd_dt_bias = dt0 + jnp.log(-jnp.expm1(-dt0))
    ssd_a_log = jnp.log(unif((L, 2, SSD_HEADS), 1.0, 16.0))
    ssd_d = gain((L, SSD_HEADS))
    ssd_norm = gain((L, SSD_WIDTH))
    s5_lam_re = -0.5 + nrm((L, 2, S5_GROUPS, S5_STATE), 0.01)
    s5_lam_im = (math.pi * jnp.arange(S5_STATE, dtype=jnp.float32)) + nrm((L, 2, S5_GROUPS, S5_STATE), 0.01)
    s5_log_step = unif((L, 2, S5_GROUPS), math.log(1e-3), math.log(1e-1))
    s5_b_re = nrm((L, S5_GROUPS, S5_STATE, S5_GROUP), (2.0 * S5_GROUP) ** -0.5)
    s5_b_im = nrm((L, S5_GROUPS, S5_STATE, S5_GROUP), (2.0 * S5_GROUP) ** -0.5)
    s5_c_re = nrm((L, S5_GROUPS, S5_GROUP, S5_STATE), (2.0 * S5_STATE) ** -0.5)
    s5_c_im = nrm((L, S5_GROUPS, S5_GROUP, S5_STATE), (2.0 * S5_STATE) ** -0.5)
    s5_d = nrm((L, S5_WIDTH), 1.0)
    s5_glu_w = nrm((L, S5_WIDTH, 2 * S5_WIDTH), S5_WIDTH ** -0.5)
    s5_glu_b = nrm((L, 2 * S5_WIDTH), 0.01)
    w_branch = nrm((L, N_BRANCH, DA_WIDTH, D_MODEL), DA_WIDTH ** -0.5)
    w_out = nrm((L, D_MODEL, D_MODEL), D_MODEL ** -0.5)
    ffn_w_gate = nrm((L, D_MODEL, D_FF), D_MODEL ** -0.5)
    ffn_w_up = nrm((L, D_MODEL, D_FF), D_MODEL ** -0.5)
    ffn_w_down = nrm((L, D_FF, D_MODEL), D_FF ** -0.5)
    return {'x': x, 'c': c, 'ctx': ctx, 'c_ctx': c_ctx, 'ada_w': ada_w, 'ada_b': ada_b,
            'norm_mix_pre': norm_mix_pre, 'norm_mix_post': norm_mix_post,
            'norm_ffn_pre': norm_ffn_pre, 'norm_ffn_post': norm_ffn_post,
            'w_in': w_in, 'da_lambda': da_lambda, 'da_subln': da_subln,
            'ssd_conv_w': ssd_conv_w, 'ssd_conv_b': ssd_conv_b, 'ssd_dt_bias': ssd_dt_bias,
            'ssd_a_log': ssd_a_log, 'ssd_d': ssd_d, 'ssd_norm': ssd_norm,
            's5_lam_re': s5_lam_re, 's5_lam_im': s5_lam_im, 's5_log_step': s5_log_step,
            's5_b_re': s5_b_re, 's5_b_im': s5_b_im, 's5_c_re': s5_c_re, 's5_c_im': s5_c_im,
            's5_d': s5_d, 's5_glu_w': s5_glu_w, 's5_glu_b': s5_glu_b,
            'w_branch': w_branch, 'w_out': w_out,
            'ffn_w_gate': ffn_w_gate, 'ffn_w_up': ffn_w_up, 'ffn_w_down': ffn_w_down}


def reference(x, c, ctx, c_ctx, ada_w, ada_b, norm_mix_pre, norm_mix_post, norm_ffn_pre, norm_ffn_post,
              w_in, da_lambda, da_subln, ssd_conv_w, ssd_conv_b, ssd_dt_bias, ssd_a_log, ssd_d, ssd_norm,
              s5_lam_re, s5_lam_im, s5_log_step, s5_b_re, s5_b_im, s5_c_re, s5_c_im, s5_d,
              s5_glu_w, s5_glu_b, w_branch, w_out, ffn_w_gate, ffn_w_up, ffn_w_down):
    xc = ctx
    silu_c = jax.nn.silu(c)
    silu_cc = jax.nn.silu(c_ctx)
    for l in range(DEPTH):
        last = l == DEPTH - 1
        lam_init = 0.8 - 0.6 * math.exp(-0.3 * l)
        mod_l = (silu_c @ ada_w[l] + ada_b[l]).reshape(x.shape[0], 1, 6, D_MODEL)
        mod_c = (silu_cc @ ada_w[l] + ada_b[l]).reshape(6, D_MODEL)
        sh1, sc1, g1, sh2, sc2, g2 = [mod_l[:, :, i] for i in range(6)]
        csh1, csc1, cg1, csh2, csc2, cg2 = [mod_c[i] for i in range(6)]

        h = modulate(rms_norm(x, norm_mix_pre[l]), sh1, sc1)
        hc = modulate(rms_norm(xc, norm_mix_pre[l]), csh1, csc1)
        o_lat, o_ctx = token_mixing(h, hc, not last, lam_init, w_in[l], da_lambda[l], da_subln[l],
                                    ssd_conv_w[l], ssd_conv_b[l], ssd_dt_bias[l], ssd_a_log[l],
                                    ssd_d[l], ssd_norm[l],
                                    s5_lam_re[l], s5_lam_im[l], s5_log_step[l], s5_b_re[l], s5_b_im[l],
                                    s5_c_re[l], s5_c_im[l], s5_d[l], s5_glu_w[l], s5_glu_b[l],
                                    w_branch[l], w_out[l])
        x = x + g1 * rms_norm(o_lat, norm_mix_post[l])
        h = modulate(rms_norm(x, norm_ffn_pre[l]), sh2, sc2)
        x = x + g2 * rms_norm(swiglu(h, ffn_w_gate[l], ffn_w_up[l], ffn_w_down[l]), norm_ffn_post[l])

        if not last:
            xc = xc + cg1 * rms_norm(o_ctx, norm_mix_post[l])
            hc = modulate(rms_norm(xc, norm_ffn_pre[l]), csh2, csc2)
            xc = xc + cg2 * rms_norm(swiglu(hc, ffn_w_gate[l], ffn_w_up[l], ffn_w_down[l]),
                                     norm_ffn_post[l])
    return x
```

```python
import math
from contextlib import ExitStack
import numpy as np
import concourse.bass as bass
import concourse.mybir as mybir
from concourse.bass_utils import run_bass_kernel_spmd

F32 = mybir.dt.float32
BF16 = mybir.dt.bfloat16
AF = mybir.ActivationFunctionType
ALU = mybir.AluOpType
AX = mybir.AxisListType

D = 2048
KD = 16
CTX = 256
LAT = 4096
T = CTX + LAT
NT = T // 128
DEPTH = 2
N_IN = 12832
Q_OFF, K_OFF, V_OFF, Z_OFF, XBC_OFF, DT_OFF, U_OFF, GATE_OFF = 0, 1024, 2048, 3072, 4096, 5632, 5664, 6688
D_FF = 5632
EPS = 1e-6
NCORES = 8
SAME_ENGINE_SYNC = True
PUMP_EVERY = 12


class Buf:
    __slots__ = ("t", "name", "w", "r", "dsem", "dcnt", "loose")

    def __init__(self, t, name, loose=False):
        self.t = t
        self.name = name
        self.w = None
        self.r = {}
        self.dsem = None
        self.dcnt = 0
        self.loose = loose

    def __getitem__(self, k):
        return self.t[k]


class Eng:
    def __init__(self, name, h):
        self.name = name
        self.h = h
        self.sem = None
        self.cnt = 0
        self.known = {}
        self.own = set()


class FW:
    SEM_ROLL = 30000

    def __init__(self, nc, stack):
        self.nc = nc
        self.stack = stack
        self.nsem = 0
        self.E = {"pe": Eng("pe", nc.tensor), "act": Eng("act", nc.scalar), "dve": Eng("dve", nc.vector),
                  "pool": Eng("pool", nc.gpsimd), "sp": Eng("sp", nc.sync)}
        self.ninst = 0
        self.dbufs = []
        self.allsems = []
        self.sempool = []
        self.phase_bufs = []

    def new_sem(self, name):
        self.nsem += 1
        s = self.stack.enter_context(self.nc.semaphore(f"{name}{self.nsem}"))
        return s

    def sbuf(self, st, name, shape, dt):
        self.uid = getattr(self, "uid", 0) + 1
        name = f"{name}_u{self.uid}"
        b = Buf(st.enter_context(self.nc.sbuf_tensor(name, list(shape), dt)), name)
        self.phase_bufs.append(b)
        return b

    def psum(self, st, name, shape, dt):
        self.uid = getattr(self, "uid", 0) + 1
        name = f"{name}_u{self.uid}"
        b = Buf(st.enter_context(self.nc.psum_tensor(name, list(shape), dt)), name)
        self.phase_bufs.append(b)
        return b

    def end_phase(self, keep=()):
        self.barrier()
        for b in self.phase_bufs:
            if any(b is k for k in keep):
                continue
            if b.dsem is not None:
                self.sempool.append((b.dsem, b.dcnt))
                b.dsem = None
                self.dbufs = [x for x in self.dbufs if x is not b]
        self.phase_bufs = [b for b in self.phase_bufs if any(b is k for k in keep)]

    def dram(self, name, shape, dt, kind="Internal", loose=True):
        t = self.nc.dram_tensor(name, list(shape), dt, kind=kind)
        return Buf(t.ap(), name, loose=loose)

    def _wait(self, e, tok):
        if tok is None:
            return
        sem, val = tok
        if id(sem) in e.own and (e.name == "pe" or not SAME_ENGINE_SYNC):
            return
        if e.known.get(id(sem), 0) >= val:
            return
        e.h.wait_ge(sem, val)
        e.known[id(sem)] = val

    def _deps(self, e, reads, writes):
        for b in reads:
            self._wait(e, b.w)
        for b in writes:
            if b.loose:
                continue
            self._wait(e, b.w)
            for t in b.r.values():
                self._wait(e, t)

    def _commit(self, tok, reads, writes):
        for b in reads:
            if not b.loose:
                b.r[id(tok[0])] = tok
        for b in writes:
            b.w = tok
            b.r = {}

    def op(self, eng, fn, reads=(), writes=()):
        e = self.E[eng]
        if e.sem is None or e.cnt >= self.SEM_ROLL:
            e.sem = self.new_sem("p" + eng)
            e.own.add(id(e.sem))
            e.cnt = 0
            self.allsems.append(e)
        self._deps(e, reads, writes)
        ins = fn(e.h)
        ins.then_inc(e.sem, 1)
        e.cnt += 1
        tok = (e.sem, e.cnt)
        self._commit(tok, reads, writes)
        self.ninst += 1
        return tok

    def dma(self, q, out_ap, in_ap, reads=(), writes=(), **kw):
        e = self.E[q]
        wb = writes[0]
        if wb.dsem is None or wb.dcnt >= 16 * 2000:
            if wb.dsem is not None:
                self._wait(e, (wb.dsem, wb.dcnt))
                wb.dsem = None
            if self.sempool and self.sempool[-1][1] < 16 * 1500:
                wb.dsem, wb.dcnt = self.sempool.pop()
            else:
                wb.dsem = self.new_sem("d")
                wb.dcnt = 0
            if not any(wb is x for x in self.dbufs):
                self.dbufs.append(wb)
        self._deps(e, reads, writes)
        ins = e.h.dma_start(out=out_ap, in_=in_ap, **kw)
        ins.then_inc(wb.dsem, 16)
        wb.dcnt += 16
        tok = (wb.dsem, wb.dcnt)
        self._commit(tok, reads, writes)
        self.ninst += 1
        return tok

    def barrier(self):
        toks = []
        for e2 in self.E.values():
            if e2.sem is not None and e2.cnt > 0:
                toks.append((e2.sem, e2.cnt))
        for b in self.dbufs:
            if b.dsem is not None and b.dcnt > 0:
                toks.append((b.dsem, b.dcnt))
        for e in self.E.values():
            for tok in toks:
                if id(tok[0]) in e.own:
                    if e.name == "pe":
                        continue
                self._wait(e, tok)


def act(fw, out, in_, func, reads, writes, eng="act", **kw):
    return fw.op(eng, lambda h: h.activation(out=out, in_=in_, func=func, **kw), reads, writes)


def tt(fw, eng, out, in0, in1, op, reads, writes):
    return fw.op(eng, lambda h: h.tensor_tensor(out=out, in0=in0, in1=in1, op=op), reads, writes)


def ts(fw, eng, out, in0, s1, s2, op0, op1, reads, writes):
    if s2 is None:
        return fw.op(eng, lambda h: h.tensor_scalar(out=out, in0=in0, scalar1=s1, scalar2=None, op0=op0), reads, writes)
    return fw.op(eng, lambda h: h.tensor_scalar(out=out, in0=in0, scalar1=s1, scalar2=s2, op0=op0, op1=op1), reads, writes)


def stt(fw, out, in0, scalar, in1, op0, op1, reads, writes):
    return fw.op("dve", lambda h: h.scalar_tensor_tensor(out=out, in0=in0, scalar=scalar, in1=in1, op0=op0, op1=op1),
                 reads, writes)


def mm(fw, out, lhsT, rhs, start, stop, reads, writes):
    return fw.op("pe", lambda h: h.matmul(out, lhsT, rhs, start=start, stop=stop), reads, writes)


def copy(fw, eng, out, in_, reads, writes):
    if eng == "act":
        return fw.op("act", lambda h: h.copy(out=out, in_=in_), reads, writes)
    return fw.op(eng, lambda h: h.tensor_copy(out=out, in_=in_), reads, writes)


class Ctx:
    pass


def declare_weights(fw, g, l):
    W = Ctx()
    W.win = fw.dram(f"win{l}", [26, 128, KD, 512], BF16)
    W.glu = fw.dram(f"glu{l}", [4, 128, 8, 512], BF16)
    W.wbr = [fw.dram(f"wbr{l}_{n}", [4, 128, 8, 512], BF16) for n in range(3)]
    W.wout = fw.dram(f"wout{l}", [4, 128, KD, 512], BF16)
    W.wg = fw.dram(f"wg{l}", [11, 128, KD, 512], BF16)
    W.wu = fw.dram(f"wu{l}", [11, 128, KD, 512], BF16)
    W.wd = fw.dram(f"wd{l}", [4, 128, 44, 512], BF16)
    return W


def cast_units(fw, g, l, W, which):
    def generic(dst, src, R, C):
        for kc in range(R // 128):
            stg = g.stg[g.stg_i % 2]
            g.stg_i += 1
            fw.dma("pool", stg[:, 0:C], src[kc * 128:(kc + 1) * 128, :], reads=[g.wsrc], writes=[stg], max_dma_last_dim=4096)
            fw.dma("sp", dst[:, :, kc, :].rearrange("b p c -> p b c"),
                   stg[:, 0:C].rearrange("p (b c) -> p b c", c=512), reads=[stg], writes=[dst])
            yield
    if which == "win":
        src = g.inp["w_in"][l]
        win = W.win
        for kc in range(KD):
            stg = g.stg[g.stg_i % 2]
            g.stg_i += 1
            fw.dma("pool", stg[:, :], src[kc * 128:(kc + 1) * 128, :], reads=[g.wsrc], writes=[stg], max_dma_last_dim=4096)
            fw.dma("sp", win[0:11, :, kc, :].rearrange("b p c -> p b c"),
                   stg[:, 0:5632].rearrange("p (b c) -> p b c", c=512), reads=[stg], writes=[win])
            fw.dma("sp", win[11, :, kc, 0:32], stg[:, 5632:5664], reads=[stg], writes=[win])
            fw.dma("sp", win[12:26, :, kc, :].rearrange("b p c -> p b c"),
                   stg[:, 5664:12832].rearrange("p (b c) -> p b c", c=512), reads=[stg], writes=[win])
            yield
    else:
        yield from generic(W.glu, g.inp["s5_glu_w"][l], 1024, 2048)
        for n in range(3):
            yield from generic(W.wbr[n], g.inp["w_branch"][l, n], 1024, 2048)
        yield from generic(W.wout, g.inp["w_out"][l], 2048, 2048)
        yield from generic(W.wg, g.inp["ffn_w_gate"][l], 2048, D_FF)
        yield from generic(W.wu, g.inp["ffn_w_up"][l], 2048, D_FF)
        yield from generic(W.wd, g.inp["ffn_w_down"][l], D_FF, 2048)


def phase_cast_blocking(fw, g, gens):
    with ExitStack() as st:
        g.stg = [fw.sbuf(st, f"stg{i}", [128, N_IN], BF16) for i in range(2)]
        g.stg_i = 0
        for gen in gens:
            for _ in gen:
                pass
        fw.end_phase()


def phase_adaln(fw, g):
    g.mod_d = [fw.dram(f"mod{l}", [2, 6 * D], F32) for l in range(DEPTH)]
    with ExitStack() as st:
        craw = fw.sbuf(st, "craw", [128, KD, 2], F32)
        sc = fw.sbuf(st, "sc", [128, KD, 2], F32)
        sig = fw.sbuf(st, "sig", [128, KD, 2], F32)
        with fw.nc.allow_non_contiguous_dma(reason="tiny"):
            fw.dma("sp", craw[:, :, 0], g.inp["c"][0, :].rearrange("(k p) -> p k", p=128), reads=[], writes=[craw])
            fw.dma("sp", craw[:, :, 1], g.inp["c_ctx"][0, :].rearrange("(k p) -> p k", p=128), reads=[], writes=[craw])
        act(fw, sig[:], craw[:], AF.Sigmoid, [craw], [sig])
        tt(fw, "dve", sc[:], craw[:], sig[:], ALU.mult, [craw, sig], [sc])
        wb = [fw.sbuf(st, f"adaw{i}", [128, KD, 512], F32) for i in range(2)]
        ps = [fw.psum(st, f"adaps{i}", [2, 512], F32) for i in range(2)]
        bias = fw.sbuf(st, "adab", [2, 6 * D], F32)
        row = fw.sbuf(st, "adarow", [2, 6 * D], F32)
        it = 0
        for l in range(DEPTH):
            fw.dma("sp", bias[:, :], g.inp["ada_b"][l:l + 1, :].partition_broadcast(2), reads=[], writes=[bias])
            for cb in range(24):
                w = wb[it % 2]
                p = ps[it % 2]
                it += 1
                fw.dma("sp", w[:], g.inp["ada_w"][l, :, cb * 512:(cb + 1) * 512].rearrange("(k p) c -> p k c", p=128),
                       reads=[], writes=[w])
                for k in range(KD):
                    mm(fw, p[:, :], sc[:, k, :], w[:, k, :], k == 0, k == KD - 1, [sc, w], [p])
                tt(fw, "dve", row[:, cb * 512:(cb + 1) * 512], p[:, :], bias[:, cb * 512:(cb + 1) * 512], ALU.add,
                   [p, bias], [row])
            fw.dma("sp", g.mod_d[l][:, :], row[:, :], reads=[row], writes=[g.mod_d[l]])
        fw.end_phase()


def load_bcast(fw, q, dst, src_row_ap, src_buf=None):
    fw.dma(q, dst[:], src_row_ap.partition_broadcast(128), reads=[src_buf] if src_buf else [], writes=[dst])


def make_mod_tiles(fw, st, g, l, which):
    res = []
    gain = g.inp["norm_mix_pre" if which == 0 else "norm_ffn_pre"]
    gt = fw.sbuf(st, f"gainb{which}", [128, D], F32)
    load_bcast(fw, "sp", gt, gain[l:l + 1, :])
    for r in range(2):
        G = fw.sbuf(st, f"G{which}{r}", [128, D], F32)
        S = fw.sbuf(st, f"S{which}{r}", [128, D], F32)
        load_bcast(fw, "sp", G, g.mod_d[l][r:r + 1, (3 * which + 1) * D:(3 * which + 2) * D], g.mod_d[l])
        load_bcast(fw, "sp", S, g.mod_d[l][r:r + 1, (3 * which) * D:(3 * which + 1) * D], g.mod_d[l])
        stt(fw, G[:], G[:], 1.0, gt[:], ALU.add, ALU.mult, [G, gt], [G])
        res.append((G, S))
    return res


def norm_mod_tile(fw, g, xt, G, S, junk, ss, tmp, hb):
    act(fw, junk[:], xt[:], AF.Square, [xt], [junk, ss], accum_out=ss[:, 0:1])
    act(fw, ss[:, 1:2], ss[:, 0:1], AF.Sqrt, [ss], [ss], scale=1.0 / D, bias=g.eps_t[:, 0:1])
    fw.op("dve", lambda h: h.reciprocal(out=ss[:, 2:3], in_=ss[:, 1:2]), [ss], [ss])
    stt(fw, tmp[:], xt[:], ss[:, 2:3], G[:], ALU.mult, ALU.mult, [xt, ss, G], [tmp])
    tt(fw, "pool", hb[:], tmp[:], S[:], ALU.add, [tmp, S], [hb])


def transpose_tile(fw, g, hb, ps2, dstT, col0, nk=KD):
    for half in range((nk + 7) // 8):
        p = ps2[half % 2]
        kk = min(8, nk - half * 8)
        for j in range(kk):
            k = half * 8 + j
            fw.op("pe", lambda h, k=k, j=j: h.transpose(p[:, j * 128:(j + 1) * 128], hb[:, k * 128:(k + 1) * 128],
                                                       g.ident[:]), [hb, g.ident], [p])
        copy(fw, "act", dstT[:, half * 8:half * 8 + kk, col0:col0 + 128],
             p[:, 0:kk * 128].rearrange("p (k t) -> p k t", t=128), [p], [dstT])


def phase_norm1(fw, g, l):
    with ExitStack() as st:
        mods = make_mod_tiles(fw, st, g, l, 0)
        xts = [fw.sbuf(st, f"xt{i}", [128, D], F32) for i in range(2)]
        junk = fw.sbuf(st, "junk", [128, D], BF16)
        tmp = fw.sbuf(st, "tmp", [128, D], F32)
        hbs = [fw.sbuf(st, f"hb{i}", [128, D], BF16) for i in range(2)]
        sss = [fw.sbuf(st, f"ss{i}", [128, 4], F32) for i in range(2)]
        hts = [fw.sbuf(st, f"hTs{i}", [128, KD, 128], BF16) for i in range(2)]
        ps2 = [fw.psum(st, f"trps{i}", [128, 1024], BF16) for i in range(2)]
        for i in range(NT):
            xt = xts[i % 2]
            hb = hbs[i % 2]
            ss = sss[i % 2]
            ht = hts[i % 2]
            fw.dma("sp", xt[:], g.xsrc(l, i), reads=[g.xres], writes=[xt])
            G, S = mods[1] if i < 2 else mods[0]
            norm_mod_tile(fw, g, xt, G, S, junk, ss, tmp, hb)
            transpose_tile(fw, g, hb, ps2, ht, 0)
            fw.dma("pool", g.hT_d[:, i * 128:(i + 1) * 128].rearrange("(k p) t -> p k t", p=128), ht[:],
                   reads=[ht], writes=[g.hT_d])
        fw.end_phase()


def proj_plan():
    plan = []
    for j in range(2):
        plan.append((0 + j, "f", "qT", j * 512))
    for j in range(2):
        plan.append((2 + j, "f", "kT", j * 512))
    for j in range(2):
        plan.append((4 + j, "t", "v_tok", j * 512))
    for j in range(2):
        plan.append((6 + j, "t", "z_tok", j * 512))
    for j in range(3):
        plan.append((8 + j, "f", "xbcT", j * 512))
    plan.append((11, "d", "dtT", 0))
    for j in range(2):
        plan.append((12 + j, "f", "uT", j * 512))
    return plan


def phase_proj(fw, g, l, W):
    with ExitStack() as st:
        hts = [fw.sbuf(st, f"hTb{i}", [128, KD, 512], BF16) for i in range(2)]
        wts = [fw.sbuf(st, f"wt{i}", [128, KD, 512], BF16) for i in range(3)]
        pss = [fw.psum(st, f"pps{i}", [128, 512], F32) for i in range(4)]
        stg = [fw.sbuf(st, f"pst{i}", [128, 4, 512], BF16) for i in range(2)]
        stgd = [fw.sbuf(st, f"pstd{i}", [32, 512], F32) for i in range(2)]
        plan = proj_plan()
        wi = 0
        pi = 0
        si = 0
        ntb = (T + 511) // 512
        for tb in range(ntb):
            t0 = tb * 512
            n = min(512, T - t0)
            ht = hts[tb % 2]
            fw.dma("sp", ht[:, :, 0:n], g.hT_d[:, t0:t0 + n].rearrange("(k p) t -> p k t", p=128),
                   reads=[g.hT_d], writes=[ht])
            for (blk, kind, dname, off) in plan:
                w = wts[wi % 3]
                wi += 1
                fw.dma("sp", w[:], W.win[blk], reads=[W.win], writes=[w])
                dst = getattr(g, dname)
                if kind == "f":
                    sg = stg[si % 2]
                    si += 1
                    for j in range(4):
                        p = pss[pi % 4]
                        pi += 1
                        for k in range(KD):
                            mm(fw, p[:, 0:n], w[:, k, j * 128:(j + 1) * 128], ht[:, k, 0:n], k == 0, k == KD - 1,
                               [w, ht], [p])
                        copy(fw, "act" if j % 2 == 0 else "dve", sg[:, j, 0:n], p[:, 0:n], [p], [sg])
                    fw.dma("pool", dst[off:off + 512, t0:t0 + n].rearrange("(j p) t -> p j t", p=128), sg[:, :, 0:n],
                           reads=[sg], writes=[dst])
                elif kind == "d":
                    p = pss[pi % 4]
                    pi += 1
                    sd = stgd[tb % 2]
                    for k in range(KD):
                        mm(fw, p[0:32, 0:n], w[:, k, 0:32], ht[:, k, 0:n], k == 0, k == KD - 1, [w, ht], [p])
                    copy(fw, "dve", sd[:, 0:n], p[0:32, 0:n], [p], [sd])
                    fw.dma("pool", dst[:, t0:t0 + n], sd[:, 0:n], reads=[sd], writes=[dst])
                else:
                    sg = stg[si % 2]
                    si += 1
                    ns = n // 128
                    for s in range(ns):
                        p = pss[pi % 4]
                        pi += 1
                        for k in range(KD):
                            mm(fw, p[:, :], ht[:, k, s * 128:(s + 1) * 128], w[:, k, :], k == 0, k == KD - 1,
                               [w, ht], [p])
                        copy(fw, "act" if s % 2 == 0 else "dve", sg[:, s, :], p[:, :], [p], [sg])
                    fw.dma("pool", dst[t0:t0 + n, off:off + 512].rearrange("(s p) c -> p s c", p=128), sg[:, 0:ns, :],
                           reads=[sg], writes=[dst])
        fw.end_phase()


def declare_inputs(fw, g):
    nc = fw.nc
    shapes = {
        "x": [LAT, D], "c": [1, D], "ctx": [CTX, D], "c_ctx": [1, D],
        "ada_w": [2, D, 6 * D], "ada_b": [2, 6 * D],
        "norm_mix_pre": [2, D], "norm_mix_post": [2, D], "norm_ffn_pre": [2, D], "norm_ffn_post": [2, D],
        "w_in": [2, D, N_IN], "da_lambda": [2, 256], "da_subln": [2, 128],
        "ssd_conv_w": [2, 5, 1536], "ssd_conv_b": [2, 1536], "ssd_dt_bias": [2, 32], "ssd_a_log": [2, 32],
        "ssd_d": [2, 16], "ssd_norm": [2, 1024],
        "s5_lam_re": [2, 2, 64, 64], "s5_lam_im": [2, 2, 64, 64], "s5_log_step": [2, 128],
        "s5_b_re": [2, 64, 64, 16], "s5_b_im": [2, 64, 64, 16], "s5_c_re": [2, 64, 16, 64], "s5_c_im": [2, 64, 16, 64],
        "s5_d": [2, 1024], "s5_glu_w": [2, 1024, 2048], "s5_glu_b": [2, 2048],
        "w_branch": [2, 3, 1024, 2048], "w_out": [2, D, D],
        "ffn_w_gate": [2, D, D_FF], "ffn_w_up": [2, D, D_FF], "ffn_w_down": [2, D_FF, D],
    }
    g.inp = {}
    for k, s in shapes.items():
        g.inp[k] = nc.dram_tensor(k, s, F32, kind="ExternalInput").ap()
    g.wsrc = Buf(None, "wsrc", loose=True)
    g.cin = {}
    for k, (s, dt) in const_specs().items():
        g.cin[k] = nc.dram_tensor(k, list(s), dt, kind="ExternalInput").ap()


def const_specs():
    return {
        "c_ident": ((128, 128), BF16),
        "c_cos": ((128, LAT), F32),
        "c_sin": ((128, LAT), F32),
        "c_rt": ((128, 128), BF16),
        "c_maskf": ((128, 128), F32),
        "c_maskb": ((128, 128), F32),
        "c_s5mf": ((128, 128), F32),
        "c_s5mb": ((128, 128), F32),
        "c_selm": ((128, 8, 240), BF16),
    }


def make_consts():
    import ml_dtypes
    c = {}
    c["c_ident"] = np.eye(128, dtype=np.float32).astype(ml_dtypes.bfloat16)
    t = np.arange(LAT)
    row = (t // 64).astype(np.float32)
    col = (t % 64).astype(np.float32)
    inv = (np.float32(10000.0) ** (-np.arange(0, 32, 2, dtype=np.float32) / np.float32(32))).astype(np.float32)
    ar = row[:, None] * inv[None, :]
    ac = col[:, None] * inv[None, :]
    ang = np.concatenate([ar, ar, ac, ac], axis=1).astype(np.float32)
    cosT = np.cos(ang).T.astype(np.float32)
    sinT = np.sin(ang).T.astype(np.float32)
    c["c_cos"] = np.ascontiguousarray(np.concatenate([cosT, cosT], 0))
    c["c_sin"] = np.ascontiguousarray(np.concatenate([sinT, sinT], 0))
    R = np.zeros((64, 64), np.float32)
    for m in list(range(0, 16)) + list(range(32, 48)):
        R[m, m + 16] = -1.0
    for m in list(range(16, 32)) + list(range(48, 64)):
        R[m, m - 16] = 1.0
    R2 = np.zeros((128, 128), np.float32)
    R2[:64, :64] = R
    R2[64:, 64:] = R
    c["c_rt"] = np.ascontiguousarray(R2.T).astype(ml_dtypes.bfloat16)
    jj = np.arange(128)[:, None]
    ii = np.arange(128)[None, :]
    c["c_maskf"] = np.where(ii >= jj, 0.0, -30000.0).astype(np.float32)
    c["c_maskb"] = np.where(jj >= ii, 0.0, -30000.0).astype(np.float32)
    c["c_s5mf"] = ((ii // 16) >= (jj // 16)).astype(np.float32)
    c["c_s5mb"] = ((jj // 16) >= (ii // 16)).astype(np.float32)
    sel = np.zeros((128, 8, 240), np.float32)
    for gl in range(8):
        for e in range(16):
            sel[gl * 16 + e, gl, 112 + e] = 1.0
    c["c_selm"] = sel.astype(ml_dtypes.bfloat16)
    return c


def build(stop_after=None, debug=(), inject=(), fast=False):
    nc = bass.Bass("TRN2", target_bir_lowering=False)
    g = Ctx()
    with ExitStack() as st:
        fw = FW(nc, st)
        g.fw = fw
        declare_inputs(fw, g)
        out = fw.dram("out", [LAT, D], F32, kind="ExternalOutput")
        g.out = out
        g.xres = fw.dram("xres", [T, D], F32)
        g.hT_d = fw.dram("hT_d", [D, T], BF16)
        g.qT = fw.dram("qT", [1024, T], BF16)
        g.kT = fw.dram("kT", [1024, T], BF16)
        g.v_tok = fw.dram("v_tok", [T, 1024], BF16)
        g.z_tok = fw.dram("z_tok", [T, 1024], BF16)
        g.xbcT = fw.dram("xbcT", [1536, T], BF16)
        g.dtT = fw.dram("dtT", [32, T], F32)
        g.uT = fw.dram("uT", [1024, T], BF16)
        g.qTr = fw.dram("qTr", [1024, T], BF16)
        g.kTr = fw.dram("kTr", [1024, T], BF16)
        g.yaT = fw.dram("yaT", [1024, T], BF16)
        g.ysT = fw.dram("ysT", [1024, T], BF16)
        g.y5T = fw.dram("y5T", [1024, T], BF16)
        g.xs_tok = fw.dram("xs_tok", [T, 1024], BF16)
        g.B_tok = fw.dram("B_tok", [T, 256], BF16)
        g.BT = fw.dram("BT", [256, T], BF16)
        g.CT = fw.dram("CT", [256, T], BF16)
        g.ssd_q = fw.dram("ssd_q", [2, 6, 16, T], F32)
        g.ssd_cd = fw.dram("ssd_cd", [2, 16, NT], F32)
        g.yf_d = fw.dram("yf_d", [T, 1024], F32)
        g.s5L = fw.dram("s5L", [2, 8, 2, 64, 64], F32)
        g.s5A = fw.dram("s5A", [2, 64, 64, 2, 512], BF16)
        g.s5D = fw.dram("s5D", [2, 64, 128, 10, 128], BF16)
        g.s5M2 = fw.dram("s5M2", [2, 64, 128, 2, 4, 64], BF16)
        g.y5f_d = fw.dram("y5f_d", [64, 128, 4, NSC], F32)
        g.y5g = fw.dram("y5g", [1024, T], BF16)
        inj = {}
        for name in inject:
            src = getattr(g, name)
            inj[name] = nc.dram_tensor("inj_" + name, list(src.t.shape), src.t.dtype, kind="ExternalInput").ap()
        dbg = {}
        for name in debug:
            src = getattr(g, name)
            dbg[name] = fw.dram("dbg_" + name, list(src.t.shape), src.t.dtype, kind="ExternalOutput")

        def xsrc(l, i):
            if l == 0:
                if i < 2:
                    return g.inp["ctx"][i * 128:(i + 1) * 128, :]
                return g.inp["x"][(i - 2) * 128:(i - 1) * 128, :]
            return g.xres[i * 128:(i + 1) * 128, :]
        g.xsrc = xsrc

        g.ident = fw.sbuf(st, "ident", [128, 128], BF16)
        fw.dma("sp", g.ident[:], g.cin["c_ident"], reads=[], writes=[g.ident])
        g.eps_t = fw.sbuf(st, "eps_t", [128, 1], F32)
        fw.op("dve", lambda h: h.memset(g.eps_t[:], EPS), [], [g.eps_t])
        g.one_t = fw.sbuf(st, "one_t", [128, 1], F32)
        fw.op("dve", lambda h: h.memset(g.one_t[:], 1.0), [], [g.one_t])
        g.keep = [g.ident, g.eps_t, g.one_t]
        fw.phase_bufs = []

        def done():
            for name in debug:
                src = getattr(g, name)
                fw.dma("sp", dbg[name][:], src[:], reads=[src], writes=[dbg[name]])
            fw.barrier()

        Ws = [declare_weights(fw, g, l) for l in range(DEPTH)]
        phase_cast_blocking(fw, g, [cast_units(fw, g, 0, Ws[0], "win")])
        phase_adaln(fw, g)
        for l in range(DEPTH):
            W = Ws[l]
            phase_norm1(fw, g, l)
            phase_proj(fw, g, l, W)
            if stop_after == "proj":
                done()
                return nc
            if not fast:
                phase_rope(fw, g, l)
                bg = []
                if l == 0:
                    bg = [cast_units(fw, g, 0, Ws[0], "rest"), cast_units(fw, g, 1, Ws[1], "win"),
                          cast_units(fw, g, 1, Ws[1], "rest")]
                phase_attn(fw, g, l, bg)
            if stop_after == "attn":
                done()
                return nc
            if stop_after != "skipssd" and not fast:
                phase_ssd_prep(fw, g, l)
                phase_ssd_scan(fw, g, l)
            if stop_after == "ssd":
                done()
                return nc
            if "y5T" not in inject or l > 0:
                phase_s5_params(fw, g, l)
                if stop_after == "s5p":
                    done()
                    return nc
                phase_s5_main(fw, g, l)
                if stop_after == "s5m":
                    done()
                    return nc
                phase_s5_glu(fw, g, l, W)
            if stop_after == "s5":
                done()
                return nc
            if l == 0:
                for name in inject:
                    dstb = getattr(g, name)
                    fw.dma("sp", dstb[:], inj[name], reads=[], writes=[dstb])
                fw.barrier()
            phase_merge(fw, g, l, W)
            phase_ffn(fw, g, l, W)
            if stop_after == f"ffn{l}":
                done()
                return nc
        done()
    return nc


_IN_KEYS = ["x", "c", "ctx", "c_ctx", "ada_w", "ada_b", "norm_mix_pre", "norm_mix_post", "norm_ffn_pre",
            "norm_ffn_post", "w_in", "da_lambda", "da_subln", "ssd_conv_w", "ssd_conv_b", "ssd_dt_bias",
            "ssd_a_log", "ssd_d", "ssd_norm", "s5_lam_re", "s5_lam_im", "s5_log_step", "s5_b_re", "s5_b_im",
            "s5_c_re", "s5_c_im", "s5_d", "s5_glu_w", "s5_glu_b", "w_branch", "w_out", "ffn_w_gate", "ffn_w_up",
            "ffn_w_down"]


def make_in_map(inputs, b):
    f = lambda a: np.ascontiguousarray(np.asarray(a, dtype=np.float32))
    m = {}
    for k in _IN_KEYS:
        a = inputs[k]
        if k == "x":
            m[k] = f(a[b])
        elif k == "ctx":
            m[k] = f(a[b])
        elif k == "c":
            m[k] = f(a[b:b + 1])
        elif k == "c_ctx":
            m[k] = f(a).reshape(1, D)
        elif k == "da_lambda":
            m[k] = f(a).reshape(2, 256)
        elif k in ("ssd_dt_bias", "ssd_a_log"):
            m[k] = f(a).reshape(2, 32)
        elif k == "s5_log_step":
            m[k] = f(a).reshape(2, 128)
        else:
            m[k] = f(a)
    m.update(make_consts())
    return m


def kernel(**inputs):
    nc = build()
    maps = [make_in_map(inputs, 0), make_in_map(inputs, 1)]
    in_maps = [maps[0] if c < NCORES // 2 else maps[1] for c in range(NCORES)]
    res = run_bass_kernel_spmd(nc, in_maps, core_ids=list(range(NCORES)))
    o0 = np.asarray(res.results[0]["out"], dtype=np.float32)
    o1 = np.asarray(res.results[NCORES // 2]["out"], dtype=np.float32)
    return np.stack([o0, o1], axis=0)


def tblocks(l):
    blks = [(0, CTX)] if l < DEPTH - 1 else []
    return blks + [(CTX + 512 * b, 512) for b in range(LAT // 512)]


def phase_rope(fw, g, l):
    with ExitStack() as st:
        cos = fw.sbuf(st, "cos", [128, LAT], F32)
        sin = fw.sbuf(st, "sin", [128, LAT], F32)
        rt = fw.sbuf(st, "rt", [128, 128], BF16)
        fw.dma("sp", cos[:], g.cin["c_cos"], reads=[], writes=[cos])
        fw.dma("sp", sin[:], g.cin["c_sin"], reads=[], writes=[sin])
        fw.dma("sp", rt[:], g.cin["c_rt"], reads=[], writes=[rt])
        qs = [fw.sbuf(st, f"rq{i}", [128, 512], BF16) for i in range(3)]
        t1 = [fw.sbuf(st, f"rt1{i}", [128, 512], F32) for i in range(2)]
        t2 = [fw.sbuf(st, f"rt2{i}", [128, 512], F32) for i in range(2)]
        ob = [fw.sbuf(st, f"rob{i}", [128, 512], BF16) for i in range(3)]
        ps = [fw.psum(st, f"rps{i}", [128, 512], F32) for i in range(2)]
        it = 0
        for (src, dst) in ((g.qT, g.qTr), (g.kT, g.kTr)):
            fw.dma("pool", dst[:, 0:CTX], src[:, 0:CTX], reads=[src], writes=[dst])
            for j in range(8):
                for b in range(LAT // 512):
                    q = qs[it % 3]
                    o = ob[it % 3]
                    a = t1[it % 2]
                    c = t2[it % 2]
                    p = ps[it % 2]
                    it += 1
                    t0 = CTX + b * 512
                    fw.dma("sp", q[:], src[j * 128:(j + 1) * 128, t0:t0 + 512], reads=[src], writes=[q])
                    mm(fw, p[:], rt[:], q[:], True, True, [rt, q], [p])
                    tt(fw, "dve", a[:], q[:], cos[:, b * 512:(b + 1) * 512], ALU.mult, [q, cos], [a])
                    tt(fw, "dve", c[:], p[:], sin[:, b * 512:(b + 1) * 512], ALU.mult, [p, sin], [c])
                    tt(fw, "pool", o[:], a[:], c[:], ALU.add, [a, c], [o])
                    fw.dma("pool", dst[j * 128:(j + 1) * 128, t0:t0 + 512], o[:], reads=[o], writes=[dst])
        fw.end_phase(keep=g.keep)


def phase_attn(fw, g, l, bg=()):
    lam_init = 0.8 - 0.6 * math.exp(-0.3 * l)
    with ExitStack() as st:
        bg = list(bg)
        if bg:
            g.stg = [fw.sbuf(st, f"astg{i}", [128, N_IN], BF16) for i in range(2)]
            g.stg_i = 0
        pstate = {"n": 0}

        def pump():
            pstate["n"] += 1
            if pstate["n"] % PUMP_EVERY != 0:
                return
            while bg:
                try:
                    next(bg[0])
                    return
                except StopIteration:
                    bg.pop(0)
        dl = fw.sbuf(st, "dl", [128, 256], F32)
        pr = fw.sbuf(st, "dlp", [128, 2, 64], F32)
        sm = fw.sbuf(st, "dls", [128, 4], F32)
        nlam = fw.sbuf(st, "nlam", [128, 1], F32)
        subln = fw.sbuf(st, "subln", [128, 1], F32)
        load_bcast(fw, "sp", dl, g.inp["da_lambda"][l:l + 1, :])
        tt(fw, "dve", pr[:, 0, :], dl[:, 0:64], dl[:, 64:128], ALU.mult, [dl], [pr])
        tt(fw, "dve", pr[:, 1, :], dl[:, 128:192], dl[:, 192:256], ALU.mult, [dl, pr], [pr])
        fw.op("dve", lambda h: h.tensor_reduce(out=sm[:, 0:2], in_=pr[:], axis=AX.X, op=ALU.add), [pr], [sm])
        act(fw, sm[:, 2:4], sm[:, 0:2], AF.Exp, [sm], [sm])
        tt(fw, "dve", nlam[:], sm[:, 3:4], sm[:, 2:3], ALU.subtract, [sm], [nlam])
        ts(fw, "dve", nlam[:], nlam[:], -lam_init, None, ALU.add, None, [nlam], [nlam])
        with fw.nc.allow_non_contiguous_dma(reason="tiny"):
            fw.dma("sp", subln[:], g.inp["da_subln"][l, :].rearrange("(p o) -> p o", o=1), reads=[], writes=[subln])
        ts(fw, "dve", subln[:], subln[:], 1.0 - lam_init, None, ALU.mult, None, [subln], [subln])
        onesb = fw.sbuf(st, "onesb", [128, 128], BF16)
        onesf = fw.sbuf(st, "onesf", [128, 128], F32)
        fw.op("dve", lambda h: h.memset(onesb[:], 1.0), [], [onesb])
        fw.op("dve", lambda h: h.memset(onesf[:], 1.0), [], [onesf])

        KT = [[fw.sbuf(st, f"KT{c}_{i}", [128, T], BF16) for i in range(2)] for c in range(2)]
        for c in range(2):
            for i in range(2):
                fw.op("pool", lambda h, b=KT[c][i]: h.memset(b[:], 0.0), [], [KT[c][i]])
        QT = [fw.sbuf(st, f"QT{i}", [128, T], BF16) for i in range(2)]
        VV = [fw.sbuf(st, f"VV{i}", [128, NT, 128], BF16) for i in range(2)]
        pbs = [fw.sbuf(st, f"pb{i}", [128, 512], BF16) for i in range(4)]
        tcs = [fw.sbuf(st, f"tc{i}", [128, 512], F32) for i in range(2)]
        rr = fw.sbuf(st, "rr", [128, 512], F32)
        oo = fw.sbuf(st, "oo", [128, 512], F32)
        sq = fw.sbuf(st, "sq", [128, 512], F32)
        rs = fw.sbuf(st, "rs", [128, 512], F32)
        ys = [fw.sbuf(st, f"ys{i}", [128, 512], BF16) for i in range(2)]
        ps_s = [fw.psum(st, f"ps_s{i}", [128, 512], F32) for i in range(3)]
        ps_o = [fw.psum(st, f"ps_o{i}", [128, 512], F32) for i in range(2)]
        ps_l = [fw.psum(st, f"ps_l{i}", [128, 512], F32) for i in range(2)]
        ps_q = fw.psum(st, "ps_q", [128, 512], F32)
        accs = [fw.sbuf(st, f"aacc{i}", [128, 512], F32) for i in range(3)]
        yi = 0
        for hd in range(8):
            ktc = [KT[0][hd % 2], KT[1][hd % 2]]
            qt_ = QT[hd % 2]
            vv = VV[hd % 2]
            for c in range(2):
                fw.dma("sp", ktc[c][c * 64:(c + 1) * 64, :], g.kTr[hd * 128 + c * 64:hd * 128 + (c + 1) * 64, :],
                       reads=[g.kTr], writes=[ktc[c]])
            fw.dma("sp", qt_[:], g.qTr[hd * 128:(hd + 1) * 128, :], reads=[g.qTr], writes=[qt_])
            fw.dma("sp", vv[:], g.v_tok[:, hd * 128:(hd + 1) * 128].rearrange("(k p) e -> p k e", p=128),
                   reads=[g.v_tok], writes=[vv])
            its = []
            for (q0, n) in tblocks(l):
                nkt = 2 if q0 == 0 else NT
                for c in range(2):
                    for kt in range(nkt):
                        its.append((q0, n, c, kt, nkt))

            def issue_S(i):
                q0, n, c, kt, nkt = its[i]
                p_s = ps_s[i % 3]
                mm(fw, p_s[:, 0:n], ktc[c][:, kt * 128:(kt + 1) * 128], qt_[:, q0:q0 + n], True, True, [ktc[c], qt_], [p_s])

            LOOK = 2
            for i in range(min(LOOK, len(its))):
                issue_S(i)
            for i, (q0, n, c, kt, nkt) in enumerate(its):
                if i + LOOK < len(its):
                    issue_S(i + LOOK)
                p_s = ps_s[i % 3]
                pb = pbs[i % 4]
                po = ps_o[c]
                act(fw, pb[:, 0:n], p_s[:, 0:n], AF.Exp, [p_s], [pb], scale=0.125)
                mm(fw, po[:, 0:n], vv[:, kt, :], pb[:, 0:n], kt == 0, kt == nkt - 1, [vv, pb], [po])
                pl = ps_l[c]
                role = ("pe", "dve", "pe", "dve", "pool")[kt % 5]
                if role == "pe":
                    mm(fw, pl[:, 0:n], onesb[:], pb[:, 0:n], kt == 0, False, [onesb, pb], [pl])
                else:
                    a_ = accs[0] if role == "dve" else accs[1]
                    first = (kt == 1) if role == "dve" else (kt == 4)
                    if first:
                        copy(fw, role, a_[:, 0:n], pb[:, 0:n], [pb], [a_])
                    else:
                        tt(fw, role, a_[:, 0:n], a_[:, 0:n], pb[:, 0:n], ALU.add, [a_, pb], [a_])
                pump()
                if kt == nkt - 1:
                    if nkt > 4:
                        tt(fw, "dve", accs[0][:, 0:n], accs[0][:, 0:n], accs[1][:, 0:n], ALU.add, [accs[0], accs[1]], [accs[0]])
                    mm(fw, pl[:, 0:n], onesf[:], accs[0][:, 0:n], False, True, [onesf, accs[0]], [pl])
                    fw.op("dve", lambda h: h.reciprocal(out=rr[:, 0:n], in_=pl[:, 0:n]), [pl], [rr])
                    tt(fw, "dve", tcs[c][:, 0:n], po[:, 0:n], rr[:, 0:n], ALU.mult, [po, rr], [tcs[c]])
                    if c == 1:
                        stt(fw, oo[:, 0:n], tcs[1][:, 0:n], nlam[:, 0:1], tcs[0][:, 0:n], ALU.mult, ALU.add,
                            [tcs[0], tcs[1], nlam], [oo])
                        act(fw, sq[:, 0:n], oo[:, 0:n], AF.Square, [oo], [sq])
                        mm(fw, ps_q[:, 0:n], onesf[:], sq[:, 0:n], True, True, [onesf, sq], [ps_q])
                        act(fw, rs[:, 0:n], ps_q[:, 0:n], AF.Sqrt, [ps_q], [rs], scale=1.0 / 128, bias=g.eps_t[:, 0:1])
                        fw.op("dve", lambda h: h.reciprocal(out=rs[:, 0:n], in_=rs[:, 0:n]), [rs], [rs])
                        y = ys[yi % 2]
                        yi += 1
                        stt(fw, y[:, 0:n], oo[:, 0:n], subln[:, 0:1], rs[:, 0:n], ALU.mult, ALU.mult, [oo, subln, rs], [y])
                        fw.dma("pool", g.yaT[hd * 128:(hd + 1) * 128, q0:q0 + n], y[:, 0:n], reads=[y], writes=[g.yaT])
        for gen in bg:
            for _ in gen:
                pass
        fw.end_phase(keep=g.keep)


def post_tile(fw, g, o_ap, o_buf, GT, xt, hb, ss, tmp, dst_ap, dst_buf, q="pool"):
    act(fw, hb[:], o_ap, AF.Square, [o_buf], [hb, ss], accum_out=ss[:, 0:1])
    act(fw, ss[:, 1:2], ss[:, 0:1], AF.Sqrt, [ss], [ss], scale=1.0 / D, bias=g.eps_t[:, 0:1])
    fw.op("dve", lambda h: h.reciprocal(out=ss[:, 2:3], in_=ss[:, 1:2]), [ss], [ss])
    stt(fw, tmp[:], o_ap, ss[:, 2:3], GT[:], ALU.mult, ALU.mult, [o_buf, ss, GT], [tmp])
    tt(fw, "pool", tmp[:], tmp[:], xt[:], ALU.add, [tmp, xt], [tmp])
    fw.dma(q, dst_ap, tmp[:], reads=[tmp], writes=[dst_buf])


def load_gate_tile(fw, g, l, GT, gb, which, r):
    load_bcast(fw, "sp", GT, g.mod_d[l][r:r + 1, (3 * which + 2) * D:(3 * which + 3) * D], g.mod_d[l])
    load_bcast(fw, "sp", gb, g.inp["norm_mix_post" if which == 0 else "norm_ffn_post"][l:l + 1, :])
    tt(fw, "dve", GT[:], GT[:], gb[:], ALU.mult, [GT, gb], [GT])


def phase_merge(fw, g, l, W):
    last = l == DEPTH - 1
    with ExitStack() as st:
        GT = fw.sbuf(st, "mGT", [128, D], F32)
        gb = fw.sbuf(st, "mgb", [128, D], F32)
        hT = fw.sbuf(st, "mhT", [128, KD, 512], BF16)
        yT = [fw.sbuf(st, f"myT{i}", [128, 8, 512], BF16) for i in range(3)]
        MT = fw.sbuf(st, "mMT", [128, KD, 512], BF16)
        w16 = [fw.sbuf(st, f"mw16_{i}", [128, KD, 512], BF16) for i in range(4)]
        w8 = [fw.sbuf(st, f"mw8_{i}", [128, 8, 512], BF16) for i in range(4)]
        sg = [fw.sbuf(st, f"msg{i}", [128, 512], F32) for i in range(2)]
        pp = [fw.sbuf(st, f"mpp{i}", [128, 512], F32) for i in range(3)]
        osb = fw.sbuf(st, "mosb", [128, D], F32)
        xt = fw.sbuf(st, "mxt", [128, D], F32)
        tmp = fw.sbuf(st, "mtmp", [128, D], F32)
        hb = fw.sbuf(st, "mhb", [128, D], BF16)
        ss = fw.sbuf(st, "mss", [128, 4], F32)
        ps_g = [fw.psum(st, f"mps_g{i}", [128, 512], F32) for i in range(2)]
        ps_b = [fw.psum(st, f"mps_b{i}", [128, 512], F32) for i in range(2)]
        ps_o = [fw.psum(st, f"mps_o{i}", [128, 512], F32) for i in range(2)]
        ysrc = [g.yaT, g.ysT, g.y5T]
        i16 = 0
        i8 = 0
        ig = 0
        ib = 0
        io = 0
        cur_r = None
        for (t0, n) in tblocks(l):
            r = 1 if t0 == 0 else 0
            if r != cur_r:
                load_gate_tile(fw, g, l, GT, gb, 0, r)
                cur_r = r
            fw.dma("sp", hT[:, :, 0:n], g.hT_d[:, t0:t0 + n].rearrange("(k p) t -> p k t", p=128),
                   reads=[g.hT_d], writes=[hT])
            for b3 in range(3):
                fw.dma("sp", yT[b3][:, :, 0:n], ysrc[b3][:, t0:t0 + n].rearrange("(k p) t -> p k t", p=128),
                       reads=[ysrc[b3]], writes=[yT[b3]])
            for dq in range(4):
                wg_ = []
                wb_ = []
                for b3 in range(3):
                    w = w16[i16 % 4]
                    i16 += 1
                    fw.dma("sp", w[:], W.win[14 + b3 * 4 + dq], reads=[W.win], writes=[w])
                    wg_.append(w)
                    w2 = w8[i8 % 4]
                    i8 += 1
                    fw.dma("sp", w2[:], W.wbr[b3][dq], reads=[W.wbr[b3]], writes=[w2])
                    wb_.append(w2)
                for j in range(4):
                    dt_ = dq * 4 + j
                    for b3 in range(3):
                        pg = ps_g[ig % 2]
                        ig += 1
                        pb = ps_b[ib % 2]
                        ib += 1
                        s_ = sg[b3 % 2]
                        for k in range(KD):
                            mm(fw, pg[:, 0:n], wg_[b3][:, k, j * 128:(j + 1) * 128], hT[:, k, 0:n], k == 0, k == KD - 1,
                               [wg_[b3], hT], [pg])
                        act(fw, s_[:, 0:n], pg[:, 0:n], AF.Sigmoid, [pg], [s_])
                        for k in range(8):
                            mm(fw, pb[:, 0:n], wb_[b3][:, k, j * 128:(j + 1) * 128], yT[b3][:, k, 0:n], k == 0, k == 7,
                               [wb_[b3], yT[b3]], [pb])
                        tt(fw, "dve", pp[b3][:, 0:n], pb[:, 0:n], s_[:, 0:n], ALU.mult, [pb, s_], [pp[b3]])
                    tt(fw, "pool", pp[0][:, 0:n], pp[0][:, 0:n], pp[1][:, 0:n], ALU.add, [pp[0], pp[1]], [pp[0]])
                    tt(fw, "pool", MT[:, dt_, 0:n], pp[0][:, 0:n], pp[2][:, 0:n], ALU.add, [pp[0], pp[2]], [MT])
            wo = []
            for cb in range(4):
                w = w16[i16 % 4] if cb < 3 else hT
                if cb < 3:
                    i16 += 1
                fw.dma("sp", w[:], W.wout[cb], reads=[W.wout], writes=[w])
                wo.append(w)
            for s in range(n // 128):
                i = t0 // 128 + s
                fw.dma("sp", xt[:], g.xsrc(l, i), reads=[g.xres], writes=[xt])
                for cb in range(4):
                    po = ps_o[io % 2]
                    io += 1
                    for k in range(KD):
                        mm(fw, po[:, :], MT[:, k, s * 128:(s + 1) * 128], wo[cb][:, k, :], k == 0, k == KD - 1,
                           [MT, wo[cb]], [po])
                    copy(fw, "act", osb[:, cb * 512:(cb + 1) * 512], po[:, :], [po], [osb])
                post_tile(fw, g, osb[:], osb, GT, xt, hb, ss, tmp, g.xres[i * 128:(i + 1) * 128, :], g.xres)
        fw.end_phase(keep=g.keep)


def phase_ffn(fw, g, l, W):
    last = l == DEPTH - 1
    with ExitStack() as st:
        G = fw.sbuf(st, "fG", [128, D], F32)
        S = fw.sbuf(st, "fS", [128, D], F32)
        GT = fw.sbuf(st, "fGT", [128, D], F32)
        gb = fw.sbuf(st, "fgb", [128, D], F32)
        h2T = fw.sbuf(st, "fh2T", [128, KD, 512], BF16)
        AT = fw.sbuf(st, "fAT", [128, 44, 512], BF16)
        w16 = [fw.sbuf(st, f"fw16_{i}", [128, KD, 512], BF16) for i in range(3)]
        wdp = [fw.sbuf(st, f"fwd{i}", [128, 11, 512], BF16) for i in range(2)]
        osb = [fw.sbuf(st, f"fosb{i}", [128, D], F32) for i in range(2)]
        xt = fw.sbuf(st, "fxt", [128, D], F32)
        tmp = fw.sbuf(st, "ftmp", [128, D], F32)
        hb = fw.sbuf(st, "fhb", [128, D], BF16)
        ss = fw.sbuf(st, "fss", [128, 4], F32)
        sg = [fw.sbuf(st, f"fsg{i}", [128, 512], F32) for i in range(2)]
        ps_g = [fw.psum(st, f"fps_g{i}", [128, 512], F32) for i in range(2)]
        ps_u = [fw.psum(st, f"fps_u{i}", [128, 512], F32) for i in range(2)]
        ps_d = [fw.psum(st, f"fps_d{i}", [128, 512], F32) for i in range(2)]
        ps2 = [fw.psum(st, f"ftr{i}", [128, 1024], BF16) for i in range(2)]
        gain = g.inp["norm_ffn_pre"]
        iw = 0
        ig = 0
        iwd = 0
        cur_r = None
        for (t0, n) in tblocks(l):
            r = 1 if t0 == 0 else 0
            if r != cur_r:
                load_bcast(fw, "sp", gb, gain[l:l + 1, :])
                load_bcast(fw, "sp", G, g.mod_d[l][r:r + 1, 4 * D:5 * D], g.mod_d[l])
                load_bcast(fw, "sp", S, g.mod_d[l][r:r + 1, 3 * D:4 * D], g.mod_d[l])
                stt(fw, G[:], G[:], 1.0, gb[:], ALU.add, ALU.mult, [G, gb], [G])
                load_gate_tile(fw, g, l, GT, gb, 1, r)
                cur_r = r
            ns = n // 128
            for s in range(ns):
                i = t0 // 128 + s
                fw.dma("sp", xt[:], g.xres[i * 128:(i + 1) * 128, :], reads=[g.xres], writes=[xt])
                norm_mod_tile(fw, g, xt, G, S, hb, ss, tmp, hb)
                transpose_tile(fw, g, hb, ps2, h2T, s * 128)
            for fb in range(11):
                wg_ = w16[iw % 3]
                iw += 1
                fw.dma("sp", wg_[:], W.wg[fb], reads=[W.wg], writes=[wg_])
                wu_ = w16[iw % 3]
                iw += 1
                fw.dma("sp", wu_[:], W.wu[fb], reads=[W.wu], writes=[wu_])
                for j in range(4):
                    pg = ps_g[ig % 2]
                    pu = ps_u[ig % 2]
                    s_ = sg[ig % 2]
                    ig += 1
                    for k in range(KD):
                        mm(fw, pg[:, 0:n], wg_[:, k, j * 128:(j + 1) * 128], h2T[:, k, 0:n], k == 0, k == KD - 1,
                           [wg_, h2T], [pg])
                    act(fw, s_[:, 0:n], pg[:, 0:n], AF.Silu, [pg], [s_])
                    for k in range(KD):
                        mm(fw, pu[:, 0:n], wu_[:, k, j * 128:(j + 1) * 128], h2T[:, k, 0:n], k == 0, k == KD - 1,
                           [wu_, h2T], [pu])
                    tt(fw, "dve", AT[:, fb * 4 + j, 0:n], pu[:, 0:n], s_[:, 0:n], ALU.mult, [pu, s_], [AT])
            for pr_ in range((ns + 1) // 2):
                subs = [s for s in (2 * pr_, 2 * pr_ + 1) if s < ns]
                for cb in range(4):
                    for pc in range(4):
                        wd_ = wdp[iwd % 2]
                        iwd += 1
                        fw.dma("sp", wd_[:], W.wd[cb][:, pc * 11:(pc + 1) * 11, :], reads=[W.wd], writes=[wd_])
                        for si_, s in enumerate(subs):
                            for kk in range(11):
                                k = pc * 11 + kk
                                mm(fw, ps_d[si_][:, :], AT[:, k, s * 128:(s + 1) * 128], wd_[:, kk, :], k == 0, k == 43,
                                   [AT, wd_], [ps_d[si_]])
                    for si_, s in enumerate(subs):
                        copy(fw, "act", osb[si_][:, cb * 512:(cb + 1) * 512], ps_d[si_][:, :], [ps_d[si_]], [osb[si_]])
                for si_, s in enumerate(subs):
                    i = t0 // 128 + s
                    fw.dma("sp", xt[:], g.xres[i * 128:(i + 1) * 128, :], reads=[g.xres], writes=[xt])
                    if last:
                        dst_ap, dst_buf = g.out[(i - 2) * 128:(i - 1) * 128, :], g.out
                    else:
                        dst_ap, dst_buf = g.xres[i * 128:(i + 1) * 128, :], g.xres
                    post_tile(fw, g, osb[si_][:], osb[si_], GT, xt, hb, ss, tmp, dst_ap, dst_buf)
        fw.end_phase(keep=g.keep)


NPAD = T + 8
NCV = NPAD - 4


def phase_ssd_prep(fw, g, l):
    with ExitStack() as st:
        cw = fw.sbuf(st, "cw", [128, 12, 5], F32)
        cb = fw.sbuf(st, "cb", [128, 12], F32)
        with fw.nc.allow_non_contiguous_dma(reason="tiny"):
            for k in range(5):
                fw.dma("sp", cw[:, :, k], g.inp["ssd_conv_w"][l, k, :].rearrange("(c p) -> p c", p=128), reads=[], writes=[cw])
            fw.dma("sp", cb[:], g.inp["ssd_conv_b"][l, :].rearrange("(c p) -> p c", p=128), reads=[], writes=[cb])
        xin = [fw.sbuf(st, f"cxin{i}", [128, NPAD], BF16) for i in range(2)]
        for b in xin:
            fw.op("pool", lambda h, b=b: h.memset(b[:], 0.0), [], [b])
        acc = fw.sbuf(st, "cacc", [128, NCV], F32)
        xo = [fw.sbuf(st, f"cxo{i}", [128, NCV], BF16) for i in range(2)]
        ps2 = [fw.psum(st, f"ctr{i}", [128, 1024], BF16) for i in range(2)]
        tst = [fw.sbuf(st, f"ctst{i}", [128, 8, 128], BF16) for i in range(2)]
        ti = 0
        for ct in range(12):
            xi = xin[ct % 2]
            o = xo[ct % 2]
            fw.dma("sp", xi[:, 2:2 + CTX], g.xbcT[ct * 128:(ct + 1) * 128, 0:CTX], reads=[g.xbcT], writes=[xi])
            fw.dma("sp", xi[:, 262:262 + LAT], g.xbcT[ct * 128:(ct + 1) * 128, CTX:T], reads=[g.xbcT], writes=[xi])
            ts(fw, "dve", acc[:], xi[:, 0:NCV], cw[:, ct, 0:1], cb[:, ct:ct + 1], ALU.mult, ALU.add, [xi, cw, cb], [acc])
            for k in range(1, 5):
                stt(fw, acc[:], xi[:, k:k + NCV], cw[:, ct, k:k + 1], acc[:], ALU.mult, ALU.add, [xi, cw, acc], [acc])
            act(fw, o[:], acc[:], AF.Silu, [acc], [o])
            if ct >= 8:
                dst = g.BT if ct < 10 else g.CT
                r0 = (ct - 8) % 2 * 128
                fw.dma("pool", dst[r0:r0 + 128, 0:CTX], o[:, 0:CTX], reads=[o], writes=[dst])
                fw.dma("pool", dst[r0:r0 + 128, CTX:T], o[:, 260:260 + LAT], reads=[o], writes=[dst])
            if ct < 10:
                dst = g.xs_tok if ct < 8 else g.B_tok
                c0 = ct * 128 if ct < 8 else (ct - 8) * 128
                for grp in range((NT + 7) // 8):
                    i0 = grp * 8
                    ni = min(8, NT - i0)
                    p = ps2[ti % 2]
                    sg = tst[ti % 2]
                    ti += 1
                    for j in range(ni):
                        i = i0 + j
                        col = i * 128 if i < 2 else 260 + (i - 2) * 128
                        fw.op("pe", lambda h, j=j, col=col: h.transpose(p[:, j * 128:(j + 1) * 128], o[:, col:col + 128],
                                                                        g.ident[:]), [o, g.ident], [p])
                    copy(fw, "act", sg[:, 0:ni, :], p[:, 0:ni * 128].rearrange("p (i c) -> p i c", c=128), [p], [sg])
                    fw.dma("pool", dst[i0 * 128:(i0 + ni) * 128, c0:c0 + 128].rearrange("(i p) c -> p i c", p=128),
                           sg[:, 0:ni, :], reads=[sg], writes=[dst])
        fw.end_phase(keep=g.keep)
    for d in range(2):
        with ExitStack() as st:
            ones = fw.sbuf(st, "dones", [16, 128], F32)
            fw.op("dve", lambda h: h.memset(ones[:], 1.0), [], [ones])
            raw = fw.sbuf(st, f"draw{d}", [16, T], F32)
            dt = fw.sbuf(st, f"ddt{d}", [16, T], F32)
            dtA = fw.sbuf(st, f"ddtA{d}", [16, T], F32)
            AC = fw.sbuf(st, f"dAC{d}", [16, T], F32)
            EX = fw.sbuf(st, f"dEX{d}", [16, T], F32)
            q1 = fw.sbuf(st, f"dq1{d}", [16, T], F32)
            q2 = fw.sbuf(st, f"dq2{d}", [16, T], F32)
            last = fw.sbuf(st, f"dlast{d}", [16, NT], F32)
            par = fw.sbuf(st, f"dpar{d}", [16, 4], F32)
            fw.dma("sp", raw[:], g.dtT[d * 16:(d + 1) * 16, :], reads=[g.dtT], writes=[raw])
            with fw.nc.allow_non_contiguous_dma(reason="tiny"):
                fw.dma("sp", par[:, 0:1], g.inp["ssd_dt_bias"][l, d * 16:(d + 1) * 16].rearrange("(p o) -> p o", o=1),
                       reads=[], writes=[par])
                fw.dma("sp", par[:, 1:2], g.inp["ssd_a_log"][l, d * 16:(d + 1) * 16].rearrange("(p o) -> p o", o=1),
                       reads=[], writes=[par])
            act(fw, par[:, 2:3], par[:, 1:2], AF.Exp, [par], [par])
            ts(fw, "dve", par[:, 2:3], par[:, 2:3], -1.0, None, ALU.mult, None, [par], [par])
            act(fw, dt[:], raw[:], AF.Exp, [raw, par], [dt], bias=par[:, 0:1])
            act(fw, dt[:], dt[:], AF.Ln, [dt], [dt], bias=g.one_t[0:16, 0:1])
            ts(fw, "dve", dtA[:], dt[:], par[:, 2:3], None, ALU.mult, None, [dt, par], [dtA])
            for c in range(NT):
                fw.op("dve", lambda h, c=c: h.tensor_tensor_scan(out=AC[:, c * 128:(c + 1) * 128], data0=ones[:],
                                                                 data1=dtA[:, c * 128:(c + 1) * 128], initial=0.0,
                                                                 op0=ALU.mult, op1=ALU.add), [ones, dtA], [AC])
            tt(fw, "dve", EX[:], AC[:], dtA[:], ALU.subtract, [AC, dtA], [EX])
            copy(fw, "dve", last[:], AC[:].rearrange("p (c t) -> p c t", t=128)[:, :, 127], [AC], [last])
            lb = last[:].unsqueeze(2).to_broadcast([16, NT, 128])
            v3 = lambda b: b[:].rearrange("p (c t) -> p c t", t=128)
            qd = g.ssd_q
            fw.dma("pool", qd[d, 0], dt[:], reads=[dt], writes=[qd])
            if d == 0:
                tt(fw, "dve", v3(q1), lb, v3(AC), ALU.subtract, [last, AC], [q1])
                act(fw, q1[:], q1[:], AF.Exp, [q1], [q1])
                tt(fw, "dve", q1[:], q1[:], dt[:], ALU.mult, [q1, dt], [q1])
                fw.dma("pool", qd[d, 1], q1[:], reads=[q1], writes=[qd])
                act(fw, q2[:], AC[:], AF.Exp, [AC], [q2])
                fw.dma("pool", qd[d, 2], q2[:], reads=[q2], writes=[qd])
                ts(fw, "dve", dtA[:], AC[:], -1.0, None, ALU.mult, None, [AC], [dtA])
                fw.dma("pool", qd[d, 3], dtA[:], reads=[dtA], writes=[qd])
                fw.dma("pool", qd[d, 4], AC[:], reads=[AC], writes=[qd])
            else:
                act(fw, q1[:], EX[:], AF.Exp, [EX], [q1])
                tt(fw, "dve", q1[:], q1[:], dt[:], ALU.mult, [q1, dt], [q1])
                fw.dma("pool", qd[d, 1], q1[:], reads=[q1], writes=[qd])
                tt(fw, "dve", v3(q2), lb, v3(EX), ALU.subtract, [last, EX], [q2])
                act(fw, q2[:], q2[:], AF.Exp, [q2], [q2])
                fw.dma("pool", qd[d, 2], q2[:], reads=[q2], writes=[qd])
                fw.dma("pool", qd[d, 3], EX[:], reads=[EX], writes=[qd])
                ts(fw, "dve", dtA[:], EX[:], -1.0, None, ALU.mult, None, [EX], [dtA])
                fw.dma("pool", qd[d, 4], dtA[:], reads=[dtA], writes=[qd])
            act(fw, last[:], last[:], AF.Exp, [last], [last])
            fw.dma("pool", g.ssd_cd[d], last[:], reads=[last], writes=[g.ssd_cd])
            fw.end_phase(keep=g.keep)


def phase_ssd_scan(fw, g, l):
    with ExitStack() as st:
        identf = fw.sbuf(st, "identf", [128, 128], F32)
        copy(fw, "dve", identf[:], g.ident[:], [g.ident], [identf])
        masks = [fw.sbuf(st, f"smask{d}", [128, 128], F32) for d in range(2)]
        fw.dma("sp", masks[0][:], g.cin["c_maskf"], reads=[], writes=[masks[0]])
        fw.dma("sp", masks[1][:], g.cin["c_maskb"], reads=[], writes=[masks[1]])
        Dbc = fw.sbuf(st, "sDbc", [128, 16], F32)
        load_bcast(fw, "sp", Dbc, g.inp["ssd_d"][l:l + 1, :])
        nrm = fw.sbuf(st, "snrm", [128, 1024], F32)
        load_bcast(fw, "sp", nrm, g.inp["ssd_norm"][l:l + 1, :])
        cdb = fw.sbuf(st, "scdb", [128, 16, NT], F32)
        SD = fw.sbuf(st, "sSD", [128, T], F32)
        H = fw.sbuf(st, "sH", [128, 16, 64], F32)
        Hb = fw.sbuf(st, "sHb", [128, 1024], BF16)
        xs = [fw.sbuf(st, f"sxs{i}", [128, 16, 64], BF16) for i in range(2)]
        Bt = [fw.sbuf(st, f"sBt{i}", [128, 256], BF16) for i in range(2)]
        BT = [fw.sbuf(st, f"sBT{i}", [128, 2, 128], BF16) for i in range(2)]
        CT = [fw.sbuf(st, f"sCT{i}", [128, 2, 128], BF16) for i in range(2)]
        bc = [fw.sbuf(st, f"sbc{i}", [128, 16, 128], F32) for i in range(2)]
        zt = [fw.sbuf(st, f"szt{i}", [128, 1024], BF16) for i in range(2)]
        yf = [fw.sbuf(st, f"syf{i}", [128, 1024], F32) for i in range(2)]
        tm = fw.sbuf(st, "stm", [128, 128], F32)
        cbs = fw.sbuf(st, "scb", [128, 2, 128], F32)
        xdt = fw.sbuf(st, "sxdt", [128, 16, 64], BF16)
        xdo = fw.sbuf(st, "sxdo", [128, 16, 64], BF16)
        seg = [fw.sbuf(st, f"sseg{i}", [128, 128], F32) for i in range(2)]
        dec = [fw.sbuf(st, f"sdec{i}", [128, 128], F32) for i in range(2)]
        MT = [fw.sbuf(st, f"sMT{i}", [128, 128], BF16) for i in range(3)]
        yo = fw.sbuf(st, "syo", [128, 16, 64], F32)
        ysum = fw.sbuf(st, "sysum", [128, 16, 64], F32)
        t2 = fw.sbuf(st, "st2", [128, 16, 64], F32)
        sz = fw.sbuf(st, "ssz", [128, 1024], F32)
        hb = fw.sbuf(st, "shb", [128, 1024], BF16)
        ss = fw.sbuf(st, "sss", [128, 4], F32)
        yst = [fw.sbuf(st, f"syst{i}", [128, 8, 128], BF16) for i in range(2)]
        ps_t = fw.psum(st, "sps_t", [128, 3, 128], F32)
        ps_y = fw.psum(st, "sps_y", [128, 1024], F32)
        ps_yo = fw.psum(st, "sps_yo", [128, 1024], F32)
        ps_sn = fw.psum(st, "sps_sn", [128, 1024], F32)
        ps2 = [fw.psum(st, "sps_tr", [128, 1024], BF16)]
        it = 0
        for d in range(2):
            order = list(range(NT)) if d == 0 else [1, 0] + list(range(NT - 1, 1, -1))
            fw.dma("sp", SD[0:64, :], g.ssd_q[d, 0:4].rearrange("q h t -> (q h) t"), reads=[g.ssd_q], writes=[SD])
            fw.dma("sp", cdb[:], g.ssd_cd[d:d + 1].partition_broadcast(128), reads=[g.ssd_cd], writes=[cdb])
            fw.op("dve", lambda h: h.memset(H[:], 0.0), [], [H])
            fw.op("dve", lambda h: h.memset(Hb[:], 0.0), [], [Hb])
            for c in order:
                b = it % 2
                it += 1
                tk = slice(c * 128, (c + 1) * 128)
                fw.dma("sp", xs[b][:], g.xs_tok[tk, :].rearrange("p (h e) -> p h e", e=64), reads=[g.xs_tok], writes=[xs[b]])
                fw.dma("sp", Bt[b][:], g.B_tok[tk, :], reads=[g.B_tok], writes=[Bt[b]])
                fw.dma("sp", BT[b][:], g.BT[:, tk].rearrange("(g n) t -> n g t", n=128), reads=[g.BT], writes=[BT[b]])
                fw.dma("sp", CT[b][:], g.CT[:, tk].rearrange("(g n) t -> n g t", n=128), reads=[g.CT], writes=[CT[b]])
                fw.dma("sp", bc[b][:], g.ssd_q[d, 4:5, :, tk].partition_broadcast(128), reads=[g.ssd_q], writes=[bc[b]])
                if d == 1:
                    fw.dma("sp", zt[b][:], g.z_tok[tk, :], reads=[g.z_tok], writes=[zt[b]])
                    fw.dma("sp", yf[b][:], g.yf_d[tk, :], reads=[g.yf_d], writes=[yf[b]])
                fw.op("pe", lambda h: h.transpose(ps_t[:, 0, :], SD[:, tk], identf[:]), [SD, identf], [ps_t])
                for gg in range(2):
                    mm(fw, ps_t[:, 1 + gg, :], BT[b][:, gg, :], CT[b][:, gg, :], True, True, [BT[b], CT[b]], [ps_t])
                copy(fw, "act", tm[:], ps_t[:, 0, :], [ps_t], [tm])
                copy(fw, "act", cbs[:], ps_t[:, 1:3, :], [ps_t], [cbs])
                tt(fw, "pool", xdt[:], xs[b][:], tm[:, 0:16].unsqueeze(2).to_broadcast([128, 16, 64]), ALU.mult,
                   [xs[b], tm], [xdt])
                tt(fw, "pool", xdo[:], xs[b][:], tm[:, 16:32].unsqueeze(2).to_broadcast([128, 16, 64]), ALU.mult,
                   [xs[b], tm], [xdo])
                for hh in range(16):
                    sgb = seg[hh % 2]
                    dcb = dec[hh % 2]
                    mt = MT[hh % 3]
                    stt(fw, sgb[:], bc[b][:, hh, :], tm[:, 48 + hh:49 + hh], masks[d][:], ALU.add, ALU.add,
                        [bc[b], tm, masks[d]], [sgb])
                    act(fw, dcb[:], sgb[:], AF.Exp, [sgb], [dcb])
                    tt(fw, "pool", mt[:], dcb[:], cbs[:, hh // 8, :], ALU.mult, [dcb, cbs], [mt])
                    mm(fw, ps_y[:, hh * 64:(hh + 1) * 64], mt[:], xdt[:, hh, :], True, True, [mt, xdt], [ps_y])
                for gg in range(2):
                    mm(fw, ps_yo[:, gg * 512:(gg + 1) * 512], CT[b][:, gg, :], Hb[:, gg * 512:(gg + 1) * 512], True, True,
                       [CT[b], Hb], [ps_yo])
                tt(fw, "dve", yo[:], ps_yo[:].rearrange("p (h e) -> p h e", e=64),
                   tm[:, 32:48].unsqueeze(2).to_broadcast([128, 16, 64]), ALU.mult, [ps_yo, tm], [yo])
                tt(fw, "dve", ysum[:], ps_y[:].rearrange("p (h e) -> p h e", e=64), yo[:], ALU.add, [ps_y, yo], [ysum])
                for gg in range(2):
                    mm(fw, ps_sn[:, gg * 512:(gg + 1) * 512], Bt[b][:, gg * 128:(gg + 1) * 128],
                       xdo[:, gg * 8:(gg + 1) * 8, :].rearrange("p h e -> p (h e)"), True, True, [Bt[b], xdo], [ps_sn])
                tt(fw, "dve", H[:], H[:], cdb[:, :, c:c + 1].to_broadcast([128, 16, 64]), ALU.mult, [H, cdb], [H])
                tt(fw, "dve", H[:], H[:], ps_sn[:].rearrange("p (h e) -> p h e", e=64), ALU.add, [H, ps_sn], [H])
                copy(fw, "act", Hb[:], H[:].rearrange("p h e -> p (h e)"), [H], [Hb])
                if d == 0:
                    fw.dma("pool", g.yf_d[tk, :], ysum[:].rearrange("p h e -> p (h e)"), reads=[ysum], writes=[g.yf_d])
                else:
                    if l == DEPTH - 1 and c < 2:
                        continue
                    ysf = ysum[:].rearrange("p h e -> p (h e)")
                    tt(fw, "pool", ysf, ysf, yf[b][:], ALU.add, [ysum, yf[b]], [ysum])
                    tt(fw, "pool", t2[:], xs[b][:], Dbc[:, :].unsqueeze(2).to_broadcast([128, 16, 64]), ALU.mult,
                       [xs[b], Dbc], [t2])
                    tt(fw, "pool", ysum[:], ysum[:], t2[:], ALU.add, [ysum, t2], [ysum])
                    act(fw, sz[:], zt[b][:], AF.Silu, [zt[b]], [sz])
                    tt(fw, "dve", sz[:], sz[:], ysf, ALU.mult, [sz, ysum], [sz])
                    act(fw, hb[:], sz[:], AF.Square, [sz], [hb, ss], accum_out=ss[:, 0:1])
                    act(fw, ss[:, 1:2], ss[:, 0:1], AF.Sqrt, [ss], [ss], scale=1.0 / 1024, bias=g.eps_t[:, 0:1])
                    fw.op("dve", lambda h: h.reciprocal(out=ss[:, 2:3], in_=ss[:, 1:2]), [ss], [ss])
                    stt(fw, hb[:], sz[:], ss[:, 2:3], nrm[:], ALU.mult, ALU.mult, [sz, ss, nrm], [hb])
                    ys_ = yst[it % 2]
                    transpose_tile(fw, g, hb, ps2, ys_, 0, nk=8)
                    fw.dma("pool", g.ysT[:, tk].rearrange("(k p) t -> p k t", p=128), ys_[:], reads=[ys_], writes=[g.ysT])
        fw.end_phase(keep=g.keep)


NSC = T // 32
NCC = CTX // 32


def cmul(fw, o_re, o_im, ar, ai, br, bi, t1, t2, bufs_in, bufs_out, tb1, tb2):
    tt(fw, "dve", t1, ar, br, ALU.mult, bufs_in, [tb1])
    tt(fw, "pool", t2, ai, bi, ALU.mult, bufs_in, [tb2])
    tt(fw, "dve", o_re, t1, t2, ALU.subtract, [tb1, tb2], bufs_out)
    tt(fw, "dve", t1, ar, bi, ALU.mult, bufs_in + bufs_out, [tb1])
    tt(fw, "pool", t2, ai, br, ALU.mult, bufs_in + bufs_out, [tb2])
    tt(fw, "dve", o_im, t1, t2, ALU.add, [tb1, tb2], bufs_out)


def phase_s5_params(fw, g, l):
    P = 64
    with ExitStack() as st:
        identf = fw.sbuf(st, "p5identf", [128, 128], F32)
        copy(fw, "dve", identf[:], g.ident[:], [g.ident], [identf])
        halfpi = fw.sbuf(st, "p5hpi", [P, 1], F32)
        fw.op("dve", lambda h: h.memset(halfpi[:], math.pi / 2), [], [halfpi])
        lraw = fw.sbuf(st, "p5lraw", [128, 2, P], F32)
        fw.dma("sp", lraw[:, 0, :], g.inp["s5_lam_re"][l].rearrange("d g p -> (d g) p"), reads=[], writes=[lraw])
        fw.dma("sp", lraw[:, 1, :], g.inp["s5_lam_im"][l].rearrange("d g p -> (d g) p"), reads=[], writes=[lraw])
        pst = fw.psum(st, "p5pst", [P, 1024], F32)
        LR = fw.sbuf(st, "p5LR", [P, 128], F32)
        LI = fw.sbuf(st, "p5LI", [P, 128], F32)
        for i, dst in enumerate((LR, LI)):
            fw.op("pe", lambda h, i=i: h.transpose(pst[:, i * 128:(i + 1) * 128], lraw[:, i, :], identf[:]),
                  [lraw, identf], [pst])
            copy(fw, "act", dst[:], pst[:, i * 128:(i + 1) * 128], [pst], [dst])
        dl = fw.sbuf(st, "p5dl", [P, 128], F32)
        fw.dma("sp", dl[:], g.inp["s5_log_step"][l:l + 1, :].partition_broadcast(P), reads=[], writes=[dl])
        act(fw, dl[:], dl[:], AF.Exp, [dl], [dl])
        Bre = fw.sbuf(st, "p5Bre", [P, 64, 16], F32)
        Bim = fw.sbuf(st, "p5Bim", [P, 64, 16], F32)
        fw.dma("sp", Bre[:], g.inp["s5_b_re"][l].rearrange("g p e -> p g e"), reads=[], writes=[Bre])
        fw.dma("sp", Bim[:], g.inp["s5_b_im"][l].rearrange("g p e -> p g e"), reads=[], writes=[Bim])
        Cre = fw.sbuf(st, "p5Cre", [P, 64, 16], F32)
        Cim = fw.sbuf(st, "p5Cim", [P, 64, 16], F32)
        craw = fw.sbuf(st, "p5craw", [128, 8, P], F32)
        for nm, dst in (("s5_c_re", Cre), ("s5_c_im", Cim)):
            fw.dma("sp", craw[:], g.inp[nm][l].rearrange("(a b) e p -> (b e) a p", b=8), reads=[], writes=[craw])
            for a in range(8):
                fw.op("pe", lambda h, a=a: h.transpose(pst[:, a * 128:(a + 1) * 128], craw[:, a, :], identf[:]),
                      [craw, identf], [pst])
            copy(fw, "act", dst[:].rearrange("p g e -> p (g e)"), pst[:, :], [pst], [dst])
        th = fw.sbuf(st, "p5th", [P, 128], F32)
        rho = fw.sbuf(st, "p5rho", [P, 128], F32)
        tt(fw, "dve", th[:], LI[:], dl[:], ALU.mult, [LI, dl], [th])
        tt(fw, "dve", rho[:], LR[:], dl[:], ALU.mult, [LR, dl], [rho])
        s1 = fw.sbuf(st, "p5s1", [P, 128], F32)
        c1 = fw.sbuf(st, "p5c1", [P, 128], F32)
        mg = fw.sbuf(st, "p5mg", [P, 128], F32)
        act(fw, s1[:], th[:], AF.Sin, [th], [s1], scale=1.0 / 16)
        act(fw, c1[:], th[:], AF.Sin, [th, halfpi], [c1], scale=1.0 / 16, bias=halfpi[:, 0:1])
        t1 = fw.sbuf(st, "p5t1", [P, 128], F32)
        t2 = fw.sbuf(st, "p5t2", [P, 128], F32)
        cur = {}
        for sgn in (1, -1):
            act(fw, mg[:], rho[:], AF.Exp, [rho], [mg], scale=sgn / 16.0)
            ar = fw.sbuf(st, f"p5ar{sgn}", [P, 128], F32)
            ai = fw.sbuf(st, f"p5ai{sgn}", [P, 128], F32)
            br = fw.sbuf(st, f"p5br{sgn}", [P, 128], F32)
            bi = fw.sbuf(st, f"p5bi{sgn}", [P, 128], F32)
            tt(fw, "dve", ar[:], mg[:], c1[:], ALU.mult, [mg, c1], [ar])
            tt(fw, "dve", ai[:], mg[:], s1[:], ALU.mult, [mg, s1], [ai])
            if sgn < 0:
                ts(fw, "dve", ai[:], ai[:], -1.0, None, ALU.mult, None, [ai], [ai])
            a_, b_ = (ar, ai), (br, bi)
            for _ in range(4):
                cmul(fw, b_[0][:], b_[1][:], a_[0][:], a_[1][:], a_[0][:], a_[1][:], t1[:], t2[:],
                     [a_[0], a_[1]], [b_[0], b_[1]], t1, t2)
                a_, b_ = b_, a_
            cur[sgn] = a_
        P1, PM1 = cur[1], cur[-1]
        ka = fw.sbuf(st, "p5ka", [P, 128], F32)
        den = fw.sbuf(st, "p5den", [P, 128], F32)
        kr = fw.sbuf(st, "p5kr", [P, 128], F32)
        ki = fw.sbuf(st, "p5ki", [P, 128], F32)
        ts(fw, "dve", ka[:], P1[0][:], -1.0, None, ALU.add, None, [P1[0]], [ka])
        tt(fw, "dve", den[:], LR[:], LR[:], ALU.mult, [LR], [den])
        tt(fw, "dve", t1[:], LI[:], LI[:], ALU.mult, [LI], [t1])
        tt(fw, "dve", den[:], den[:], t1[:], ALU.add, [den, t1], [den])
        fw.op("dve", lambda h: h.reciprocal(out=den[:], in_=den[:]), [den], [den])
        tt(fw, "dve", t1[:], ka[:], LR[:], ALU.mult, [ka, LR], [t1])
        tt(fw, "dve", t2[:], P1[1][:], LI[:], ALU.mult, [P1[1], LI], [t2])
        tt(fw, "dve", t1[:], t1[:], t2[:], ALU.add, [t1, t2], [t1])
        tt(fw, "dve", kr[:], t1[:], den[:], ALU.mult, [t1, den], [kr])
        tt(fw, "dve", t1[:], P1[1][:], LR[:], ALU.mult, [P1[1], LR], [t1])
        tt(fw, "dve", t2[:], ka[:], LI[:], ALU.mult, [ka, LI], [t2])
        tt(fw, "dve", t1[:], t1[:], t2[:], ALU.subtract, [t1, t2], [t1])
        tt(fw, "dve", ki[:], t1[:], den[:], ALU.mult, [t1, den], [ki])
        u1 = fw.sbuf(st, "p5u1", [P, 64, 16], F32)
        u2 = fw.sbuf(st, "p5u2", [P, 64, 16], F32)
        TPr = fw.sbuf(st, "p5TPr", [P, 64, 33], F32)
        TPi = fw.sbuf(st, "p5TPi", [P, 64, 33], F32)
        TNr = fw.sbuf(st, "p5TNr", [P, 64, 32], F32)
        TNi = fw.sbuf(st, "p5TNi", [P, 64, 32], F32)
        TRr = fw.sbuf(st, "p5TRr", [P, 64, 32], F32)
        TRi = fw.sbuf(st, "p5TRi", [P, 64, 32], F32)
        Bbr = fw.sbuf(st, "p5Bbr", [P, 64, 16], F32)
        Bbi = fw.sbuf(st, "p5Bbi", [P, 64, 16], F32)
        GBS = 2
        tmps = [[fw.sbuf(st, f"p5t{c}{i}", [P, GBS, 32, 16], F32) for c in "ABCD"] for i in range(2)]
        NK = 8
        outs = [[fw.sbuf(st, f"p5o{k}_{i}", [P, GBS, 512], BF16) for k in range(NK)] for i in range(2)]
        psd = [fw.psum(st, f"p5psd{i}", [128, 4, 128], F32) for i in range(2)]
        psm = fw.psum(st, "p5psm", [128, 2, 4, 64], BF16)
        Dsb = [fw.sbuf(st, f"p5Dsb{i}", [128, 10, 128], BF16) for i in range(2)]
        M2sb = [fw.sbuf(st, f"p5M2sb{i}", [128, 2, 4, 64], BF16) for i in range(2)]
        dmask = fw.sbuf(st, "p5dmask", [128, 2, 128], F32)
        fw.dma("sp", dmask[:, 0, :], g.cin["c_s5mf"], reads=[], writes=[dmask])
        fw.dma("sp", dmask[:, 1, :], g.cin["c_s5mb"], reads=[], writes=[dmask])
        Lst = fw.sbuf(st, "p5Lst", [P, 8, 2, 64], F32)
        L2 = fw.sbuf(st, "p5L2", [P, 2, 64], F32)
        bi_ = 0
        for d in range(2):
            ds = slice(d * 64, (d + 1) * 64)
            krb = kr[:, ds].unsqueeze(2).to_broadcast([P, 64, 16])
            kib = ki[:, ds].unsqueeze(2).to_broadcast([P, 64, 16])
            cmul(fw, Bbr[:], Bbi[:], krb, kib, Bre[:], Bim[:], u1[:], u2[:], [kr, ki, Bre, Bim], [Bbr, Bbi], u1, u2)
            fw.op("dve", lambda h: h.memset(TPr[:, :, 0], 1.0), [], [TPr])
            fw.op("dve", lambda h: h.memset(TPi[:, :, 0], 0.0), [], [TPi])
            fw.op("dve", lambda h: h.memset(TNr[:, :, 0], 1.0), [], [TNr])
            fw.op("dve", lambda h: h.memset(TNi[:, :, 0], 0.0), [], [TNi])
            for k in range(32):
                cmul(fw, TPr[:, :, k + 1], TPi[:, :, k + 1], TPr[:, :, k], TPi[:, :, k], P1[0][:, ds], P1[1][:, ds],
                     t1[:, 0:64], t2[:, 0:64], [TPr, TPi, P1[0], P1[1]], [TPr, TPi], t1, t2)
            for k in range(31):
                cmul(fw, TNr[:, :, k + 1], TNi[:, :, k + 1], TNr[:, :, k], TNi[:, :, k], PM1[0][:, ds], PM1[1][:, ds],
                     t1[:, 0:64], t2[:, 0:64], [TNr, TNi, PM1[0], PM1[1]], [TNr, TNi], t1, t2)
            top = 31 if d == 0 else 32
            for t_ in range(32):
                copy(fw, "dve", TRr[:, :, t_], TPr[:, :, top - t_], [TPr], [TRr])
                copy(fw, "pool", TRi[:, :, t_], TPi[:, :, top - t_], [TPi], [TRi])
            copy(fw, "dve", Lst[:, 0, 0, :], TPr[:, :, 32], [TPr], [Lst])
            copy(fw, "dve", Lst[:, 0, 1, :], TPi[:, :, 32], [TPi], [Lst])
            for k in range(7):
                cmul(fw, L2[:, 0, :], L2[:, 1, :], Lst[:, k, 0, :], Lst[:, k, 1, :], Lst[:, k, 0, :], Lst[:, k, 1, :],
                     t1[:, 0:64], t2[:, 0:64], [Lst], [L2], t1, t2)
                copy(fw, "dve", Lst[:, k + 1, :, :], L2[:], [L2], [Lst])
            fw.dma("pool", g.s5L[d].rearrange("k r p g -> p k r g"), Lst[:], reads=[Lst], writes=[g.s5L])
            if d == 0:
                specs = [((0, 1), (TNr, TNi, 0), (Bbr, Bbi), False), ((2, 3), (TPr, TPi, 0), (Cre, Cim), True),
                         ((4, 5), (TRr, TRi, 0), (Bbr, Bbi), False), ((6, 7), (TPr, TPi, 1), (Cre, Cim), True)]
            else:
                specs = [((0, 1), (TPr, TPi, 0), (Bbr, Bbi), False), ((2, 3), (TNr, TNi, 0), (Cre, Cim), True),
                         ((6, 7), (TRr, TRi, 0), (Cre, Cim), True)]
            si_ = 0
            for gbk in range(64 // GBS):
                g0 = gbk * GBS
                O = outs[bi_ % 2]
                Dt = Dsb[bi_ % 2]
                bi_ += 1
                for (kre, kim), (Tr_, Ti_, off), (Vr_, Vi_), neg in specs:
                    tA, tB, tC, tD = tmps[si_ % 2]
                    si_ += 1
                    tr = Tr_[:, g0:g0 + GBS, off:off + 32].unsqueeze(3).to_broadcast([P, GBS, 32, 16])
                    ti = Ti_[:, g0:g0 + GBS, off:off + 32].unsqueeze(3).to_broadcast([P, GBS, 32, 16])
                    vr = Vr_[:, g0:g0 + GBS, :].unsqueeze(2).to_broadcast([P, GBS, 32, 16])
                    vi = Vi_[:, g0:g0 + GBS, :].unsqueeze(2).to_broadcast([P, GBS, 32, 16])
                    ore = O[kre][:].rearrange("p g (t e) -> p g t e", e=16)
                    oim = O[kim][:].rearrange("p g (t e) -> p g t e", e=16)
                    rb = [Tr_, Ti_, Vr_, Vi_]
                    tt(fw, "dve", tA[:], tr, vr, ALU.mult, rb, [tA])
                    tt(fw, "pool", tB[:], ti, vi, ALU.mult, rb, [tB])
                    tt(fw, "dve", tC[:], tr, vi, ALU.mult, rb, [tC])
                    tt(fw, "pool", tD[:], ti, vr, ALU.mult, rb, [tD])
                    tt(fw, "dve", ore, tA[:], tB[:], ALU.subtract, [tA, tB], [O[kre]])
                    if neg:
                        stt(fw, oim, tC[:], -1.0, tD[:], ALU.mult, ALU.subtract, [tC, tD], [O[kim]])
                    else:
                        tt(fw, "dve", oim, tC[:], tD[:], ALU.add, [tC, tD], [O[kim]])
                fw.dma("pool", g.s5A[d, g0:g0 + GBS, :, 0, :].rearrange("g p x -> p g x"), O[6][:], reads=[O[6]], writes=[g.s5A])
                fw.dma("pool", g.s5A[d, g0:g0 + GBS, :, 1, :].rearrange("g p x -> p g x"), O[7][:], reads=[O[7]], writes=[g.s5A])
                m2r, m2i = (O[4], O[5]) if d == 0 else (O[0], O[1])
                for gi in range(GBS):
                    gg = g0 + gi
                    bidx = 0
                    for I in range(4):
                        Js = list(range(0, I + 1)) if d == 0 else list(range(I, 4))
                        pd = psd[I % 2]
                        for jn, J in enumerate(Js):
                            mm(fw, pd[:, jn, :], O[0][:, gi, J * 128:(J + 1) * 128], O[2][:, gi, I * 128:(I + 1) * 128],
                               True, False, [O[0], O[2]], [pd])
                            mm(fw, pd[:, jn, :], O[1][:, gi, J * 128:(J + 1) * 128], O[3][:, gi, I * 128:(I + 1) * 128],
                               False, True, [O[1], O[3]], [pd])
                        for jn, J in enumerate(Js):
                            if J == I:
                                tt(fw, "dve", Dt[:, bidx + jn, :], pd[:, jn, :], dmask[:, d, :], ALU.mult, [pd, dmask], [Dt])
                            else:
                                copy(fw, "dve", Dt[:, bidx + jn, :], pd[:, jn, :], [pd], [Dt])
                        bidx += len(Js)
                    fw.dma("pool", g.s5D[d, gg], Dt[:], reads=[Dt], writes=[g.s5D])
                    M2t = M2sb[gi % 2]
                    for ri, src in enumerate((m2r, m2i)):
                        for r in range(4):
                            fw.op("pe", lambda h, ri=ri, r=r, src=src: h.transpose(
                                psm[:, ri, r, :], src[:, gi, r * 128:(r + 1) * 128], g.ident[0:64, 0:64]),
                                [src, g.ident], [psm])
                    copy(fw, "act", M2t[:], psm[:], [psm], [M2t])
                    fw.dma("pool", g.s5M2[d, gg], M2t[:], reads=[M2t], writes=[g.s5M2])
        fw.end_phase(keep=g.keep)


def ks_scan(fw, X, Y, tA, tB, Lt, lo, hi, descending):
    n = hi - lo
    k = 0
    sh = 1
    while sh < n:
        m = n - sh
        if not descending:
            dst = slice(lo + sh, hi)
            src = slice(lo, hi - sh)
            keep = slice(lo, lo + sh)
        else:
            dst = slice(lo, hi - sh)
            src = slice(lo + sh, hi)
            keep = slice(hi - sh, hi)
        Lr = Lt[:, k, 0, :].unsqueeze(2).to_broadcast([128, 16, m])
        Li = Lt[:, k, 1, :].unsqueeze(2).to_broadcast([128, 16, m])
        xr, xi = X
        yr, yi = Y
        tt(fw, "dve", tA[:, :, 0:m], xr[:, :, src], Lr, ALU.mult, [xr, Lt], [tA])
        tt(fw, "pool", tB[:, :, 0:m], xi[:, :, src], Li, ALU.mult, [xi, Lt], [tB])
        tt(fw, "dve", tA[:, :, 0:m], tA[:, :, 0:m], tB[:, :, 0:m], ALU.subtract, [tA, tB], [tA])
        tt(fw, "dve", yr[:, :, dst], xr[:, :, dst], tA[:, :, 0:m], ALU.add, [xr, tA], [yr])
        copy(fw, "pool", yr[:, :, keep], xr[:, :, keep], [xr], [yr])
        tt(fw, "dve", tA[:, :, 0:m], xi[:, :, src], Lr, ALU.mult, [xi, Lt, yr], [tA])
        tt(fw, "pool", tB[:, :, 0:m], xr[:, :, src], Li, ALU.mult, [xr, Lt, yr], [tB])
        tt(fw, "dve", tA[:, :, 0:m], tA[:, :, 0:m], tB[:, :, 0:m], ALU.add, [tA, tB], [tA])
        tt(fw, "dve", yi[:, :, dst], xi[:, :, dst], tA[:, :, 0:m], ALU.add, [xi, tA], [yi])
        copy(fw, "pool", yi[:, :, keep], xi[:, :, keep], [xi], [yi])
        X, Y = Y, X
        sh *= 2
        k += 1
    return X, Y


def phase_s5_main(fw, g, l):
    import os
    stage = int(os.environ.get("S5_STAGE", "9"))
    with ExitStack() as st:
        selm = fw.sbuf(st, "m5selm", [128, 8, 240], BF16)
        fw.dma("sp", selm[:], g.cin["c_selm"], reads=[], writes=[selm])
        Dte = fw.sbuf(st, "m5Dte", [128, 64], F32)
        with fw.nc.allow_non_contiguous_dma(reason="tiny"):
            for t_ in range(8):
                fw.dma("sp", Dte[t_ * 16:(t_ + 1) * 16, :], g.inp["s5_d"][l, :].rearrange("(g e) -> e g", e=16),
                       reads=[], writes=[Dte])
        U32 = fw.sbuf(st, "m5U32", [128, 32, 4, NSC], BF16)
        uT = [fw.sbuf(st, f"m5uT{i}", [128, T], BF16) for i in range(2)]
        S = [(fw.sbuf(st, f"m5Sr{i}", [128, 16, NSC], F32), fw.sbuf(st, f"m5Si{i}", [128, 16, NSC], F32)) for i in range(2)]
        tA = fw.sbuf(st, "m5tA", [128, 16, NSC], F32)
        tB = fw.sbuf(st, "m5tB", [128, 16, NSC], F32)
        XPr = fw.sbuf(st, "m5XPr", [128, 16, NSC], BF16)
        XPi = fw.sbuf(st, "m5XPi", [128, 16, NSC], BF16)
        Lt = fw.sbuf(st, "m5Lt", [128, 8, 2, 16], F32)
        seed = fw.sbuf(st, "m5seed", [128, 4, 16], F32)
        Dl = [fw.sbuf(st, f"m5Dl{i}", [128, 10, 128], BF16) for i in range(2)]
        M2l = [[fw.sbuf(st, f"m5M2l{hf}_{i}", [128, 2, 4, 128], BF16) for i in range(2)] for hf in range(2)]
        Al = [[fw.sbuf(st, f"m5Al{hf}_{i}", [128, 2, 512], BF16) for i in range(2)] for hf in range(2)]
        for hf in range(2):
            for i in range(2):
                fw.op("pool", lambda h, b=M2l[hf][i]: h.memset(b[:], 0.0), [], [M2l[hf][i]])
                fw.op("pool", lambda h, b=Al[hf][i]: h.memset(b[:], 0.0), [], [Al[hf][i]])
        yfl = [fw.sbuf(st, f"m5yfl{i}", [128, 4, NSC], F32) for i in range(2)]
        yfs = [fw.sbuf(st, f"m5yfs{i}", [128, 4, NSC], F32) for i in range(2)]
        yt = fw.sbuf(st, "m5yt", [128, 4, NSC], F32)
        y2 = fw.sbuf(st, "m5y2", [128, 4, NSC], F32)
        Yg = fw.sbuf(st, "m5Yg", [128, 2, 8, 4, NSC], BF16)
        y5t = [fw.sbuf(st, f"m5y5t{i}", [128, T], BF16) for i in range(2)]
        ps_u = [fw.psum(st, f"m5psu{i}", [128, 2, NSC], F32) for i in range(2)]
        ps_s = [fw.psum(st, f"m5pss{i}", [128, 2, NSC], F32) for i in range(2)]
        ps_y = [fw.psum(st, f"m5psy{i}", [128, 2, NSC], F32) for i in range(2)]
        ps_r = [fw.psum(st, f"m5psr{i}", [128, NSC], F32) for i in range(2)]
        iu = 0
        il = 0
        isx = 0
        iy = 0
        ir = 0
        for gb in range(2):
            gbase = gb * 32
            for gi in range(32):
                gg = gbase + gi
                ci, gl = gg // 8, gg % 8
                ut = uT[ci % 2]
                if gl == 0:
                    fw.dma("sp", ut[:], g.uT[ci * 128:(ci + 1) * 128, :], reads=[g.uT], writes=[ut])
                u3 = ut[:].rearrange("p (c t) -> p c t", t=32)
                for rp in range(2):
                    pu = ps_u[iu % 2]
                    iu += 1
                    for r2 in range(2):
                        r = rp * 2 + r2
                        for tq in range(8):
                            mm(fw, pu[:, r2, :], selm[:, gl, 112 - tq * 16:240 - tq * 16], u3[:, :, r * 8 + tq],
                               tq == 0, tq == 7, [selm, ut], [pu])
                    copy(fw, "act" if rp == 0 else "dve", U32[:, gi, rp * 2:rp * 2 + 2, :], pu[:], [pu], [U32])
            if stage <= 1:
                continue
            for d in range(2):
                for hf in range(2):
                    fw.dma("sp", Lt[hf * 64:(hf + 1) * 64], g.s5L[d, :, :, :, gbase + hf * 16:gbase + hf * 16 + 16]
                           .rearrange("k r p g -> p k r g"), reads=[g.s5L], writes=[Lt])
                Sx, Sy = S
                for gj in range(16):
                    m2s = []
                    for hf in range(2):
                        gg = gbase + hf * 16 + gj
                        m2 = M2l[hf][il % 2]
                        if os.environ.get("S5_SKIP", "") != "dma":
                            fw.dma("sp", m2[:, :, :, hf * 64:(hf + 1) * 64], g.s5M2[d, gg], reads=[g.s5M2], writes=[m2])
                        m2s.append(m2)
                    il += 1
                    if os.environ.get("S5_SKIP", "") == "mm":
                        continue
                    ps = ps_s[isx % 2]
                    isx += 1
                    for ri in range(2):
                        for hf in range(2):
                            gi = hf * 16 + gj
                            for r in range(4):
                                mm(fw, ps[:, ri, :], m2s[hf][:, ri, r, :], U32[:, gi, r, :], hf == 0 and r == 0,
                                   hf == 1 and r == 3, [m2s[hf], U32], [ps])
                    copy(fw, "act", Sx[0][:, gj, :], ps[:, 0, :], [ps], [Sx[0]])
                    copy(fw, "act", Sx[1][:, gj, :], ps[:, 1, :], [ps], [Sx[1]])
                if stage <= 2:
                    continue
                if d == 0:
                    Xf, Yo = ks_scan(fw, Sx, Sy, tA, tB, Lt, 0, NSC, False)
                    for (src, dstb) in ((Xf[0], XPr), (Xf[1], XPi)):
                        fw.op("pool", lambda h, dstb=dstb: h.memset(dstb[:, :, 0:1], 0.0), [], [dstb])
                        copy(fw, "act", dstb[:, :, 1:NSC], src[:, :, 0:NSC - 1], [src], [dstb])
                else:
                    Xc, Yc = ks_scan(fw, Sx, Sy, tA, tB, Lt, 0, NCC, True)
                    if Xc[0] is not Sx[0]:
                        copy(fw, "act", Sx[0][:, :, 0:NCC], Xc[0][:, :, 0:NCC], [Xc[0]], [Sx[0]])
                        copy(fw, "act", Sx[1][:, :, 0:NCC], Xc[1][:, :, 0:NCC], [Xc[1]], [Sx[1]])
                    Xc, Yc = Sx, Sy
                    xr0, xi0 = Xc[0][:, :, 0], Xc[1][:, :, 0]
                    cmul(fw, seed[:, 0, :], seed[:, 1, :], Lt[:, 0, 0, :], Lt[:, 0, 1, :], xr0, xi0, seed[:, 2, :], seed[:, 3, :],
                         [Lt, Xc[0], Xc[1]], [seed], seed, seed)
                    tt(fw, "dve", Xc[0][:, :, NSC - 1], Xc[0][:, :, NSC - 1], seed[:, 0, :], ALU.add, [Xc[0], seed], [Xc[0]])
                    tt(fw, "dve", Xc[1][:, :, NSC - 1], Xc[1][:, :, NSC - 1], seed[:, 1, :], ALU.add, [Xc[1], seed], [Xc[1]])
                    Xf, Yo = ks_scan(fw, Xc, Yc, tA, tB, Lt, NCC, NSC, True)
                    for (srcl, srcc, dstb) in ((Xf[0], Xc[0], XPr), (Xf[1], Xc[1], XPi)):
                        copy(fw, "act", dstb[:, :, 0:NCC - 1], srcc[:, :, 1:NCC], [srcc], [dstb])
                        fw.op("pool", lambda h, dstb=dstb: h.memset(dstb[:, :, NCC - 1:NCC], 0.0), [], [dstb])
                        copy(fw, "act", dstb[:, :, NCC:NSC - 1], srcl[:, :, NCC + 1:NSC], [srcl], [dstb])
                        copy(fw, "act", dstb[:, :, NSC - 1:NSC], srcc[:, :, 0:1], [srcc], [dstb])
                if stage <= 3:
                    continue
                for gj in range(16):
                    for hf in range(2):
                        gi = hf * 16 + gj
                        gg = gbase + gi
                        b = iy % 2
                        iy += 1
                        Dm = Dl[b]
                        Am = Al[hf][(iy // 2) % 2]
                        fw.dma("sp", Dm[:], g.s5D[d, gg], reads=[g.s5D], writes=[Dm])
                        fw.dma("sp", Am[hf * 64:(hf + 1) * 64], g.s5A[d, gg], reads=[g.s5A], writes=[Am])
                        if d == 1:
                            fw.dma("sp", yfl[b][:], g.y5f_d[gg], reads=[g.y5f_d], writes=[yfl[b]])
                        bidx = 0
                        yset = (ps_y, ps_u)[iy % 2]
                        for I in range(4):
                            Js = list(range(0, I + 1)) if d == 0 else list(range(I, 4))
                            py = yset[I // 2]
                            for jn, J in enumerate(Js):
                                mm(fw, py[:, I % 2, :], Dm[:, bidx + jn, :], U32[:, gi, J, :], jn == 0, False, [Dm, U32], [py])
                            bidx += len(Js)
                            mm(fw, py[:, I % 2, :], Am[:, 0, I * 128:(I + 1) * 128], XPr[:, gj, :], False, False,
                               [Am, XPr], [py])
                            mm(fw, py[:, I % 2, :], Am[:, 1, I * 128:(I + 1) * 128], XPi[:, gj, :], False, True,
                               [Am, XPi], [py])
                        if d == 0:
                            ys_ = yfs[b]
                            copy(fw, "act", ys_[:, 0:2, :], yset[0][:], [yset[0]], [ys_])
                            copy(fw, "dve", ys_[:, 2:4, :], yset[1][:], [yset[1]], [ys_])
                            fw.dma("pool", g.y5f_d[gg], ys_[:], reads=[ys_], writes=[g.y5f_d])
                        else:
                            gl = gg % 8
                            tt(fw, "dve", yt[:, 0:2, :], yset[0][:], yfl[b][:, 0:2, :], ALU.add, [yset[0], yfl[b]], [yt])
                            tt(fw, "dve", yt[:, 2:4, :], yset[1][:], yfl[b][:, 2:4, :], ALU.add, [yset[1], yfl[b]], [yt])
                            stt(fw, yt[:], U32[:, gi, :, :], Dte[:, gg:gg + 1], yt[:], ALU.mult, ALU.add, [U32, Dte, yt], [yt])
                            act(fw, y2[:], yt[:], AF.Square, [yt], [y2])
                            ts(fw, "dve", y2[:], y2[:], 0.044715, 1.0, ALU.mult, ALU.add, [y2], [y2])
                            tt(fw, "pool", y2[:], y2[:], yt[:], ALU.mult, [y2, yt], [y2])
                            act(fw, y2[:], y2[:], AF.Sigmoid, [y2], [y2], scale=2.0 * math.sqrt(2.0 / math.pi))
                            tt(fw, "dve", Yg[:, hf, gl, :, :], y2[:], yt[:], ALU.mult, [y2, yt], [Yg])
                            if gl == 7 and hf == 1:
                              for hf2 in range(2):
                                ci = (gbase + hf2 * 16 + gj) // 8
                                yo_ = y5t[hf2]
                                y3 = yo_[:].rearrange("p (c t) -> p c t", t=32)
                                for I in range(4):
                                    for tq in range(8):
                                        pr_ = ps_r[ir % 2]
                                        ir += 1
                                        for g8 in range(8):
                                            mm(fw, pr_[:], selm[:, tq, 112 - g8 * 16:240 - g8 * 16], Yg[:, hf2, g8, I, :],
                                               g8 == 0, g8 == 7, [selm, Yg], [pr_])
                                        copy(fw, "act" if tq % 2 == 0 else "dve", y3[:, :, I * 8 + tq], pr_[:], [pr_], [yo_])
                                fw.dma("pool", g.y5g[ci * 128:(ci + 1) * 128, :], yo_[:], reads=[yo_], writes=[g.y5g])
        fw.end_phase(keep=g.keep)


def phase_s5_glu(fw, g, l, W):
    with ExitStack() as st:
        wgl = fw.sbuf(st, "gwgl", [128, 4, 8, 512], BF16)
        for cb in range(4):
            fw.dma("sp", wgl[:, cb], W.glu[cb], reads=[W.glu], writes=[wgl])
        gbias = fw.sbuf(st, "gbias", [128, 16], F32)
        with fw.nc.allow_non_contiguous_dma(reason="tiny"):
            fw.dma("sp", gbias[:], g.inp["s5_glu_b"][l, :].rearrange("(t p) -> p t", p=128), reads=[], writes=[gbias])
        yin = [fw.sbuf(st, f"gyin{i}", [128, 8, 512], BF16) for i in range(2)]
        sgt = [fw.sbuf(st, f"gsg{i}", [128, 512], F32) for i in range(2)]
        yo = [fw.sbuf(st, f"gyo{i}", [128, 8, 512], BF16) for i in range(2)]
        ps_a = [fw.psum(st, f"gpsa{i}", [128, 512], F32) for i in range(2)]
        ps_b = [fw.psum(st, f"gpsb{i}", [128, 512], F32) for i in range(2)]
        i2 = 0
        for bi, (t0, n) in enumerate(tblocks(l)):
            yi_ = yin[bi % 2]
            yo_ = yo[bi % 2]
            fw.dma("sp", yi_[:, :, 0:n], g.y5g[:, t0:t0 + n].rearrange("(k p) t -> p k t", p=128), reads=[g.y5g], writes=[yi_])
            for ot in range(8):
                pa = ps_a[i2 % 2]
                pb = ps_b[i2 % 2]
                sg = sgt[i2 % 2]
                i2 += 1
                for k in range(8):
                    mm(fw, pa[:, 0:n], wgl[:, ot // 4, k, (ot % 4) * 128:(ot % 4 + 1) * 128], yi_[:, k, 0:n], k == 0, k == 7,
                       [wgl, yi_], [pa])
                for k in range(8):
                    mm(fw, pb[:, 0:n], wgl[:, 2 + ot // 4, k, (ot % 4) * 128:(ot % 4 + 1) * 128], yi_[:, k, 0:n], k == 0, k == 7,
                       [wgl, yi_], [pb])
                act(fw, sg[:, 0:n], pb[:, 0:n], AF.Sigmoid, [pb, gbias], [sg], bias=gbias[:, 8 + ot:9 + ot])
                stt(fw, yo_[:, ot, 0:n], pa[:, 0:n], gbias[:, ot:ot + 1], sg[:, 0:n], ALU.add, ALU.mult, [pa, gbias, sg], [yo_])
            fw.dma("pool", g.y5T[:, t0:t0 + n].rearrange("(k p) t -> p k t", p=128), yo_[:, :, 0:n], reads=[yo_], writes=[g.y5T])
        fw.end_phase(keep=g.keep)
```

```python
import math
from contextlib import ExitStack
import numpy as np
import concourse.bass as bass
import concourse.mybir as mybir
from concourse.bass_utils import run_bass_kernel_spmd

F32 = mybir.dt.float32
BF16 = mybir.dt.bfloat16
AF = mybir.ActivationFunctionType
ALU = mybir.AluOpType
AX = mybir.AxisListType

D = 2048
KD = 16
CTX = 256
LAT = 4096
T = CTX + LAT
NT = T // 128
DEPTH = 2
N_IN = 12832
Q_OFF, K_OFF, V_OFF, Z_OFF, XBC_OFF, DT_OFF, U_OFF, GATE_OFF = 0, 1024, 2048, 3072, 4096, 5632, 5664, 6688
D_FF = 5632
EPS = 1e-6
NCORES = 8
SAME_ENGINE_SYNC = True
PUMP_EVERY = 12


class Buf:
    __slots__ = ("t", "name", "w", "r", "dsem", "dcnt", "loose")

    def __init__(self, t, name, loose=False):
        self.t = t
        self.name = name
        self.w = None
        self.r = {}
        self.dsem = None
        self.dcnt = 0
        self.loose = loose

    def __getitem__(self, k):
        return self.t[k]


class Eng:
    def __init__(self, name, h):
        self.name = name
        self.h = h
        self.sem = None
        self.cnt = 0
        self.known = {}
        self.own = set()


class FW:
    SEM_ROLL = 30000

    def __init__(self, nc, stack):
        self.nc = nc
        self.stack = stack
        self.nsem = 0
        self.E = {"pe": Eng("pe", nc.tensor), "act": Eng("act", nc.scalar), "dve": Eng("dve", nc.vector),
                  "pool": Eng("pool", nc.gpsimd), "sp": Eng("sp", nc.sync)}
        self.ninst = 0
        self.dbufs = []
        self.allsems = []
        self.sempool = []
        self.phase_bufs = []

    def new_sem(self, name):
        self.nsem += 1
        s = self.stack.enter_context(self.nc.semaphore(f"{name}{self.nsem}"))
        return s

    def sbuf(self, st, name, shape, dt):
        self.uid = getattr(self, "uid", 0) + 1
        name = f"{name}_u{self.uid}"
        b = Buf(st.enter_context(self.nc.sbuf_tensor(name, list(shape), dt)), name)
        self.phase_bufs.append(b)
        return b

    def psum(self, st, name, shape, dt):
        self.uid = getattr(self, "uid", 0) + 1
        name = f"{name}_u{self.uid}"
        b = Buf(st.enter_context(self.nc.psum_tensor(name, list(shape), dt)), name)
        self.phase_bufs.append(b)
        return b

    def end_phase(self, keep=()):
        self.barrier()
        for b in self.phase_bufs:
            if any(b is k for k in keep):
                continue
            if b.dsem is not None:
                self.sempool.append((b.dsem, b.dcnt))
                b.dsem = None
                self.dbufs = [x for x in self.dbufs if x is not b]
        self.phase_bufs = [b for b in self.phase_bufs if any(b is k for k in keep)]

    def dram(self, name, shape, dt, kind="Internal", loose=True):
        t = self.nc.dram_tensor(name, list(shape), dt, kind=kind)
        return Buf(t.ap(), name, loose=loose)

    def _wait(self, e, tok):
        if tok is None:
            return
        sem, val = tok
        if id(sem) in e.own and (e.name == "pe" or not SAME_ENGINE_SYNC):
            return
        if e.known.get(id(sem), 0) >= val:
            return
        e.h.wait_ge(sem, val)
        e.known[id(sem)] = val

    def _deps(self, e, reads, writes):
        for b in reads:
            self._wait(e, b.w)
        for b in writes:
            if b.loose:
                continue
            self._wait(e, b.w)
            for t in b.r.values():
                self._wait(e, t)

    def _commit(self, tok, reads, writes):
        for b in reads:
            if not b.loose:
                b.r[id(tok[0])] = tok
        for b in writes:
            b.w = tok
            b.r = {}

    def op(self, eng, fn, reads=(), writes=()):
        e = self.E[eng]
        if e.sem is None or e.cnt >= self.SEM_ROLL:
            e.sem = self.new_sem("p" + eng)
            e.own.add(id(e.sem))
            e.cnt = 0
            self.allsems.append(e)
        self._deps(e, reads, writes)
        ins = fn(e.h)
        ins.then_inc(e.sem, 1)
        e.cnt += 1
        tok = (e.sem, e.cnt)
        self._commit(tok, reads, writes)
        self.ninst += 1
        return tok

    def dma(self, q, out_ap, in_ap, reads=(), writes=(), **kw):
        e = self.E[q]
        wb = writes[0]
        if wb.dsem is None or wb.dcnt >= 16 * 2000:
            if wb.dsem is not None:
                self._wait(e, (wb.dsem, wb.dcnt))
                wb.dsem = None
            if self.sempool and self.sempool[-1][1] < 16 * 1500:
                wb.dsem, wb.dcnt = self.sempool.pop()
            else:
                wb.dsem = self.new_sem("d")
                wb.dcnt = 0
            if not any(wb is x for x in self.dbufs):
                self.dbufs.append(wb)
        self._deps(e, reads, writes)
        ins = e.h.dma_start(out=out_ap, in_=in_ap, **kw)
        ins.then_inc(wb.dsem, 16)
        wb.dcnt += 16
        tok = (wb.dsem, wb.dcnt)
        self._commit(tok, reads, writes)
        self.ninst += 1
        return tok

    def barrier(self):
        toks = []
        for e2 in self.E.values():
            if e2.sem is not None and e2.cnt > 0:
                toks.append((e2.sem, e2.cnt))
        for b in self.dbufs:
            if b.dsem is not None and b.dcnt > 0:
                toks.append((b.dsem, b.dcnt))
        for e in self.E.values():
            for tok in toks:
                if id(tok[0]) in e.own:
                    if e.name == "pe":
                        continue
                self._wait(e, tok)


def act(fw, out, in_, func, reads, writes, eng="act", **kw):
    return fw.op(eng, lambda h: h.activation(out=out, in_=in_, func=func, **kw), reads, writes)


def tt(fw, eng, out, in0, in1, op, reads, writes):
    return fw.op(eng, lambda h: h.tensor_tensor(out=out, in0=in0, in1=in1, op=op), reads, writes)


def ts(fw, eng, out, in0, s1, s2, op0, op1, reads, writes):
    if s2 is None:
        return fw.op(eng, lambda h: h.tensor_scalar(out=out, in0=in0, scalar1=s1, scalar2=None, op0=op0), reads, writes)
    return fw.op(eng, lambda h: h.tensor_scalar(out=out, in0=in0, scalar1=s1, scalar2=s2, op0=op0, op1=op1), reads, writes)


def stt(fw, out, in0, scalar, in1, op0, op1, reads, writes):
    return fw.op("dve", lambda h: h.scalar_tensor_tensor(out=out, in0=in0, scalar=scalar, in1=in1, op0=op0, op1=op1),
                 reads, writes)


def mm(fw, out, lhsT, rhs, start, stop, reads, writes):
    return fw.op("pe", lambda h: h.matmul(out, lhsT, rhs, start=start, stop=stop), reads, writes)


def copy(fw, eng, out, in_, reads, writes):
    if eng == "act":
        return fw.op("act", lambda h: h.copy(out=out, in_=in_), reads, writes)
    return fw.op(eng, lambda h: h.tensor_copy(out=out, in_=in_), reads, writes)


class Ctx:
    pass


def declare_weights(fw, g, l):
    W = Ctx()
    W.win = fw.dram(f"win{l}", [26, 128, KD, 512], BF16)
    W.glu = fw.dram(f"glu{l}", [4, 128, 8, 512], BF16)
    W.wbr = [fw.dram(f"wbr{l}_{n}", [4, 128, 8, 512], BF16) for n in range(3)]
    W.wout = fw.dram(f"wout{l}", [4, 128, KD, 512], BF16)
    W.wg = fw.dram(f"wg{l}", [11, 128, KD, 512], BF16)
    W.wu = fw.dram(f"wu{l}", [11, 128, KD, 512], BF16)
    W.wd = fw.dram(f"wd{l}", [4, 128, 44, 512], BF16)
    return W


def cast_units(fw, g, l, W, which):
    def generic(dst, src, R, C):
        for kc in range(R // 128):
            stg = g.stg[g.stg_i % 2]
            g.stg_i += 1
            fw.dma("pool", stg[:, 0:C], src[kc * 128:(kc + 1) * 128, :], reads=[g.wsrc], writes=[stg], max_dma_last_dim=4096)
            fw.dma("sp", dst[:, :, kc, :].rearrange("b p c -> p b c"),
                   stg[:, 0:C].rearrange("p (b c) -> p b c", c=512), reads=[stg], writes=[dst])
            yield
    if which == "win":
        src = g.inp["w_in"][l]
        win = W.win
        for kc in range(KD):
            stg = g.stg[g.stg_i % 2]
            g.stg_i += 1
            fw.dma("pool", stg[:, :], src[kc * 128:(kc + 1) * 128, :], reads=[g.wsrc], writes=[stg], max_dma_last_dim=4096)
            fw.dma("sp", win[0:11, :, kc, :].rearrange("b p c -> p b c"),
                   stg[:, 0:5632].rearrange("p (b c) -> p b c", c=512), reads=[stg], writes=[win])
            fw.dma("sp", win[11, :, kc, 0:32], stg[:, 5632:5664], reads=[stg], writes=[win])
            fw.dma("sp", win[12:26, :, kc, :].rearrange("b p c -> p b c"),
                   stg[:, 5664:12832].rearrange("p (b c) -> p b c", c=512), reads=[stg], writes=[win])
            yield
    else:
        yield from generic(W.glu, g.inp["s5_glu_w"][l], 1024, 2048)
        for n in range(3):
            yield from generic(W.wbr[n], g.inp["w_branch"][l, n], 1024, 2048)
        yield from generic(W.wout, g.inp["w_out"][l], 2048, 2048)
        yield from generic(W.wg, g.inp["ffn_w_gate"][l], 2048, D_FF)
        yield from generic(W.wu, g.inp["ffn_w_up"][l], 2048, D_FF)
        yield from generic(W.wd, g.inp["ffn_w_down"][l], D_FF, 2048)


def phase_cast_blocking(fw, g, gens):
    with ExitStack() as st:
        g.stg = [fw.sbuf(st, f"stg{i}", [128, N_IN], BF16) for i in range(2)]
        g.stg_i = 0
        for gen in gens:
            for _ in gen:
                pass
        fw.end_phase()


def phase_adaln(fw, g):
    g.mod_d = [fw.dram(f"mod{l}", [2, 6 * D], F32) for l in range(DEPTH)]
    with ExitStack() as st:
        craw = fw.sbuf(st, "craw", [128, KD, 2], F32)
        sc = fw.sbuf(st, "sc", [128, KD, 2], F32)
        sig = fw.sbuf(st, "sig", [128, KD, 2], F32)
        with fw.nc.allow_non_contiguous_dma(reason="tiny"):
            fw.dma("sp", craw[:, :, 0], g.inp["c"][0, :].rearrange("(k p) -> p k", p=128), reads=[], writes=[craw])
            fw.dma("sp", craw[:, :, 1], g.inp["c_ctx"][0, :].rearrange("(k p) -> p k", p=128), reads=[], writes=[craw])
        act(fw, sig[:], craw[:], AF.Sigmoid, [craw], [sig])
        tt(fw, "dve", sc[:], craw[:], sig[:], ALU.mult, [craw, sig], [sc])
        wb = [fw.sbuf(st, f"adaw{i}", [128, KD, 512], F32) for i in range(2)]
        ps = [fw.psum(st, f"adaps{i}", [2, 512], F32) for i in range(2)]
        bias = fw.sbuf(st, "adab", [2, 6 * D], F32)
        row = fw.sbuf(st, "adarow", [2, 6 * D], F32)
        it = 0
        for l in range(DEPTH):
            fw.dma("sp", bias[:, :], g.inp["ada_b"][l:l + 1, :].partition_broadcast(2), reads=[], writes=[bias])
            for cb in range(24):
                w = wb[it % 2]
                p = ps[it % 2]
                it += 1
                fw.dma("sp", w[:], g.inp["ada_w"][l, :, cb * 512:(cb + 1) * 512].rearrange("(k p) c -> p k c", p=128),
                       reads=[], writes=[w])
                for k in range(KD):
                    mm(fw, p[:, :], sc[:, k, :], w[:, k, :], k == 0, k == KD - 1, [sc, w], [p])
                tt(fw, "dve", row[:, cb * 512:(cb + 1) * 512], p[:, :], bias[:, cb * 512:(cb + 1) * 512], ALU.add,
                   [p, bias], [row])
            fw.dma("sp", g.mod_d[l][:, :], row[:, :], reads=[row], writes=[g.mod_d[l]])
        fw.end_phase()


def load_bcast(fw, q, dst, src_row_ap, src_buf=None):
    fw.dma(q, dst[:], src_row_ap.partition_broadcast(128), reads=[src_buf] if src_buf else [], writes=[dst])


def make_mod_tiles(fw, st, g, l, which):
    res = []
    gain = g.inp["norm_mix_pre" if which == 0 else "norm_ffn_pre"]
    gt = fw.sbuf(st, f"gainb{which}", [128, D], F32)
    load_bcast(fw, "sp", gt, gain[l:l + 1, :])
    for r in range(2):
        G = fw.sbuf(st, f"G{which}{r}", [128, D], F32)
        S = fw.sbuf(st, f"S{which}{r}", [128, D], F32)
        load_bcast(fw, "sp", G, g.mod_d[l][r:r + 1, (3 * which + 1) * D:(3 * which + 2) * D], g.mod_d[l])
        load_bcast(fw, "sp", S, g.mod_d[l][r:r + 1, (3 * which) * D:(3 * which + 1) * D], g.mod_d[l])
        stt(fw, G[:], G[:], 1.0, gt[:], ALU.add, ALU.mult, [G, gt], [G])
        res.append((G, S))
    return res


def norm_mod_tile(fw, g, xt, G, S, junk, ss, tmp, hb):
    act(fw, junk[:], xt[:], AF.Square, [xt], [junk, ss], accum_out=ss[:, 0:1])
    act(fw, ss[:, 1:2], ss[:, 0:1], AF.Sqrt, [ss], [ss], scale=1.0 / D, bias=g.eps_t[:, 0:1])
    fw.op("dve", lambda h: h.reciprocal(out=ss[:, 2:3], in_=ss[:, 1:2]), [ss], [ss])
    stt(fw, tmp[:], xt[:], ss[:, 2:3], G[:], ALU.mult, ALU.mult, [xt, ss, G], [tmp])
    tt(fw, "pool", hb[:], tmp[:], S[:], ALU.add, [tmp, S], [hb])


def transpose_tile(fw, g, hb, ps2, dstT, col0, nk=KD):
    for half in range((nk + 7) // 8):
        p = ps2[half % 2]
        kk = min(8, nk - half * 8)
        for j in range(kk):
            k = half * 8 + j
            fw.op("pe", lambda h, k=k, j=j: h.transpose(p[:, j * 128:(j + 1) * 128], hb[:, k * 128:(k + 1) * 128],
                                                       g.ident[:]), [hb, g.ident], [p])
        copy(fw, "act", dstT[:, half * 8:half * 8 + kk, col0:col0 + 128],
             p[:, 0:kk * 128].rearrange("p (k t) -> p k t", t=128), [p], [dstT])


def phase_norm1(fw, g, l):
    with ExitStack() as st:
        mods = make_mod_tiles(fw, st, g, l, 0)
        xts = [fw.sbuf(st, f"xt{i}", [128, D], F32) for i in range(2)]
        junk = fw.sbuf(st, "junk", [128, D], BF16)
        tmp = fw.sbuf(st, "tmp", [128, D], F32)
        hbs = [fw.sbuf(st, f"hb{i}", [128, D], BF16) for i in range(2)]
        sss = [fw.sbuf(st, f"ss{i}", [128, 4], F32) for i in range(2)]
        hts = [fw.sbuf(st, f"hTs{i}", [128, KD, 128], BF16) for i in range(2)]
        ps2 = [fw.psum(st, f"trps{i}", [128, 1024], BF16) for i in range(2)]
        for i in range(NT):
            xt = xts[i % 2]
            hb = hbs[i % 2]
            ss = sss[i % 2]
            ht = hts[i % 2]
            fw.dma("sp", xt[:], g.xsrc(l, i), reads=[g.xres], writes=[xt])
            G, S = mods[1] if i < 2 else mods[0]
            norm_mod_tile(fw, g, xt, G, S, junk, ss, tmp, hb)
            transpose_tile(fw, g, hb, ps2, ht, 0)
            fw.dma("pool", g.hT_d[:, i * 128:(i + 1) * 128].rearrange("(k p) t -> p k t", p=128), ht[:],
                   reads=[ht], writes=[g.hT_d])
        fw.end_phase()


def proj_plan():
    plan = []
    for j in range(2):
        plan.append((0 + j, "f", "qT", j * 512))
    for j in range(2):
        plan.append((2 + j, "f", "kT", j * 512))
    for j in range(2):
        plan.append((4 + j, "t", "v_tok", j * 512))
    for j in range(2):
        plan.append((6 + j, "t", "z_tok", j * 512))
    for j in range(3):
        plan.append((8 + j, "f", "xbcT", j * 512))
    plan.append((11, "d", "dtT", 0))
    for j in range(2):
        plan.append((12 + j, "f", "uT", j * 512))
    return plan


def phase_proj(fw, g, l, W):
    with ExitStack() as st:
        hts = [fw.sbuf(st, f"hTb{i}", [128, KD, 512], BF16) for i in range(2)]
        wts = [fw.sbuf(st, f"wt{i}", [128, KD, 512], BF16) for i in range(3)]
        pss = [fw.psum(st, f"pps{i}", [128, 512], F32) for i in range(4)]
        stg = [fw.sbuf(st, f"pst{i}", [128, 4, 512], BF16) for i in range(2)]
        stgd = [fw.sbuf(st, f"pstd{i}", [32, 512], F32) for i in range(2)]
        plan = proj_plan()
        wi = 0
        pi = 0
        si = 0
        ntb = (T + 511) // 512
        for tb in range(ntb):
            t0 = tb * 512
            n = min(512, T - t0)
            ht = hts[tb % 2]
            fw.dma("sp", ht[:, :, 0:n], g.hT_d[:, t0:t0 + n].rearrange("(k p) t -> p k t", p=128),
                   reads=[g.hT_d], writes=[ht])
            for (blk, kind, dname, off) in plan:
                w = wts[wi % 3]
                wi += 1
                fw.dma("sp", w[:], W.win[blk], reads=[W.win], writes=[w])
                dst = getattr(g, dname)
                if kind == "f":
                    sg = stg[si % 2]
                    si += 1
                    for j in range(4):
                        p = pss[pi % 4]
                        pi += 1
                        for k in range(KD):
                            mm(fw, p[:, 0:n], w[:, k, j * 128:(j + 1) * 128], ht[:, k, 0:n], k == 0, k == KD - 1,
                               [w, ht], [p])
                        copy(fw, "act" if j % 2 == 0 else "dve", sg[:, j, 0:n], p[:, 0:n], [p], [sg])
                    fw.dma("pool", dst[off:off + 512, t0:t0 + n].rearrange("(j p) t -> p j t", p=128), sg[:, :, 0:n],
                           reads=[sg], writes=[dst])
                elif kind == "d":
                    p = pss[pi % 4]
                    pi += 1
                    sd = stgd[tb % 2]
                    for k in range(KD):
                        mm(fw, p[0:32, 0:n], w[:, k, 0:32], ht[:, k, 0:n], k == 0, k == KD - 1, [w, ht], [p])
                    copy(fw, "dve", sd[:, 0:n], p[0:32, 0:n], [p], [sd])
                    fw.dma("pool", dst[:, t0:t0 + n], sd[:, 0:n], reads=[sd], writes=[dst])
                else:
                    sg = stg[si % 2]
                    si += 1
                    ns = n // 128
                    for s in range(ns):
                        p = pss[pi % 4]
                        pi += 1
                        for k in range(KD):
                            mm(fw, p[:, :], ht[:, k, s * 128:(s + 1) * 128], w[:, k, :], k == 0, k == KD - 1,
                               [w, ht], [p])
                        copy(fw, "act" if s % 2 == 0 else "dve", sg[:, s, :], p[:, :], [p], [sg])
                    fw.dma("pool", dst[t0:t0 + n, off:off + 512].rearrange("(s p) c -> p s c", p=128), sg[:, 0:ns, :],
                           reads=[sg], writes=[dst])
        fw.end_phase()


def declare_inputs(fw, g):
    nc = fw.nc
    shapes = {
        "x": [LAT, D], "c": [1, D], "ctx": [CTX, D], "c_ctx": [1, D],
        "ada_w": [2, D, 6 * D], "ada_b": [2, 6 * D],
        "norm_mix_pre": [2, D], "norm_mix_post": [2, D], "norm_ffn_pre": [2, D], "norm_ffn_post": [2, D],
        "w_in": [2, D, N_IN], "da_lambda": [2, 256], "da_subln": [2, 128],
        "ssd_conv_w": [2, 5, 1536], "ssd_conv_b": [2, 1536], "ssd_dt_bias": [2, 32], "ssd_a_log": [2, 32],
        "ssd_d": [2, 16], "ssd_norm": [2, 1024],
        "s5_lam_re": [2, 2, 64, 64], "s5_lam_im": [2, 2, 64, 64], "s5_log_step": [2, 128],
        "s5_b_re": [2, 64, 64, 16], "s5_b_im": [2, 64, 64, 16], "s5_c_re": [2, 64, 16, 64], "s5_c_im": [2, 64, 16, 64],
        "s5_d": [2, 1024], "s5_glu_w": [2, 1024, 2048], "s5_glu_b": [2, 2048],
        "w_branch": [2, 3, 1024, 2048], "w_out": [2, D, D],
        "ffn_w_gate": [2, D, D_FF], "ffn_w_up": [2, D, D_FF], "ffn_w_down": [2, D_FF, D],
    }
    g.inp = {}
    for k, s in shapes.items():
        g.inp[k] = nc.dram_tensor(k, s, F32, kind="ExternalInput").ap()
    g.wsrc = Buf(None, "wsrc", loose=True)
    g.cin = {}
    for k, (s, dt) in const_specs().items():
        g.cin[k] = nc.dram_tensor(k, list(s), dt, kind="ExternalInput").ap()


def const_specs():
    return {
        "c_ident": ((128, 128), BF16),
        "c_cos": ((128, LAT), F32),
        "c_sin": ((128, LAT), F32),
        "c_rt": ((128, 128), BF16),
        "c_maskf": ((128, 128), F32),
        "c_maskb": ((128, 128), F32),
        "c_s5mf": ((128, 128), F32),
        "c_s5mb": ((128, 128), F32),
        "c_selm": ((128, 8, 240), BF16),
    }


def make_consts():
    import ml_dtypes
    c = {}
    c["c_ident"] = np.eye(128, dtype=np.float32).astype(ml_dtypes.bfloat16)
    t = np.arange(LAT)
    row = (t // 64).astype(np.float32)
    col = (t % 64).astype(np.float32)
    inv = (np.float32(10000.0) ** (-np.arange(0, 32, 2, dtype=np.float32) / np.float32(32))).astype(np.float32)
    ar = row[:, None] * inv[None, :]
    ac = col[:, None] * inv[None, :]
    ang = np.concatenate([ar, ar, ac, ac], axis=1).astype(np.float32)
    cosT = np.cos(ang).T.astype(np.float32)
    sinT = np.sin(ang).T.astype(np.float32)
    c["c_cos"] = np.ascontiguousarray(np.concatenate([cosT, cosT], 0))
    c["c_sin"] = np.ascontiguousarray(np.concatenate([sinT, sinT], 0))
    R = np.zeros((64, 64), np.float32)
    for m in list(range(0, 16)) + list(range(32, 48)):
        R[m, m + 16] = -1.0
    for m in list(range(16, 32)) + list(range(48, 64)):
        R[m, m - 16] = 1.0
    R2 = np.zeros((128, 128), np.float32)
    R2[:64, :64] = R
    R2[64:, 64:] = R
    c["c_rt"] = np.ascontiguousarray(R2.T).astype(ml_dtypes.bfloat16)
    jj = np.arange(128)[:, None]
    ii = np.arange(128)[None, :]
    c["c_maskf"] = np.where(ii >= jj, 0.0, -30000.0).astype(np.float32)
    c["c_maskb"] = np.where(jj >= ii, 0.0, -30000.0).astype(np.float32)
    c["c_s5mf"] = ((ii // 16) >= (jj // 16)).astype(np.float32)
    c["c_s5mb"] = ((jj // 16) >= (ii // 16)).astype(np.float32)
    sel = np.zeros((128, 8, 240), np.float32)
    for gl in range(8):
        for e in range(16):
            sel[gl * 16 + e, gl, 112 + e] = 1.0
    c["c_selm"] = sel.astype(ml_dtypes.bfloat16)
    return c


def build(stop_after=None, debug=(), inject=(), fast=False):
    nc = bass.Bass("TRN2", target_bir_lowering=False)
    g = Ctx()
    with ExitStack() as st:
        fw = FW(nc, st)
        g.fw = fw
        declare_inputs(fw, g)
        out = fw.dram("out", [LAT, D], F32, kind="ExternalOutput")
        g.out = out
        g.xres = fw.dram("xres", [T, D], F32)
        g.hT_d = fw.dram("hT_d", [D, T], BF16)
        g.qT = fw.dram("qT", [1024, T], BF16)
        g.kT = fw.dram("kT", [1024, T], BF16)
        g.v_tok = fw.dram("v_tok", [T, 1024], BF16)
        g.z_tok = fw.dram("z_tok", [T, 1024], BF16)
        g.xbcT = fw.dram("xbcT", [1536, T], BF16)
        g.dtT = fw.dram("dtT", [32, T], F32)
        g.uT = fw.dram("uT", [1024, T], BF16)
        g.qTr = fw.dram("qTr", [1024, T], BF16)
        g.kTr = fw.dram("kTr", [1024, T], BF16)
        g.yaT = fw.dram("yaT", [1024, T], BF16)
        g.ysT = fw.dram("ysT", [1024, T], BF16)
        g.y5T = fw.dram("y5T", [1024, T], BF16)
        g.xs_tok = fw.dram("xs_tok", [T, 1024], BF16)
        g.B_tok = fw.dram("B_tok", [T, 256], BF16)
        g.BT = fw.dram("BT", [256, T], BF16)
        g.CT = fw.dram("CT", [256, T], BF16)
        g.ssd_q = fw.dram("ssd_q", [2, 6, 16, T], F32)
        g.ssd_cd = fw.dram("ssd_cd", [2, 16, NT], F32)
        g.yf_d = fw.dram("yf_d", [T, 1024], F32)
        g.s5L = fw.dram("s5L", [2, 8, 2, 64, 64], F32)
        g.s5A = fw.dram("s5A", [2, 64, 64, 2, 512], BF16)
        g.s5D = fw.dram("s5D", [2, 64, 128, 10, 128], BF16)
        g.s5M2 = fw.dram("s5M2", [2, 64, 128, 2, 4, 64], BF16)
        g.y5f_d = fw.dram("y5f_d", [64, 128, 4, NSC], F32)
        g.y5g = fw.dram("y5g", [1024, T], BF16)
        inj = {}
        for name in inject:
            src = getattr(g, name)
            inj[name] = nc.dram_tensor("inj_" + name, list(src.t.shape), src.t.dtype, kind="ExternalInput").ap()
        dbg = {}
        for name in debug:
            src = getattr(g, name)
            dbg[name] = fw.dram("dbg_" + name, list(src.t.shape), src.t.dtype, kind="ExternalOutput")

        def xsrc(l, i):
            if l == 0:
                if i < 2:
                    return g.inp["ctx"][i * 128:(i + 1) * 128, :]
                return g.inp["x"][(i - 2) * 128:(i - 1) * 128, :]
            return g.xres[i * 128:(i + 1) * 128, :]
        g.xsrc = xsrc

        g.ident = fw.sbuf(st, "ident", [128, 128], BF16)
        fw.dma("sp", g.ident[:], g.cin["c_ident"], reads=[], writes=[g.ident])
        g.eps_t = fw.sbuf(st, "eps_t", [128, 1], F32)
        fw.op("dve", lambda h: h.memset(g.eps_t[:], EPS), [], [g.eps_t])
        g.one_t = fw.sbuf(st, "one_t", [128, 1], F32)
        fw.op("dve", lambda h: h.memset(g.one_t[:], 1.0), [], [g.one_t])
        g.keep = [g.ident, g.eps_t, g.one_t]
        fw.phase_bufs = []

        def done():
            for name in debug:
                src = getattr(g, name)
                fw.dma("sp", dbg[name][:], src[:], reads=[src], writes=[dbg[name]])
            fw.barrier()

        Ws = [declare_weights(fw, g, l) for l in range(DEPTH)]
        phase_cast_blocking(fw, g, [cast_units(fw, g, 0, Ws[0], "win")])
        phase_adaln(fw, g)
        for l in range(DEPTH):
            W = Ws[l]
            phase_norm1(fw, g, l)
            phase_proj(fw, g, l, W)
            if stop_after == "proj":
                done()
                return nc
            if not fast:
                phase_rope(fw, g, l)
                bg = []
                if l == 0:
                    bg = [cast_units(fw, g, 0, Ws[0], "rest"), cast_units(fw, g, 1, Ws[1], "win"),
                          cast_units(fw, g, 1, Ws[1], "rest")]
                phase_attn(fw, g, l, bg)
            if stop_after == "attn":
                done()
                return nc
            if stop_after != "skipssd" and not fast:
                phase_ssd_prep(fw, g, l)
                phase_ssd_scan(fw, g, l)
            if stop_after == "ssd":
                done()
                return nc
            if "y5T" not in inject or l > 0:
                phase_s5_params(fw, g, l)
                if stop_after == "s5p":
                    done()
                    return nc
                phase_s5_main(fw, g, l)
                if stop_after == "s5m":
                    done()
                    return nc
                phase_s5_glu(fw, g, l, W)
            if stop_after == "s5":
                done()
                return nc
            if l == 0:
                for name in inject:
                    dstb = getattr(g, name)
                    fw.dma("sp", dstb[:], inj[name], reads=[], writes=[dstb])
                fw.barrier()
            phase_merge(fw, g, l, W)
            phase_ffn(fw, g, l, W)
            if stop_after == f"ffn{l}":
                done()
                return nc
        done()
    return nc


_IN_KEYS = ["x", "c", "ctx", "c_ctx", "ada_w", "ada_b", "norm_mix_pre", "norm_mix_post", "norm_ffn_pre",
            "norm_ffn_post", "w_in", "da_lambda", "da_subln", "ssd_conv_w", "ssd_conv_b", "ssd_dt_bias",
            "ssd_a_log", "ssd_d", "ssd_norm", "s5_lam_re", "s5_lam_im", "s5_log_step", "s5_b_re", "s5_b_im",
            "s5_c_re", "s5_c_im", "s5_d", "s5_glu_w", "s5_glu_b", "w_branch", "w_out", "ffn_w_gate", "ffn_w_up",
            "ffn_w_down"]


def make_in_map(inputs, b):
    f = lambda a: np.ascontiguousarray(np.asarray(a, dtype=np.float32))
    m = {}
    for k in _IN_KEYS:
        a = inputs[k]
        if k == "x":
            m[k] = f(a[b])
        elif k == "ctx":
            m[k] = f(a[b])
        elif k == "c":
            m[k] = f(a[b:b + 1])
        elif k == "c_ctx":
            m[k] = f(a).reshape(1, D)
        elif k == "da_lambda":
            m[k] = f(a).reshape(2, 256)
        elif k in ("ssd_dt_bias", "ssd_a_log"):
            m[k] = f(a).reshape(2, 32)
        elif k == "s5_log_step":
            m[k] = f(a).reshape(2, 128)
        else:
            m[k] = f(a)
    m.update(make_consts())
    return m


def kernel(**inputs):
    nc = build()
    maps = [make_in_map(inputs, 0), make_in_map(inputs, 1)]
    in_maps = [maps[0] if c < NCORES // 2 else maps[1] for c in range(NCORES)]
    res = run_bass_kernel_spmd(nc, in_maps, core_ids=list(range(NCORES)))
    o0 = np.asarray(res.results[0]["out"], dtype=np.float32)
    o1 = np.asarray(res.results[NCORES // 2]["out"], dtype=np.float32)
    return np.stack([o0, o1], axis=0)


def tblocks(l):
    blks = [(0, CTX)] if l < DEPTH - 1 else []
    return blks + [(CTX + 512 * b, 512) for b in range(LAT // 512)]


def phase_rope(fw, g, l):
    with ExitStack() as st:
        cos = fw.sbuf(st, "cos", [128, LAT], F32)
        sin = fw.sbuf(st, "sin", [128, LAT], F32)
        rt = fw.sbuf(st, "rt", [128, 128], BF16)
        fw.dma("sp", cos[:], g.cin["c_cos"], reads=[], writes=[cos])
        fw.dma("sp", sin[:], g.cin["c_sin"], reads=[], writes=[sin])
        fw.dma("sp", rt[:], g.cin["c_rt"], reads=[], writes=[rt])
        qs = [fw.sbuf(st, f"rq{i}", [128, 512], BF16) for i in range(3)]
        t1 = [fw.sbuf(st, f"rt1{i}", [128, 512], F32) for i in range(2)]
        t2 = [fw.sbuf(st, f"rt2{i}", [128, 512], F32) for i in range(2)]
        ob = [fw.sbuf(st, f"rob{i}", [128, 512], BF16) for i in range(3)]
        ps = [fw.psum(st, f"rps{i}", [128, 512], F32) for i in range(2)]
        it = 0
        for (src, dst) in ((g.qT, g.qTr), (g.kT, g.kTr)):
            fw.dma("pool", dst[:, 0:CTX], src[:, 0:CTX], reads=[src], writes=[dst])
            for j in range(8):
                for b in range(LAT // 512):
                    q = qs[it % 3]
                    o = ob[it % 3]
                    a = t1[it % 2]
                    c = t2[it % 2]
                    p = ps[it % 2]
                    it += 1
                    t0 = CTX + b * 512
                    fw.dma("sp", q[:], src[j * 128:(j + 1) * 128, t0:t0 + 512], reads=[src], writes=[q])
                    mm(fw, p[:], rt[:], q[:], True, True, [rt, q], [p])
                    tt(fw, "dve", a[:], q[:], cos[:, b * 512:(b + 1) * 512], ALU.mult, [q, cos], [a])
                    tt(fw, "dve", c[:], p[:], sin[:, b * 512:(b + 1) * 512], ALU.mult, [p, sin], [c])
                    tt(fw, "pool", o[:], a[:], c[:], ALU.add, [a, c], [o])
                    fw.dma("pool", dst[j * 128:(j + 1) * 128, t0:t0 + 512], o[:], reads=[o], writes=[dst])
        fw.end_phase(keep=g.keep)


def phase_attn(fw, g, l, bg=()):
    lam_init = 0.8 - 0.6 * math.exp(-0.3 * l)
    with ExitStack() as st:
        bg = list(bg)
        if bg:
            g.stg = [fw.sbuf(st, f"astg{i}", [128, N_IN], BF16) for i in range(2)]
            g.stg_i = 0
        pstate = {"n": 0}

        def pump():
            pstate["n"] += 1
            if pstate["n"] % PUMP_EVERY != 0:
                return
            while bg:
                try:
                    next(bg[0])
                    return
                except StopIteration:
                    bg.pop(0)
        dl = fw.sbuf(st, "dl", [128, 256], F32)
        pr = fw.sbuf(st, "dlp", [128, 2, 64], F32)
        sm = fw.sbuf(st, "dls", [128, 4], F32)
        nlam = fw.sbuf(st, "nlam", [128, 1], F32)
        subln = fw.sbuf(st, "subln", [128, 1], F32)
        load_bcast(fw, "sp", dl, g.inp["da_lambda"][l:l + 1, :])
        tt(fw, "dve", pr[:, 0, :], dl[:, 0:64], dl[:, 64:128], ALU.mult, [dl], [pr])
        tt(fw, "dve", pr[:, 1, :], dl[:, 128:192], dl[:, 192:256], ALU.mult, [dl, pr], [pr])
        fw.op("dve", lambda h: h.tensor_reduce(out=sm[:, 0:2], in_=pr[:], axis=AX.X, op=ALU.add), [pr], [sm])
        act(fw, sm[:, 2:4], sm[:, 0:2], AF.Exp, [sm], [sm])
        tt(fw, "dve", nlam[:], sm[:, 3:4], sm[:, 2:3], ALU.subtract, [sm], [nlam])
        ts(fw, "dve", nlam[:], nlam[:], -lam_init, None, ALU.add, None, [nlam], [nlam])
        with fw.nc.allow_non_contiguous_dma(reason="tiny"):
            fw.dma("sp", subln[:], g.inp["da_subln"][l, :].rearrange("(p o) -> p o", o=1), reads=[], writes=[subln])
        ts(fw, "dve", subln[:], subln[:], 1.0 - lam_init, None, ALU.mult, None, [subln], [subln])
        onesb = fw.sbuf(st, "onesb", [128, 128], BF16)
        onesf = fw.sbuf(st, "onesf", [128, 128], F32)
        fw.op("dve", lambda h: h.memset(onesb[:], 1.0), [], [onesb])
        fw.op("dve", lambda h: h.memset(onesf[:], 1.0), [], [onesf])

        KT = [[fw.sbuf(st, f"KT{c}_{i}", [128, T], BF16) for i in range(2)] for c in range(2)]
        for c in range(2):
            for i in range(2):
                fw.op("pool", lambda h, b=KT[c][i]: h.memset(b[:], 0.0), [], [KT[c][i]])
        QT = [fw.sbuf(st, f"QT{i}", [128, T], BF16) for i in range(2)]
        VV = [fw.sbuf(st, f"VV{i}", [128, NT, 128], BF16) for i in range(2)]
        pbs = [fw.sbuf(st, f"pb{i}", [128, 512], BF16) for i in range(4)]
        tcs = [fw.sbuf(st, f"tc{i}", [128, 512], F32) for i in range(2)]
        rr = fw.sbuf(st, "rr", [128, 512], F32)
        oo = fw.sbuf(st, "oo", [128, 512], F32)
        sq = fw.sbuf(st, "sq", [128, 512], F32)
        rs = fw.sbuf(st, "rs", [128, 512], F32)
        ys = [fw.sbuf(st, f"ys{i}", [128, 512], BF16) for i in range(2)]
        ps_s = [fw.psum(st, f"ps_s{i}", [128, 512], F32) for i in range(3)]
        ps_o = [fw.psum(st, f"ps_o{i}", [128, 512], F32) for i in range(2)]
        ps_l = [fw.psum(st, f"ps_l{i}", [128, 512], F32) for i in range(2)]
        ps_q = fw.psum(st, "ps_q", [128, 512], F32)
        accs = [fw.sbuf(st, f"aacc{i}", [128, 512], F32) for i in range(3)]
        yi = 0
        for hd in range(8):
            ktc = [KT[0][hd % 2], KT[1][hd % 2]]
            qt_ = QT[hd % 2]
            vv = VV[hd % 2]
            for c in range(2):
                fw.dma("sp", ktc[c][c * 64:(c + 1) * 64, :], g.kTr[hd * 128 + c * 64:hd * 128 + (c + 1) * 64, :],
                       reads=[g.kTr], writes=[ktc[c]])
            fw.dma("sp", qt_[:], g.qTr[hd * 128:(hd + 1) * 128, :], reads=[g.qTr], writes=[qt_])
            fw.dma("sp", vv[:], g.v_tok[:, hd * 128:(hd + 1) * 128].rearrange("(k p) e -> p k e", p=128),
                   reads=[g.v_tok], writes=[vv])
            its = []
            for (q0, n) in tblocks(l):
                nkt = 2 if q0 == 0 else NT
                for c in range(2):
                    for kt in range(nkt):
                        its.append((q0, n, c, kt, nkt))

            def issue_S(i):
                q0, n, c, kt, nkt = its[i]
                p_s = ps_s[i % 3]
                mm(fw, p_s[:, 0:n], ktc[c][:, kt * 128:(kt + 1) * 128], qt_[:, q0:q0 + n], True, True, [ktc[c], qt_], [p_s])

            LOOK = 2
            for i in range(min(LOOK, len(its))):
                issue_S(i)
            for i, (q0, n, c, kt, nkt) in enumerate(its):
                if i + LOOK < len(its):
                    issue_S(i + LOOK)
                p_s = ps_s[i % 3]
                pb = pbs[i % 4]
                po = ps_o[c]
                act(fw, pb[:, 0:n], p_s[:, 0:n], AF.Exp, [p_s], [pb], scale=0.125)
                mm(fw, po[:, 0:n], vv[:, kt, :], pb[:, 0:n], kt == 0, kt == nkt - 1, [vv, pb], [po])
                pl = ps_l[c]
                role = ("pe", "dve", "pe", "pool", "pe")[kt % 5]
                if role == "pe":
                    mm(fw, pl[:, 0:n], onesb[:], pb[:, 0:n], kt == 0, False, [onesb, pb], [pl])
                else:
                    a_ = accs[0] if role == "dve" else accs[1]
                    first = (kt == 1) if role == "dve" else (kt == 3)
                    if first:
                        copy(fw, role, a_[:, 0:n], pb[:, 0:n], [pb], [a_])
                    else:
                        tt(fw, role, a_[:, 0:n], a_[:, 0:n], pb[:, 0:n], ALU.add, [a_, pb], [a_])
                pump()
                if kt == nkt - 1:
                    if nkt > 3:
                        tt(fw, "dve", accs[0][:, 0:n], accs[0][:, 0:n], accs[1][:, 0:n], ALU.add, [accs[0], accs[1]], [accs[0]])
                    mm(fw, pl[:, 0:n], onesf[:], accs[0][:, 0:n], False, True, [onesf, accs[0]], [pl])
                    fw.op("dve", lambda h: h.reciprocal(out=rr[:, 0:n], in_=pl[:, 0:n]), [pl], [rr])
                    tt(fw, "dve", tcs[c][:, 0:n], po[:, 0:n], rr[:, 0:n], ALU.mult, [po, rr], [tcs[c]])
                    if c == 1:
                        stt(fw, oo[:, 0:n], tcs[1][:, 0:n], nlam[:, 0:1], tcs[0][:, 0:n], ALU.mult, ALU.add,
                            [tcs[0], tcs[1], nlam], [oo])
                        act(fw, sq[:, 0:n], oo[:, 0:n], AF.Square, [oo], [sq])
                        mm(fw, ps_q[:, 0:n], onesf[:], sq[:, 0:n], True, True, [onesf, sq], [ps_q])
                        act(fw, rs[:, 0:n], ps_q[:, 0:n], AF.Sqrt, [ps_q], [rs], scale=1.0 / 128, bias=g.eps_t[:, 0:1])
                        fw.op("dve", lambda h: h.reciprocal(out=rs[:, 0:n], in_=rs[:, 0:n]), [rs], [rs])
                        y = ys[yi % 2]
                        yi += 1
                        stt(fw, y[:, 0:n], oo[:, 0:n], subln[:, 0:1], rs[:, 0:n], ALU.mult, ALU.mult, [oo, subln, rs], [y])
                        fw.dma("pool", g.yaT[hd * 128:(hd + 1) * 128, q0:q0 + n], y[:, 0:n], reads=[y], writes=[g.yaT])
        for gen in bg:
            for _ in gen:
                pass
        fw.end_phase(keep=g.keep)


def post_tile(fw, g, o_ap, o_buf, GT, xt, hb, ss, tmp, dst_ap, dst_buf, q="pool"):
    act(fw, hb[:], o_ap, AF.Square, [o_buf], [hb, ss], accum_out=ss[:, 0:1])
    act(fw, ss[:, 1:2], ss[:, 0:1], AF.Sqrt, [ss], [ss], scale=1.0 / D, bias=g.eps_t[:, 0:1])
    fw.op("dve", lambda h: h.reciprocal(out=ss[:, 2:3], in_=ss[:, 1:2]), [ss], [ss])
    stt(fw, tmp[:], o_ap, ss[:, 2:3], GT[:], ALU.mult, ALU.mult, [o_buf, ss, GT], [tmp])
    tt(fw, "pool", tmp[:], tmp[:], xt[:], ALU.add, [tmp, xt], [tmp])
    fw.dma(q, dst_ap, tmp[:], reads=[tmp], writes=[dst_buf])


def load_gate_tile(fw, g, l, GT, gb, which, r):
    load_bcast(fw, "sp", GT, g.mod_d[l][r:r + 1, (3 * which + 2) * D:(3 * which + 3) * D], g.mod_d[l])
    load_bcast(fw, "sp", gb, g.inp["norm_mix_post" if which == 0 else "norm_ffn_post"][l:l + 1, :])
    tt(fw, "dve", GT[:], GT[:], gb[:], ALU.mult, [GT, gb], [GT])


def phase_merge(fw, g, l, W):
    last = l == DEPTH - 1
    with ExitStack() as st:
        GT = fw.sbuf(st, "mGT", [128, D], F32)
        gb = fw.sbuf(st, "mgb", [128, D], F32)
        hT = fw.sbuf(st, "mhT", [128, KD, 512], BF16)
        yT = [fw.sbuf(st, f"myT{i}", [128, 8, 512], BF16) for i in range(3)]
        MT = fw.sbuf(st, "mMT", [128, KD, 512], BF16)
        w16 = [fw.sbuf(st, f"mw16_{i}", [128, KD, 512], BF16) for i in range(4)]
        w8 = [fw.sbuf(st, f"mw8_{i}", [128, 8, 512], BF16) for i in range(4)]
        sg = [fw.sbuf(st, f"msg{i}", [128, 512], F32) for i in range(2)]
        pp = [fw.sbuf(st, f"mpp{i}", [128, 512], F32) for i in range(3)]
        osb = fw.sbuf(st, "mosb", [128, D], F32)
        xt = fw.sbuf(st, "mxt", [128, D], F32)
        tmp = fw.sbuf(st, "mtmp", [128, D], F32)
        hb = fw.sbuf(st, "mhb", [128, D], BF16)
        ss = fw.sbuf(st, "mss", [128, 4], F32)
        ps_g = [fw.psum(st, f"mps_g{i}", [128, 512], F32) for i in range(2)]
        ps_b = [fw.psum(st, f"mps_b{i}", [128, 512], F32) for i in range(2)]
        ps_o = [fw.psum(st, f"mps_o{i}", [128, 512], F32) for i in range(2)]
        ysrc = [g.yaT, g.ysT, g.y5T]
        i16 = 0
        i8 = 0
        ig = 0
        ib = 0
        io = 0
        cur_r = None
        for (t0, n) in tblocks(l):
            r = 1 if t0 == 0 else 0
            if r != cur_r:
                load_gate_tile(fw, g, l, GT, gb, 0, r)
                cur_r = r
            fw.dma("sp", hT[:, :, 0:n], g.hT_d[:, t0:t0 + n].rearrange("(k p) t -> p k t", p=128),
                   reads=[g.hT_d], writes=[hT])
            for b3 in range(3):
                fw.dma("sp", yT[b3][:, :, 0:n], ysrc[b3][:, t0:t0 + n].rearrange("(k p) t -> p k t", p=128),
                       reads=[ysrc[b3]], writes=[yT[b3]])
            for dq in range(4):
                wg_ = []
                wb_ = []
                for b3 in range(3):
                    w = w16[i16 % 4]
                    i16 += 1
                    fw.dma("sp", w[:], W.win[14 + b3 * 4 + dq], reads=[W.win], writes=[w])
                    wg_.append(w)
                    w2 = w8[i8 % 4]
                    i8 += 1
                    fw.dma("sp", w2[:], W.wbr[b3][dq], reads=[W.wbr[b3]], writes=[w2])
                    wb_.append(w2)
                for j in range(4):
                    dt_ = dq * 4 + j
                    for b3 in range(3):
                        pg = ps_g[ig % 2]
                        ig += 1
                        pb = ps_b[ib % 2]
                        ib += 1
                        s_ = sg[b3 % 2]
                        for k in range(KD):
                            mm(fw, pg[:, 0:n], wg_[b3][:, k, j * 128:(j + 1) * 128], hT[:, k, 0:n], k == 0, k == KD - 1,
                               [wg_[b3], hT], [pg])
                        act(fw, s_[:, 0:n], pg[:, 0:n], AF.Sigmoid, [pg], [s_])
                        for k in range(8):
                            mm(fw, pb[:, 0:n], wb_[b3][:, k, j * 128:(j + 1) * 128], yT[b3][:, k, 0:n], k == 0, k == 7,
                               [wb_[b3], yT[b3]], [pb])
                        tt(fw, "dve", pp[b3][:, 0:n], pb[:, 0:n], s_[:, 0:n], ALU.mult, [pb, s_], [pp[b3]])
                    tt(fw, "pool", pp[0][:, 0:n], pp[0][:, 0:n], pp[1][:, 0:n], ALU.add, [pp[0], pp[1]], [pp[0]])
                    tt(fw, "pool", MT[:, dt_, 0:n], pp[0][:, 0:n], pp[2][:, 0:n], ALU.add, [pp[0], pp[2]], [MT])
            wo = []
            for cb in range(4):
                w = w16[i16 % 4] if cb < 3 else hT
                if cb < 3:
                    i16 += 1
                fw.dma("sp", w[:], W.wout[cb], reads=[W.wout], writes=[w])
                wo.append(w)
            for s in range(n // 128):
                i = t0 // 128 + s
                fw.dma("sp", xt[:], g.xsrc(l, i), reads=[g.xres], writes=[xt])
                for cb in range(4):
                    po = ps_o[io % 2]
                    io += 1
                    for k in range(KD):
                        mm(fw, po[:, :], MT[:, k, s * 128:(s + 1) * 128], wo[cb][:, k, :], k == 0, k == KD - 1,
                           [MT, wo[cb]], [po])
                    copy(fw, "act", osb[:, cb * 512:(cb + 1) * 512], po[:, :], [po], [osb])
                post_tile(fw, g, osb[:], osb, GT, xt, hb, ss, tmp, g.xres[i * 128:(i + 1) * 128, :], g.xres)
        fw.end_phase(keep=g.keep)


def phase_ffn(fw, g, l, W):
    last = l == DEPTH - 1
    with ExitStack() as st:
        G = fw.sbuf(st, "fG", [128, D], F32)
        S = fw.sbuf(st, "fS", [128, D], F32)
        GT = fw.sbuf(st, "fGT", [128, D], F32)
        gb = fw.sbuf(st, "fgb", [128, D], F32)
        h2T = fw.sbuf(st, "fh2T", [128, KD, 512], BF16)
        AT = fw.sbuf(st, "fAT", [128, 44, 512], BF16)
        w16 = [fw.sbuf(st, f"fw16_{i}", [128, KD, 512], BF16) for i in range(3)]
        wdp = [fw.sbuf(st, f"fwd{i}", [128, 11, 512], BF16) for i in range(2)]
        osb = [fw.sbuf(st, f"fosb{i}", [128, D], F32) for i in range(2)]
        xt = fw.sbuf(st, "fxt", [128, D], F32)
        tmp = fw.sbuf(st, "ftmp", [128, D], F32)
        hb = fw.sbuf(st, "fhb", [128, D], BF16)
        ss = fw.sbuf(st, "fss", [128, 4], F32)
        sg = [fw.sbuf(st, f"fsg{i}", [128, 512], F32) for i in range(2)]
        ps_g = [fw.psum(st, f"fps_g{i}", [128, 512], F32) for i in range(2)]
        ps_u = [fw.psum(st, f"fps_u{i}", [128, 512], F32) for i in range(2)]
        ps_d = [fw.psum(st, f"fps_d{i}", [128, 512], F32) for i in range(2)]
        ps2 = [fw.psum(st, f"ftr{i}", [128, 1024], BF16) for i in range(2)]
        gain = g.inp["norm_ffn_pre"]
        iw = 0
        ig = 0
        iwd = 0
        cur_r = None
        for (t0, n) in tblocks(l):
            r = 1 if t0 == 0 else 0
            if r != cur_r:
                load_bcast(fw, "sp", gb, gain[l:l + 1, :])
                load_bcast(fw, "sp", G, g.mod_d[l][r:r + 1, 4 * D:5 * D], g.mod_d[l])
                load_bcast(fw, "sp", S, g.mod_d[l][r:r + 1, 3 * D:4 * D], g.mod_d[l])
                stt(fw, G[:], G[:], 1.0, gb[:], ALU.add, ALU.mult, [G, gb], [G])
                load_gate_tile(fw, g, l, GT, gb, 1, r)
                cur_r = r
            ns = n // 128
            for s in range(ns):
                i = t0 // 128 + s
                fw.dma("sp", xt[:], g.xres[i * 128:(i + 1) * 128, :], reads=[g.xres], writes=[xt])
                norm_mod_tile(fw, g, xt, G, S, hb, ss, tmp, hb)
                transpose_tile(fw, g, hb, ps2, h2T, s * 128)
            for fb in range(11):
                wg_ = w16[iw % 3]
                iw += 1
                fw.dma("sp", wg_[:], W.wg[fb], reads=[W.wg], writes=[wg_])
                wu_ = w16[iw % 3]
                iw += 1
                fw.dma("sp", wu_[:], W.wu[fb], reads=[W.wu], writes=[wu_])
                for j in range(4):
                    pg = ps_g[ig % 2]
                    pu = ps_u[ig % 2]
                    s_ = sg[ig % 2]
                    ig += 1
                    for k in range(KD):
                        mm(fw, pg[:, 0:n], wg_[:, k, j * 128:(j + 1) * 128], h2T[:, k, 0:n], k == 0, k == KD - 1,
                           [wg_, h2T], [pg])
                    act(fw, s_[:, 0:n], pg[:, 0:n], AF.Silu, [pg], [s_])
                    for k in range(KD):
                        mm(fw, pu[:, 0:n], wu_[:, k, j * 128:(j + 1) * 128], h2T[:, k, 0:n], k == 0, k == KD - 1,
                           [wu_, h2T], [pu])
                    tt(fw, "dve", AT[:, fb * 4 + j, 0:n], pu[:, 0:n], s_[:, 0:n], ALU.mult, [pu, s_], [AT])
            for pr_ in range((ns + 1) // 2):
                subs = [s for s in (2 * pr_, 2 * pr_ + 1) if s < ns]
                for cb in range(4):
                    for pc in range(4):
                        wd_ = wdp[iwd % 2]
                        iwd += 1
                        fw.dma("sp", wd_[:], W.wd[cb][:, pc * 11:(pc + 1) * 11, :], reads=[W.wd], writes=[wd_])
                        for si_, s in enumerate(subs):
                            for kk in range(11):
                                k = pc * 11 + kk
                                mm(fw, ps_d[si_][:, :], AT[:, k, s * 128:(s + 1) * 128], wd_[:, kk, :], k == 0, k == 43,
                                   [AT, wd_], [ps_d[si_]])
                    for si_, s in enumerate(subs):
                        copy(fw, "act", osb[si_][:, cb * 512:(cb + 1) * 512], ps_d[si_][:, :], [ps_d[si_]], [osb[si_]])
                for si_, s in enumerate(subs):
                    i = t0 // 128 + s
                    fw.dma("sp", xt[:], g.xres[i * 128:(i + 1) * 128, :], reads=[g.xres], writes=[xt])
                    if last:
                        dst_ap, dst_buf = g.out[(i - 2) * 128:(i - 1) * 128, :], g.out
                    else:
                        dst_ap, dst_buf = g.xres[i * 128:(i + 1) * 128, :], g.xres
                    post_tile(fw, g, osb[si_][:], osb[si_], GT, xt, hb, ss, tmp, dst_ap, dst_buf)
        fw.end_phase(keep=g.keep)


NPAD = T + 8
NCV = NPAD - 4


def phase_ssd_prep(fw, g, l):
    with ExitStack() as st:
        cw = fw.sbuf(st, "cw", [128, 12, 5], F32)
        cb = fw.sbuf(st, "cb", [128, 12], F32)
        with fw.nc.allow_non_contiguous_dma(reason="tiny"):
            for k in range(5):
                fw.dma("sp", cw[:, :, k], g.inp["ssd_conv_w"][l, k, :].rearrange("(c p) -> p c", p=128), reads=[], writes=[cw])
            fw.dma("sp", cb[:], g.inp["ssd_conv_b"][l, :].rearrange("(c p) -> p c", p=128), reads=[], writes=[cb])
        xin = [fw.sbuf(st, f"cxin{i}", [128, NPAD], BF16) for i in range(2)]
        for b in xin:
            fw.op("pool", lambda h, b=b: h.memset(b[:], 0.0), [], [b])
        acc = fw.sbuf(st, "cacc", [128, NCV], F32)
        xo = [fw.sbuf(st, f"cxo{i}", [128, NCV], BF16) for i in range(2)]
        ps2 = [fw.psum(st, f"ctr{i}", [128, 1024], BF16) for i in range(2)]
        tst = [fw.sbuf(st, f"ctst{i}", [128, 8, 128], BF16) for i in range(2)]
        ti = 0
        for ct in range(12):
            xi = xin[ct % 2]
            o = xo[ct % 2]
            fw.dma("sp", xi[:, 2:2 + CTX], g.xbcT[ct * 128:(ct + 1) * 128, 0:CTX], reads=[g.xbcT], writes=[xi])
            fw.dma("sp", xi[:, 262:262 + LAT], g.xbcT[ct * 128:(ct + 1) * 128, CTX:T], reads=[g.xbcT], writes=[xi])
            ts(fw, "dve", acc[:], xi[:, 0:NCV], cw[:, ct, 0:1], cb[:, ct:ct + 1], ALU.mult, ALU.add, [xi, cw, cb], [acc])
            for k in range(1, 5):
                stt(fw, acc[:], xi[:, k:k + NCV], cw[:, ct, k:k + 1], acc[:], ALU.mult, ALU.add, [xi, cw, acc], [acc])
            act(fw, o[:], acc[:], AF.Silu, [acc], [o])
            if ct >= 8:
                dst = g.BT if ct < 10 else g.CT
                r0 = (ct - 8) % 2 * 128
                fw.dma("pool", dst[r0:r0 + 128, 0:CTX], o[:, 0:CTX], reads=[o], writes=[dst])
                fw.dma("pool", dst[r0:r0 + 128, CTX:T], o[:, 260:260 + LAT], reads=[o], writes=[dst])
            if ct < 10:
                dst = g.xs_tok if ct < 8 else g.B_tok
                c0 = ct * 128 if ct < 8 else (ct - 8) * 128
                for grp in range((NT + 7) // 8):
                    i0 = grp * 8
                    ni = min(8, NT - i0)
                    p = ps2[ti % 2]
                    sg = tst[ti % 2]
                    ti += 1
                    for j in range(ni):
                        i = i0 + j
                        col = i * 128 if i < 2 else 260 + (i - 2) * 128
                        fw.op("pe", lambda h, j=j, col=col: h.transpose(p[:, j * 128:(j + 1) * 128], o[:, col:col + 128],
                                                                        g.ident[:]), [o, g.ident], [p])
                    copy(fw, "act", sg[:, 0:ni, :], p[:, 0:ni * 128].rearrange("p (i c) -> p i c", c=128), [p], [sg])
                    fw.dma("pool", dst[i0 * 128:(i0 + ni) * 128, c0:c0 + 128].rearrange("(i p) c -> p i c", p=128),
                           sg[:, 0:ni, :], reads=[sg], writes=[dst])
        fw.end_phase(keep=g.keep)
    for d in range(2):
        with ExitStack() as st:
            ones = fw.sbuf(st, "dones", [16, 128], F32)
            fw.op("dve", lambda h: h.memset(ones[:], 1.0), [], [ones])
            raw = fw.sbuf(st, f"draw{d}", [16, T], F32)
            dt = fw.sbuf(st, f"ddt{d}", [16, T], F32)
            dtA = fw.sbuf(st, f"ddtA{d}", [16, T], F32)
            AC = fw.sbuf(st, f"dAC{d}", [16, T], F32)
            EX = fw.sbuf(st, f"dEX{d}", [16, T], F32)
            q1 = fw.sbuf(st, f"dq1{d}", [16, T], F32)
            q2 = fw.sbuf(st, f"dq2{d}", [16, T], F32)
            last = fw.sbuf(st, f"dlast{d}", [16, NT], F32)
            par = fw.sbuf(st, f"dpar{d}", [16, 4], F32)
            fw.dma("sp", raw[:], g.dtT[d * 16:(d + 1) * 16, :], reads=[g.dtT], writes=[raw])
            with fw.nc.allow_non_contiguous_dma(reason="tiny"):
                fw.dma("sp", par[:, 0:1], g.inp["ssd_dt_bias"][l, d * 16:(d + 1) * 16].rearrange("(p o) -> p o", o=1),
                       reads=[], writes=[par])
                fw.dma("sp", par[:, 1:2], g.inp["ssd_a_log"][l, d * 16:(d + 1) * 16].rearrange("(p o) -> p o", o=1),
                       reads=[], writes=[par])
            act(fw, par[:, 2:3], par[:, 1:2], AF.Exp, [par], [par])
            ts(fw, "dve", par[:, 2:3], par[:, 2:3], -1.0, None, ALU.mult, None, [par], [par])
            act(fw, dt[:], raw[:], AF.Exp, [raw, par], [dt], bias=par[:, 0:1])
            act(fw, dt[:], dt[:], AF.Ln, [dt], [dt], bias=g.one_t[0:16, 0:1])
            ts(fw, "dve", dtA[:], dt[:], par[:, 2:3], None, ALU.mult, None, [dt, par], [dtA])
            for c in range(NT):
                fw.op("dve", lambda h, c=c: h.tensor_tensor_scan(out=AC[:, c * 128:(c + 1) * 128], data0=ones[:],
                                                                 data1=dtA[:, c * 128:(c + 1) * 128], initial=0.0,
                                                                 op0=ALU.mult, op1=ALU.add), [ones, dtA], [AC])
            tt(fw, "dve", EX[:], AC[:], dtA[:], ALU.subtract, [AC, dtA], [EX])
            copy(fw, "dve", last[:], AC[:].rearrange("p (c t) -> p c t", t=128)[:, :, 127], [AC], [last])
            lb = last[:].unsqueeze(2).to_broadcast([16, NT, 128])
            v3 = lambda b: b[:].rearrange("p (c t) -> p c t", t=128)
            qd = g.ssd_q
            fw.dma("pool", qd[d, 0], dt[:], reads=[dt], writes=[qd])
            if d == 0:
                tt(fw, "dve", v3(q1), lb, v3(AC), ALU.subtract, [last, AC], [q1])
                act(fw, q1[:], q1[:], AF.Exp, [q1], [q1])
                tt(fw, "dve", q1[:], q1[:], dt[:], ALU.mult, [q1, dt], [q1])
                fw.dma("pool", qd[d, 1], q1[:], reads=[q1], writes=[qd])
                act(fw, q2[:], AC[:], AF.Exp, [AC], [q2])
                fw.dma("pool", qd[d, 2], q2[:], reads=[q2], writes=[qd])
                ts(fw, "dve", dtA[:], AC[:], -1.0, None, ALU.mult, None, [AC], [dtA])
                fw.dma("pool", qd[d, 3], dtA[:], reads=[dtA], writes=[qd])
                fw.dma("pool", qd[d, 4], AC[:], reads=[AC], writes=[qd])
            else:
                act(fw, q1[:], EX[:], AF.Exp, [EX], [q1])
                tt(fw, "dve", q1[:], q1[:], dt[:], ALU.mult, [q1, dt], [q1])
                fw.dma("pool", qd[d, 1], q1[:], reads=[q1], writes=[qd])
                tt(fw, "dve", v3(q2), lb, v3(EX), ALU.subtract, [last, EX], [q2])
                act(fw, q2[:], q2[:], AF.Exp, [q2], [q2])
                fw.dma("pool", qd[d, 2], q2[:], reads=[q2], writes=[qd])
                fw.dma("pool", qd[d, 3], EX[:], reads=[EX], writes=[qd])
                ts(fw, "dve", dtA[:], EX[:], -1.0, None, ALU.mult, None, [EX], [dtA])
                fw.dma("pool", qd[d, 4], dtA[:], reads=[dtA], writes=[qd])
            act(fw, last[:], last[:], AF.Exp, [last], [last])
            fw.dma("pool", g.ssd_cd[d], last[:], reads=[last], writes=[g.ssd_cd])
            fw.end_phase(keep=g.keep)


def phase_ssd_scan(fw, g, l):
    with ExitStack() as st:
        identf = fw.sbuf(st, "identf", [128, 128], F32)
        copy(fw, "dve", identf[:], g.ident[:], [g.ident], [identf])
        masks = [fw.sbuf(st, f"smask{d}", [128, 128], F32) for d in range(2)]
        fw.dma("sp", masks[0][:], g.cin["c_maskf"], reads=[], writes=[masks[0]])
        fw.dma("sp", masks[1][:], g.cin["c_maskb"], reads=[], writes=[masks[1]])
        Dbc = fw.sbuf(st, "sDbc", [128, 16], F32)
        load_bcast(fw, "sp", Dbc, g.inp["ssd_d"][l:l + 1, :])
        nrm = fw.sbuf(st, "snrm", [128, 1024], F32)
        load_bcast(fw, "sp", nrm, g.inp["ssd_norm"][l:l + 1, :])
        cdb = fw.sbuf(st, "scdb", [128, 16, NT], F32)
        SD = fw.sbuf(st, "sSD", [128, T], F32)
        H = fw.sbuf(st, "sH", [128, 16, 64], F32)
        Hb = fw.sbuf(st, "sHb", [128, 1024], BF16)
        xs = [fw.sbuf(st, f"sxs{i}", [128, 16, 64], BF16) for i in range(2)]
        Bt = [fw.sbuf(st, f"sBt{i}", [128, 256], BF16) for i in range(2)]
        BT = [fw.sbuf(st, f"sBT{i}", [128, 2, 128], BF16) for i in range(2)]
        CT = [fw.sbuf(st, f"sCT{i}", [128, 2, 128], BF16) for i in range(2)]
        bc = [fw.sbuf(st, f"sbc{i}", [128, 16, 128], F32) for i in range(2)]
        zt = [fw.sbuf(st, f"szt{i}", [128, 1024], BF16) for i in range(2)]
        yf = [fw.sbuf(st, f"syf{i}", [128, 1024], F32) for i in range(2)]
        tm = fw.sbuf(st, "stm", [128, 128], F32)
        cbs = fw.sbuf(st, "scb", [128, 2, 128], F32)
        xdt = fw.sbuf(st, "sxdt", [128, 16, 64], BF16)
        xdo = fw.sbuf(st, "sxdo", [128, 16, 64], BF16)
        seg = [fw.sbuf(st, f"sseg{i}", [128, 128], F32) for i in range(2)]
        dec = [fw.sbuf(st, f"sdec{i}", [128, 128], F32) for i in range(2)]
        MT = [fw.sbuf(st, f"sMT{i}", [128, 128], BF16) for i in range(3)]
        yo = fw.sbuf(st, "syo", [128, 16, 64], F32)
        ysum = fw.sbuf(st, "sysum", [128, 16, 64], F32)
        t2 = fw.sbuf(st, "st2", [128, 16, 64], F32)
        sz = fw.sbuf(st, "ssz", [128, 1024], F32)
        hb = fw.sbuf(st, "shb", [128, 1024], BF16)
        ss = fw.sbuf(st, "sss", [128, 4], F32)
        yst = [fw.sbuf(st, f"syst{i}", [128, 8, 128], BF16) for i in range(2)]
        ps_t = fw.psum(st, "sps_t", [128, 3, 128], F32)
        ps_y = fw.psum(st, "sps_y", [128, 1024], F32)
        ps_yo = fw.psum(st, "sps_yo", [128, 1024], F32)
        ps_sn = fw.psum(st, "sps_sn", [128, 1024], F32)
        ps2 = [fw.psum(st, "sps_tr", [128, 1024], BF16)]
        it = 0
        for d in range(2):
            order = list(range(NT)) if d == 0 else [1, 0] + list(range(NT - 1, 1, -1))
            fw.dma("sp", SD[0:64, :], g.ssd_q[d, 0:4].rearrange("q h t -> (q h) t"), reads=[g.ssd_q], writes=[SD])
            fw.dma("sp", cdb[:], g.ssd_cd[d:d + 1].partition_broadcast(128), reads=[g.ssd_cd], writes=[cdb])
            fw.op("dve", lambda h: h.memset(H[:], 0.0), [], [H])
            fw.op("dve", lambda h: h.memset(Hb[:], 0.0), [], [Hb])
            for c in order:
                b = it % 2
                it += 1
                tk = slice(c * 128, (c + 1) * 128)
                fw.dma("sp", xs[b][:], g.xs_tok[tk, :].rearrange("p (h e) -> p h e", e=64), reads=[g.xs_tok], writes=[xs[b]])
                fw.dma("sp", Bt[b][:], g.B_tok[tk, :], reads=[g.B_tok], writes=[Bt[b]])
                fw.dma("sp", BT[b][:], g.BT[:, tk].rearrange("(g n) t -> n g t", n=128), reads=[g.BT], writes=[BT[b]])
                fw.dma("sp", CT[b][:], g.CT[:, tk].rearrange("(g n) t -> n g t", n=128), reads=[g.CT], writes=[CT[b]])
                fw.dma("sp", bc[b][:], g.ssd_q[d, 4:5, :, tk].partition_broadcast(128), reads=[g.ssd_q], writes=[bc[b]])
                if d == 1:
                    fw.dma("sp", zt[b][:], g.z_tok[tk, :], reads=[g.z_tok], writes=[zt[b]])
                    fw.dma("sp", yf[b][:], g.yf_d[tk, :], reads=[g.yf_d], writes=[yf[b]])
                fw.op("pe", lambda h: h.transpose(ps_t[:, 0, :], SD[:, tk], identf[:]), [SD, identf], [ps_t])
                for gg in range(2):
                    mm(fw, ps_t[:, 1 + gg, :], BT[b][:, gg, :], CT[b][:, gg, :], True, True, [BT[b], CT[b]], [ps_t])
                copy(fw, "act", tm[:], ps_t[:, 0, :], [ps_t], [tm])
                copy(fw, "act", cbs[:], ps_t[:, 1:3, :], [ps_t], [cbs])
                tt(fw, "pool", xdt[:], xs[b][:], tm[:, 0:16].unsqueeze(2).to_broadcast([128, 16, 64]), ALU.mult,
                   [xs[b], tm], [xdt])
                tt(fw, "pool", xdo[:], xs[b][:], tm[:, 16:32].unsqueeze(2).to_broadcast([128, 16, 64]), ALU.mult,
                   [xs[b], tm], [xdo])
                for hh in range(16):
                    sgb = seg[hh % 2]
                    dcb = dec[hh % 2]
                    mt = MT[hh % 3]
                    stt(fw, sgb[:], bc[b][:, hh, :], tm[:, 48 + hh:49 + hh], masks[d][:], ALU.add, ALU.add,
                        [bc[b], tm, masks[d]], [sgb])
                    act(fw, dcb[:], sgb[:], AF.Exp, [sgb], [dcb])
                    tt(fw, "pool", mt[:], dcb[:], cbs[:, hh // 8, :], ALU.mult, [dcb, cbs], [mt])
                    mm(fw, ps_y[:, hh * 64:(hh + 1) * 64], mt[:], xdt[:, hh, :], True, True, [mt, xdt], [ps_y])
                for gg in range(2):
                    mm(fw, ps_yo[:, gg * 512:(gg + 1) * 512], CT[b][:, gg, :], Hb[:, gg * 512:(gg + 1) * 512], True, True,
                       [CT[b], Hb], [ps_yo])
                tt(fw, "dve", yo[:], ps_yo[:].rearrange("p (h e) -> p h e", e=64),
                   tm[:, 32:48].unsqueeze(2).to_broadcast([128, 16, 64]), ALU.mult, [ps_yo, tm], [yo])
                tt(fw, "dve", ysum[:], ps_y[:].rearrange("p (h e) -> p h e", e=64), yo[:], ALU.add, [ps_y, yo], [ysum])
                for gg in range(2):
                    mm(fw, ps_sn[:, gg * 512:(gg + 1) * 512], Bt[b][:, gg * 128:(gg + 1) * 128],
                       xdo[:, gg * 8:(gg + 1) * 8, :].rearrange("p h e -> p (h e)"), True, True, [Bt[b], xdo], [ps_sn])
                tt(fw, "dve", H[:], H[:], cdb[:, :, c:c + 1].to_broadcast([128, 16, 64]), ALU.mult, [H, cdb], [H])
                tt(fw, "dve", H[:], H[:], ps_sn[:].rearrange("p (h e) -> p h e", e=64), ALU.add, [H, ps_sn], [H])
                copy(fw, "act", Hb[:], H[:].rearrange("p h e -> p (h e)"), [H], [Hb])
                if d == 0:
                    fw.dma("pool", g.yf_d[tk, :], ysum[:].rearrange("p h e -> p (h e)"), reads=[ysum], writes=[g.yf_d])
                else:
                    if l == DEPTH - 1 and c < 2:
                        continue
                    ysf = ysum[:].rearrange("p h e -> p (h e)")
                    tt(fw, "pool", ysf, ysf, yf[b][:], ALU.add, [ysum, yf[b]], [ysum])
                    tt(fw, "pool", t2[:], xs[b][:], Dbc[:, :].unsqueeze(2).to_broadcast([128, 16, 64]), ALU.mult,
                       [xs[b], Dbc], [t2])
                    tt(fw, "pool", ysum[:], ysum[:], t2[:], ALU.add, [ysum, t2], [ysum])
                    act(fw, sz[:], zt[b][:], AF.Silu, [zt[b]], [sz])
                    tt(fw, "dve", sz[:], sz[:], ysf, ALU.mult, [sz, ysum], [sz])
                    act(fw, hb[:], sz[:], AF.Square, [sz], [hb, ss], accum_out=ss[:, 0:1])
                    act(fw, ss[:, 1:2], ss[:, 0:1], AF.Sqrt, [ss], [ss], scale=1.0 / 1024, bias=g.eps_t[:, 0:1])
                    fw.op("dve", lambda h: h.reciprocal(out=ss[:, 2:3], in_=ss[:, 1:2]), [ss], [ss])
                    stt(fw, hb[:], sz[:], ss[:, 2:3], nrm[:], ALU.mult, ALU.mult, [sz, ss, nrm], [hb])
                    ys_ = yst[it % 2]
                    transpose_tile(fw, g, hb, ps2, ys_, 0, nk=8)
                    fw.dma("pool", g.ysT[:, tk].rearrange("(k p) t -> p k t", p=128), ys_[:], reads=[ys_], writes=[g.ysT])
        fw.end_phase(keep=g.keep)


NSC = T // 32
NCC = CTX // 32


def cmul(fw, o_re, o_im, ar, ai, br, bi, t1, t2, bufs_in, bufs_out, tb1, tb2):
    tt(fw, "dve", t1, ar, br, ALU.mult, bufs_in, [tb1])
    tt(fw, "pool", t2, ai, bi, ALU.mult, bufs_in, [tb2])
    tt(fw, "dve", o_re, t1, t2, ALU.subtract, [tb1, tb2], bufs_out)
    tt(fw, "dve", t1, ar, bi, ALU.mult, bufs_in + bufs_out, [tb1])
    tt(fw, "pool", t2, ai, br, ALU.mult, bufs_in + bufs_out, [tb2])
    tt(fw, "dve", o_im, t1, t2, ALU.add, [tb1, tb2], bufs_out)


def phase_s5_params(fw, g, l):
    P = 64
    with ExitStack() as st:
        identf = fw.sbuf(st, "p5identf", [128, 128], F32)
        copy(fw, "dve", identf[:], g.ident[:], [g.ident], [identf])
        halfpi = fw.sbuf(st, "p5hpi", [P, 1], F32)
        fw.op("dve", lambda h: h.memset(halfpi[:], math.pi / 2), [], [halfpi])
        lraw = fw.sbuf(st, "p5lraw", [128, 2, P], F32)
        fw.dma("sp", lraw[:, 0, :], g.inp["s5_lam_re"][l].rearrange("d g p -> (d g) p"), reads=[], writes=[lraw])
        fw.dma("sp", lraw[:, 1, :], g.inp["s5_lam_im"][l].rearrange("d g p -> (d g) p"), reads=[], writes=[lraw])
        pst = fw.psum(st, "p5pst", [P, 1024], F32)
        LR = fw.sbuf(st, "p5LR", [P, 128], F32)
        LI = fw.sbuf(st, "p5LI", [P, 128], F32)
        for i, dst in enumerate((LR, LI)):
            fw.op("pe", lambda h, i=i: h.transpose(pst[:, i * 128:(i + 1) * 128], lraw[:, i, :], identf[:]),
                  [lraw, identf], [pst])
            copy(fw, "act", dst[:], pst[:, i * 128:(i + 1) * 128], [pst], [dst])
        dl = fw.sbuf(st, "p5dl", [P, 128], F32)
        fw.dma("sp", dl[:], g.inp["s5_log_step"][l:l + 1, :].partition_broadcast(P), reads=[], writes=[dl])
        act(fw, dl[:], dl[:], AF.Exp, [dl], [dl])
        Bre = fw.sbuf(st, "p5Bre", [P, 64, 16], F32)
        Bim = fw.sbuf(st, "p5Bim", [P, 64, 16], F32)
        fw.dma("sp", Bre[:], g.inp["s5_b_re"][l].rearrange("g p e -> p g e"), reads=[], writes=[Bre])
        fw.dma("sp", Bim[:], g.inp["s5_b_im"][l].rearrange("g p e -> p g e"), reads=[], writes=[Bim])
        Cre = fw.sbuf(st, "p5Cre", [P, 64, 16], F32)
        Cim = fw.sbuf(st, "p5Cim", [P, 64, 16], F32)
        craw = fw.sbuf(st, "p5craw", [128, 8, P], F32)
        for nm, dst in (("s5_c_re", Cre), ("s5_c_im", Cim)):
            fw.dma("sp", craw[:], g.inp[nm][l].rearrange("(a b) e p -> (b e) a p", b=8), reads=[], writes=[craw])
            for a in range(8):
                fw.op("pe", lambda h, a=a: h.transpose(pst[:, a * 128:(a + 1) * 128], craw[:, a, :], identf[:]),
                      [craw, identf], [pst])
            copy(fw, "act", dst[:].rearrange("p g e -> p (g e)"), pst[:, :], [pst], [dst])
        th = fw.sbuf(st, "p5th", [P, 128], F32)
        rho = fw.sbuf(st, "p5rho", [P, 128], F32)
        tt(fw, "dve", th[:], LI[:], dl[:], ALU.mult, [LI, dl], [th])
        tt(fw, "dve", rho[:], LR[:], dl[:], ALU.mult, [LR, dl], [rho])
        s1 = fw.sbuf(st, "p5s1", [P, 128], F32)
        c1 = fw.sbuf(st, "p5c1", [P, 128], F32)
        mg = fw.sbuf(st, "p5mg", [P, 128], F32)
        act(fw, s1[:], th[:], AF.Sin, [th], [s1], scale=1.0 / 16)
        act(fw, c1[:], th[:], AF.Sin, [th, halfpi], [c1], scale=1.0 / 16, bias=halfpi[:, 0:1])
        t1 = fw.sbuf(st, "p5t1", [P, 128], F32)
        t2 = fw.sbuf(st, "p5t2", [P, 128], F32)
        cur = {}
        for sgn in (1, -1):
            act(fw, mg[:], rho[:], AF.Exp, [rho], [mg], scale=sgn / 16.0)
            ar = fw.sbuf(st, f"p5ar{sgn}", [P, 128], F32)
            ai = fw.sbuf(st, f"p5ai{sgn}", [P, 128], F32)
            br = fw.sbuf(st, f"p5br{sgn}", [P, 128], F32)
            bi = fw.sbuf(st, f"p5bi{sgn}", [P, 128], F32)
            tt(fw, "dve", ar[:], mg[:], c1[:], ALU.mult, [mg, c1], [ar])
            tt(fw, "dve", ai[:], mg[:], s1[:], ALU.mult, [mg, s1], [ai])
            if sgn < 0:
                ts(fw, "dve", ai[:], ai[:], -1.0, None, ALU.mult, None, [ai], [ai])
            a_, b_ = (ar, ai), (br, bi)
            for _ in range(4):
                cmul(fw, b_[0][:], b_[1][:], a_[0][:], a_[1][:], a_[0][:], a_[1][:], t1[:], t2[:],
                     [a_[0], a_[1]], [b_[0], b_[1]], t1, t2)
                a_, b_ = b_, a_
            cur[sgn] = a_
        P1, PM1 = cur[1], cur[-1]
        ka = fw.sbuf(st, "p5ka", [P, 128], F32)
        den = fw.sbuf(st, "p5den", [P, 128], F32)
        kr = fw.sbuf(st, "p5kr", [P, 128], F32)
        ki = fw.sbuf(st, "p5ki", [P, 128], F32)
        ts(fw, "dve", ka[:], P1[0][:], -1.0, None, ALU.add, None, [P1[0]], [ka])
        tt(fw, "dve", den[:], LR[:], LR[:], ALU.mult, [LR], [den])
        tt(fw, "dve", t1[:], LI[:], LI[:], ALU.mult, [LI], [t1])
        tt(fw, "dve", den[:], den[:], t1[:], ALU.add, [den, t1], [den])
        fw.op("dve", lambda h: h.reciprocal(out=den[:], in_=den[:]), [den], [den])
        tt(fw, "dve", t1[:], ka[:], LR[:], ALU.mult, [ka, LR], [t1])
        tt(fw, "dve", t2[:], P1[1][:], LI[:], ALU.mult, [P1[1], LI], [t2])
        tt(fw, "dve", t1[:], t1[:], t2[:], ALU.add, [t1, t2], [t1])
        tt(fw, "dve", kr[:], t1[:], den[:], ALU.mult, [t1, den], [kr])
        tt(fw, "dve", t1[:], P1[1][:], LR[:], ALU.mult, [P1[1], LR], [t1])
        tt(fw, "dve", t2[:], ka[:], LI[:], ALU.mult, [ka, LI], [t2])
        tt(fw, "dve", t1[:], t1[:], t2[:], ALU.subtract, [t1, t2], [t1])
        tt(fw, "dve", ki[:], t1[:], den[:], ALU.mult, [t1, den], [ki])
        u1 = fw.sbuf(st, "p5u1", [P, 64, 16], F32)
        u2 = fw.sbuf(st, "p5u2", [P, 64, 16], F32)
        TPr = fw.sbuf(st, "p5TPr", [P, 64, 33], F32)
        TPi = fw.sbuf(st, "p5TPi", [P, 64, 33], F32)
        TNr = fw.sbuf(st, "p5TNr", [P, 64, 32], F32)
        TNi = fw.sbuf(st, "p5TNi", [P, 64, 32], F32)
        TRr = fw.sbuf(st, "p5TRr", [P, 64, 32], F32)
        TRi = fw.sbuf(st, "p5TRi", [P, 64, 32], F32)
        Bbr = fw.sbuf(st, "p5Bbr", [P, 64, 16], F32)
        Bbi = fw.sbuf(st, "p5Bbi", [P, 64, 16], F32)
        GBS = 2
        tmps = [[fw.sbuf(st, f"p5t{c}{i}", [P, GBS, 32, 16], F32) for c in "ABCD"] for i in range(2)]
        NK = 8
        outs = [[fw.sbuf(st, f"p5o{k}_{i}", [P, GBS, 512], BF16) for k in range(NK)] for i in range(2)]
        psd = [fw.psum(st, f"p5psd{i}", [128, 4, 128], F32) for i in range(2)]
        psm = fw.psum(st, "p5psm", [128, 2, 4, 64], BF16)
        Dsb = [fw.sbuf(st, f"p5Dsb{i}", [128, 10, 128], BF16) for i in range(2)]
        M2sb = [fw.sbuf(st, f"p5M2sb{i}", [128, 2, 4, 64], BF16) for i in range(2)]
        dmask = fw.sbuf(st, "p5dmask", [128, 2, 128], F32)
        fw.dma("sp", dmask[:, 0, :], g.cin["c_s5mf"], reads=[], writes=[dmask])
        fw.dma("sp", dmask[:, 1, :], g.cin["c_s5mb"], reads=[], writes=[dmask])
        Lst = fw.sbuf(st, "p5Lst", [P, 8, 2, 64], F32)
        L2 = fw.sbuf(st, "p5L2", [P, 2, 64], F32)
        bi_ = 0
        for d in range(2):
            ds = slice(d * 64, (d + 1) * 64)
            krb = kr[:, ds].unsqueeze(2).to_broadcast([P, 64, 16])
            kib = ki[:, ds].unsqueeze(2).to_broadcast([P, 64, 16])
            cmul(fw, Bbr[:], Bbi[:], krb, kib, Bre[:], Bim[:], u1[:], u2[:], [kr, ki, Bre, Bim], [Bbr, Bbi], u1, u2)
            fw.op("dve", lambda h: h.memset(TPr[:, :, 0], 1.0), [], [TPr])
            fw.op("dve", lambda h: h.memset(TPi[:, :, 0], 0.0), [], [TPi])
            fw.op("dve", lambda h: h.memset(TNr[:, :, 0], 1.0), [], [TNr])
            fw.op("dve", lambda h: h.memset(TNi[:, :, 0], 0.0), [], [TNi])
            for k in range(32):
                cmul(fw, TPr[:, :, k + 1], TPi[:, :, k + 1], TPr[:, :, k], TPi[:, :, k], P1[0][:, ds], P1[1][:, ds],
                     t1[:, 0:64], t2[:, 0:64], [TPr, TPi, P1[0], P1[1]], [TPr, TPi], t1, t2)
            for k in range(31):
                cmul(fw, TNr[:, :, k + 1], TNi[:, :, k + 1], TNr[:, :, k], TNi[:, :, k], PM1[0][:, ds], PM1[1][:, ds],
                     t1[:, 0:64], t2[:, 0:64], [TNr, TNi, PM1[0], PM1[1]], [TNr, TNi], t1, t2)
            top = 31 if d == 0 else 32
            for t_ in range(32):
                copy(fw, "dve", TRr[:, :, t_], TPr[:, :, top - t_], [TPr], [TRr])
                copy(fw, "pool", TRi[:, :, t_], TPi[:, :, top - t_], [TPi], [TRi])
            copy(fw, "dve", Lst[:, 0, 0, :], TPr[:, :, 32], [TPr], [Lst])
            copy(fw, "dve", Lst[:, 0, 1, :], TPi[:, :, 32], [TPi], [Lst])
            for k in range(7):
                cmul(fw, L2[:, 0, :], L2[:, 1, :], Lst[:, k, 0, :], Lst[:, k, 1, :], Lst[:, k, 0, :], Lst[:, k, 1, :],
                     t1[:, 0:64], t2[:, 0:64], [Lst], [L2], t1, t2)
                copy(fw, "dve", Lst[:, k + 1, :, :], L2[:], [L2], [Lst])
            fw.dma("pool", g.s5L[d].rearrange("k r p g -> p k r g"), Lst[:], reads=[Lst], writes=[g.s5L])
            if d == 0:
                specs = [((0, 1), (TNr, TNi, 0), (Bbr, Bbi), False), ((2, 3), (TPr, TPi, 0), (Cre, Cim), True),
                         ((4, 5), (TRr, TRi, 0), (Bbr, Bbi), False), ((6, 7), (TPr, TPi, 1), (Cre, Cim), True)]
            else:
                specs = [((0, 1), (TPr, TPi, 0), (Bbr, Bbi), False), ((2, 3), (TNr, TNi, 0), (Cre, Cim), True),
                         ((6, 7), (TRr, TRi, 0), (Cre, Cim), True)]
            si_ = 0
            for gbk in range(64 // GBS):
                g0 = gbk * GBS
                O = outs[bi_ % 2]
                Dt = Dsb[bi_ % 2]
                bi_ += 1
                for (kre, kim), (Tr_, Ti_, off), (Vr_, Vi_), neg in specs:
                    tA, tB, tC, tD = tmps[si_ % 2]
                    si_ += 1
                    tr = Tr_[:, g0:g0 + GBS, off:off + 32].unsqueeze(3).to_broadcast([P, GBS, 32, 16])
                    ti = Ti_[:, g0:g0 + GBS, off:off + 32].unsqueeze(3).to_broadcast([P, GBS, 32, 16])
                    vr = Vr_[:, g0:g0 + GBS, :].unsqueeze(2).to_broadcast([P, GBS, 32, 16])
                    vi = Vi_[:, g0:g0 + GBS, :].unsqueeze(2).to_broadcast([P, GBS, 32, 16])
                    ore = O[kre][:].rearrange("p g (t e) -> p g t e", e=16)
                    oim = O[kim][:].rearrange("p g (t e) -> p g t e", e=16)
                    rb = [Tr_, Ti_, Vr_, Vi_]
                    tt(fw, "dve", tA[:], tr, vr, ALU.mult, rb, [tA])
                    tt(fw, "pool", tB[:], ti, vi, ALU.mult, rb, [tB])
                    tt(fw, "dve", tC[:], tr, vi, ALU.mult, rb, [tC])
                    tt(fw, "pool", tD[:], ti, vr, ALU.mult, rb, [tD])
                    tt(fw, "dve", ore, tA[:], tB[:], ALU.subtract, [tA, tB], [O[kre]])
                    if neg:
                        stt(fw, oim, tC[:], -1.0, tD[:], ALU.mult, ALU.subtract, [tC, tD], [O[kim]])
                    else:
                        tt(fw, "dve", oim, tC[:], tD[:], ALU.add, [tC, tD], [O[kim]])
                fw.dma("pool", g.s5A[d, g0:g0 + GBS, :, 0, :].rearrange("g p x -> p g x"), O[6][:], reads=[O[6]], writes=[g.s5A])
                fw.dma("pool", g.s5A[d, g0:g0 + GBS, :, 1, :].rearrange("g p x -> p g x"), O[7][:], reads=[O[7]], writes=[g.s5A])
                m2r, m2i = (O[4], O[5]) if d == 0 else (O[0], O[1])
                for gi in range(GBS):
                    gg = g0 + gi
                    bidx = 0
                    for I in range(4):
                        Js = list(range(0, I + 1)) if d == 0 else list(range(I, 4))
                        pd = psd[I % 2]
                        for jn, J in enumerate(Js):
                            mm(fw, pd[:, jn, :], O[0][:, gi, J * 128:(J + 1) * 128], O[2][:, gi, I * 128:(I + 1) * 128],
                               True, False, [O[0], O[2]], [pd])
                            mm(fw, pd[:, jn, :], O[1][:, gi, J * 128:(J + 1) * 128], O[3][:, gi, I * 128:(I + 1) * 128],
                               False, True, [O[1], O[3]], [pd])
                        for jn, J in enumerate(Js):
                            if J == I:
                                tt(fw, "dve", Dt[:, bidx + jn, :], pd[:, jn, :], dmask[:, d, :], ALU.mult, [pd, dmask], [Dt])
                            else:
                                copy(fw, "dve", Dt[:, bidx + jn, :], pd[:, jn, :], [pd], [Dt])
                        bidx += len(Js)
                    fw.dma("pool", g.s5D[d, gg], Dt[:], reads=[Dt], writes=[g.s5D])
                    M2t = M2sb[gi % 2]
                    for ri, src in enumerate((m2r, m2i)):
                        for r in range(4):
                            fw.op("pe", lambda h, ri=ri, r=r, src=src: h.transpose(
                                psm[:, ri, r, :], src[:, gi, r * 128:(r + 1) * 128], g.ident[0:64, 0:64]),
                                [src, g.ident], [psm])
                    copy(fw, "act", M2t[:], psm[:], [psm], [M2t])
                    fw.dma("pool", g.s5M2[d, gg], M2t[:], reads=[M2t], writes=[g.s5M2])
        fw.end_phase(keep=g.keep)


def ks_scan(fw, X, Y, tA, tB, Lt, lo, hi, descending):
    n = hi - lo
    k = 0
    sh = 1
    while sh < n:
        m = n - sh
        if not descending:
            dst = slice(lo + sh, hi)
            src = slice(lo, hi - sh)
            keep = slice(lo, lo + sh)
        else:
            dst = slice(lo, hi - sh)
            src = slice(lo + sh, hi)
            keep = slice(hi - sh, hi)
        Lr = Lt[:, k, 0, :].unsqueeze(2).to_broadcast([128, 16, m])
        Li = Lt[:, k, 1, :].unsqueeze(2).to_broadcast([128, 16, m])
        xr, xi = X
        yr, yi = Y
        tt(fw, "dve", tA[:, :, 0:m], xr[:, :, src], Lr, ALU.mult, [xr, Lt], [tA])
        tt(fw, "pool", tB[:, :, 0:m], xi[:, :, src], Li, ALU.mult, [xi, Lt], [tB])
        tt(fw, "dve", tA[:, :, 0:m], tA[:, :, 0:m], tB[:, :, 0:m], ALU.subtract, [tA, tB], [tA])
        tt(fw, "dve", yr[:, :, dst], xr[:, :, dst], tA[:, :, 0:m], ALU.add, [xr, tA], [yr])
        copy(fw, "pool", yr[:, :, keep], xr[:, :, keep], [xr], [yr])
        tt(fw, "dve", tA[:, :, 0:m], xi[:, :, src], Lr, ALU.mult, [xi, Lt, yr], [tA])
        tt(fw, "pool", tB[:, :, 0:m], xr[:, :, src], Li, ALU.mult, [xr, Lt, yr], [tB])
        tt(fw, "dve", tA[:, :, 0:m], tA[:, :, 0:m], tB[:, :, 0:m], ALU.add, [tA, tB], [tA])
        tt(fw, "dve", yi[:, :, dst], xi[:, :, dst], tA[:, :, 0:m], ALU.add, [xi, tA], [yi])
        copy(fw, "pool", yi[:, :, keep], xi[:, :, keep], [xi], [yi])
        X, Y = Y, X
        sh *= 2
        k += 1
    return X, Y


def phase_s5_main(fw, g, l):
    import os
    stage = int(os.environ.get("S5_STAGE", "9"))
    with ExitStack() as st:
        selm = fw.sbuf(st, "m5selm", [128, 8, 240], BF16)
        fw.dma("sp", selm[:], g.cin["c_selm"], reads=[], writes=[selm])
        Dte = fw.sbuf(st, "m5Dte", [128, 64], F32)
        with fw.nc.allow_non_contiguous_dma(reason="tiny"):
            for t_ in range(8):
                fw.dma("sp", Dte[t_ * 16:(t_ + 1) * 16, :], g.inp["s5_d"][l, :].rearrange("(g e) -> e g", e=16),
                       reads=[], writes=[Dte])
        U32 = fw.sbuf(st, "m5U32", [128, 32, 4, NSC], BF16)
        uT = [fw.sbuf(st, f"m5uT{i}", [128, T], BF16) for i in range(2)]
        S = [(fw.sbuf(st, f"m5Sr{i}", [128, 16, NSC], F32), fw.sbuf(st, f"m5Si{i}", [128, 16, NSC], F32)) for i in range(2)]
        tA = fw.sbuf(st, "m5tA", [128, 16, NSC], F32)
        tB = fw.sbuf(st, "m5tB", [128, 16, NSC], F32)
        XPr = fw.sbuf(st, "m5XPr", [128, 16, NSC], BF16)
        XPi = fw.sbuf(st, "m5XPi", [128, 16, NSC], BF16)
        Lt = fw.sbuf(st, "m5Lt", [128, 8, 2, 16], F32)
        seed = fw.sbuf(st, "m5seed", [128, 4, 16], F32)
        Dl = [fw.sbuf(st, f"m5Dl{i}", [128, 10, 128], BF16) for i in range(2)]
        M2l = [[fw.sbuf(st, f"m5M2l{hf}_{i}", [128, 2, 4, 128], BF16) for i in range(2)] for hf in range(2)]
        Al = [[fw.sbuf(st, f"m5Al{hf}_{i}", [128, 2, 512], BF16) for i in range(2)] for hf in range(2)]
        for hf in range(2):
            for i in range(2):
                fw.op("pool", lambda h, b=M2l[hf][i]: h.memset(b[:], 0.0), [], [M2l[hf][i]])
                fw.op("pool", lambda h, b=Al[hf][i]: h.memset(b[:], 0.0), [], [Al[hf][i]])
        yfl = [fw.sbuf(st, f"m5yfl{i}", [128, 4, NSC], F32) for i in range(2)]
        yfs = [fw.sbuf(st, f"m5yfs{i}", [128, 4, NSC], F32) for i in range(2)]
        yt = fw.sbuf(st, "m5yt", [128, 4, NSC], F32)
        y2 = fw.sbuf(st, "m5y2", [128, 4, NSC], F32)
        Yg = fw.sbuf(st, "m5Yg", [128, 2, 8, 4, NSC], BF16)
        y5t = [fw.sbuf(st, f"m5y5t{i}", [128, T], BF16) for i in range(2)]
        ps_u = [fw.psum(st, f"m5psu{i}", [128, 2, NSC], F32) for i in range(2)]
        ps_s = [fw.psum(st, f"m5pss{i}", [128, 2, NSC], F32) for i in range(2)]
        ps_y = [fw.psum(st, f"m5psy{i}", [128, 2, NSC], F32) for i in range(2)]
        ps_r = [fw.psum(st, f"m5psr{i}", [128, NSC], F32) for i in range(2)]
        iu = 0
        il = 0
        isx = 0
        iy = 0
        ir = 0
        for gb in range(2):
            gbase = gb * 32
            for gi in range(32):
                gg = gbase + gi
                ci, gl = gg // 8, gg % 8
                ut = uT[ci % 2]
                if gl == 0:
                    fw.dma("sp", ut[:], g.uT[ci * 128:(ci + 1) * 128, :], reads=[g.uT], writes=[ut])
                u3 = ut[:].rearrange("p (c t) -> p c t", t=32)
                for rp in range(2):
                    pu = ps_u[iu % 2]
                    iu += 1
                    for r2 in range(2):
                        r = rp * 2 + r2
                        for tq in range(8):
                            mm(fw, pu[:, r2, :], selm[:, gl, 112 - tq * 16:240 - tq * 16], u3[:, :, r * 8 + tq],
                               tq == 0, tq == 7, [selm, ut], [pu])
                    copy(fw, "act" if rp == 0 else "dve", U32[:, gi, rp * 2:rp * 2 + 2, :], pu[:], [pu], [U32])
            if stage <= 1:
                continue
            for d in range(2):
                for hf in range(2):
                    fw.dma("sp", Lt[hf * 64:(hf + 1) * 64], g.s5L[d, :, :, :, gbase + hf * 16:gbase + hf * 16 + 16]
                           .rearrange("k r p g -> p k r g"), reads=[g.s5L], writes=[Lt])
                Sx, Sy = S
                for gj in range(16):
                    m2s = []
                    for hf in range(2):
                        gg = gbase + hf * 16 + gj
                        m2 = M2l[hf][il % 2]
                        if os.environ.get("S5_SKIP", "") != "dma":
                            fw.dma("sp", m2[:, :, :, hf * 64:(hf + 1) * 64], g.s5M2[d, gg], reads=[g.s5M2], writes=[m2])
                        m2s.append(m2)
                    il += 1
                    if os.environ.get("S5_SKIP", "") == "mm":
                        continue
                    ps = ps_s[isx % 2]
                    isx += 1
                    for ri in range(2):
                        for hf in range(2):
                            gi = hf * 16 + gj
                            for r in range(4):
                                mm(fw, ps[:, ri, :], m2s[hf][:, ri, r, :], U32[:, gi, r, :], hf == 0 and r == 0,
                                   hf == 1 and r == 3, [m2s[hf], U32], [ps])
                    copy(fw, "act", Sx[0][:, gj, :], ps[:, 0, :], [ps], [Sx[0]])
                    copy(fw, "act", Sx[1][:, gj, :], ps[:, 1, :], [ps], [Sx[1]])
                if stage <= 2:
                    continue
                if d == 0:
                    Xf, Yo = ks_scan(fw, Sx, Sy, tA, tB, Lt, 0, NSC, False)
                    for (src, dstb) in ((Xf[0], XPr), (Xf[1], XPi)):
                        fw.op("pool", lambda h, dstb=dstb: h.memset(dstb[:, :, 0:1], 0.0), [], [dstb])
                        copy(fw, "act", dstb[:, :, 1:NSC], src[:, :, 0:NSC - 1], [src], [dstb])
                else:
                    Xc, Yc = ks_scan(fw, Sx, Sy, tA, tB, Lt, 0, NCC, True)
                    if Xc[0] is not Sx[0]:
                        copy(fw, "act", Sx[0][:, :, 0:NCC], Xc[0][:, :, 0:NCC], [Xc[0]], [Sx[0]])
                        copy(fw, "act", Sx[1][:, :, 0:NCC], Xc[1][:, :, 0:NCC], [Xc[1]], [Sx[1]])
                    Xc, Yc = Sx, Sy
                    xr0, xi0 = Xc[0][:, :, 0], Xc[1][:, :, 0]
                    cmul(fw, seed[:, 0, :], seed[:, 1, :], Lt[:, 0, 0, :], Lt[:, 0, 1, :], xr0, xi0, seed[:, 2, :], seed[:, 3, :],
                         [Lt, Xc[0], Xc[1]], [seed], seed, seed)
                    tt(fw, "dve", Xc[0][:, :, NSC - 1], Xc[0][:, :, NSC - 1], seed[:, 0, :], ALU.add, [Xc[0], seed], [Xc[0]])
                    tt(fw, "dve", Xc[1][:, :, NSC - 1], Xc[1][:, :, NSC - 1], seed[:, 1, :], ALU.add, [Xc[1], seed], [Xc[1]])
                    Xf, Yo = ks_scan(fw, Xc, Yc, tA, tB, Lt, NCC, NSC, True)
                    for (srcl, srcc, dstb) in ((Xf[0], Xc[0], XPr), (Xf[1], Xc[1], XPi)):
                        copy(fw, "act", dstb[:, :, 0:NCC - 1], srcc[:, :, 1:NCC], [srcc], [dstb])
                        fw.op("pool", lambda h, dstb=dstb: h.memset(dstb[:, :, NCC - 1:NCC], 0.0), [], [dstb])
                        copy(fw, "act", dstb[:, :, NCC:NSC - 1], srcl[:, :, NCC + 1:NSC], [srcl], [dstb])
                        copy(fw, "act", dstb[:, :, NSC - 1:NSC], srcc[:, :, 0:1], [srcc], [dstb])
                if stage <= 3:
                    continue
                for gj in range(16):
                    for hf in range(2):
                        gi = hf * 16 + gj
                        gg = gbase + gi
                        b = iy % 2
                        iy += 1
                        Dm = Dl[b]
                        Am = Al[hf][(iy // 2) % 2]
                        fw.dma("sp", Dm[:], g.s5D[d, gg], reads=[g.s5D], writes=[Dm])
                        fw.dma("sp", Am[hf * 64:(hf + 1) * 64], g.s5A[d, gg], reads=[g.s5A], writes=[Am])
                        if d == 1:
                            fw.dma("sp", yfl[b][:], g.y5f_d[gg], reads=[g.y5f_d], writes=[yfl[b]])
                        bidx = 0
                        pys = []
                        for I in range(4):
                            Js = list(range(0, I + 1)) if d == 0 else list(range(I, 4))
                            py = ps_y[(iy * 2 + I // 2) % 2] if False else ps_y[I // 2]
                            for jn, J in enumerate(Js):
                                mm(fw, py[:, I % 2, :], Dm[:, bidx + jn, :], U32[:, gi, J, :], jn == 0, False, [Dm, U32], [py])
                            bidx += len(Js)
                            mm(fw, py[:, I % 2, :], Am[:, 0, I * 128:(I + 1) * 128], XPr[:, gj, :], False, False,
                               [Am, XPr], [py])
                            mm(fw, py[:, I % 2, :], Am[:, 1, I * 128:(I + 1) * 128], XPi[:, gj, :], False, True,
                               [Am, XPi], [py])
                        if d == 0:
                            ys_ = yfs[b]
                            copy(fw, "act", ys_[:, 0:2, :], ps_y[0][:], [ps_y[0]], [ys_])
                            copy(fw, "dve", ys_[:, 2:4, :], ps_y[1][:], [ps_y[1]], [ys_])
                            fw.dma("pool", g.y5f_d[gg], ys_[:], reads=[ys_], writes=[g.y5f_d])
                        else:
                            gl = gg % 8
                            tt(fw, "dve", yt[:, 0:2, :], ps_y[0][:], yfl[b][:, 0:2, :], ALU.add, [ps_y[0], yfl[b]], [yt])
                            tt(fw, "dve", yt[:, 2:4, :], ps_y[1][:], yfl[b][:, 2:4, :], ALU.add, [ps_y[1], yfl[b]], [yt])
                            stt(fw, yt[:], U32[:, gi, :, :], Dte[:, gg:gg + 1], yt[:], ALU.mult, ALU.add, [U32, Dte, yt], [yt])
                            act(fw, y2[:], yt[:], AF.Square, [yt], [y2])
                            ts(fw, "dve", y2[:], y2[:], 0.044715, 1.0, ALU.mult, ALU.add, [y2], [y2])
                            tt(fw, "pool", y2[:], y2[:], yt[:], ALU.mult, [y2, yt], [y2])
                            act(fw, y2[:], y2[:], AF.Sigmoid, [y2], [y2], scale=2.0 * math.sqrt(2.0 / math.pi))
                            tt(fw, "dve", Yg[:, hf, gl, :, :], y2[:], yt[:], ALU.mult, [y2, yt], [Yg])
                            if gl == 7 and hf == 1:
                              for hf2 in range(2):
                                ci = (gbase + hf2 * 16 + gj) // 8
                                yo_ = y5t[hf2]
                                y3 = yo_[:].rearrange("p (c t) -> p c t", t=32)
                                for I in range(4):
                                    for tq in range(8):
                                        pr_ = ps_r[ir % 2]
                                        ir += 1
                                        for g8 in range(8):
                                            mm(fw, pr_[:], selm[:, tq, 112 - g8 * 16:240 - g8 * 16], Yg[:, hf2, g8, I, :],
                                               g8 == 0, g8 == 7, [selm, Yg], [pr_])
                                        copy(fw, "act" if tq % 2 == 0 else "dve", y3[:, :, I * 8 + tq], pr_[:], [pr_], [yo_])
                                fw.dma("pool", g.y5g[ci * 128:(ci + 1) * 128, :], yo_[:], reads=[yo_], writes=[g.y5g])
        fw.end_phase(keep=g.keep)


def phase_s5_glu(fw, g, l, W):
    with ExitStack() as st:
        wgl = fw.sbuf(st, "gwgl", [128, 4, 8, 512], BF16)
        for cb in range(4):
            fw.dma("sp", wgl[:, cb], W.glu[cb], reads=[W.glu], writes=[wgl])
        gbias = fw.sbuf(st, "gbias", [128, 16], F32)
        with fw.nc.allow_non_contiguous_dma(reason="tiny"):
            fw.dma("sp", gbias[:], g.inp["s5_glu_b"][l, :].rearrange("(t p) -> p t", p=128), reads=[], writes=[gbias])
        yin = [fw.sbuf(st, f"gyin{i}", [128, 8, 512], BF16) for i in range(2)]
        sgt = [fw.sbuf(st, f"gsg{i}", [128, 512], F32) for i in range(2)]
        yo = [fw.sbuf(st, f"gyo{i}", [128, 8, 512], BF16) for i in range(2)]
        ps_a = [fw.psum(st, f"gpsa{i}", [128, 512], F32) for i in range(2)]
        ps_b = [fw.psum(st, f"gpsb{i}", [128, 512], F32) for i in range(2)]
        i2 = 0
        for bi, (t0, n) in enumerate(tblocks(l)):
            yi_ = yin[bi % 2]
            yo_ = yo[bi % 2]
            fw.dma("sp", yi_[:, :, 0:n], g.y5g[:, t0:t0 + n].rearrange("(k p) t -> p k t", p=128), reads=[g.y5g], writes=[yi_])
            for ot in range(8):
                pa = ps_a[i2 % 2]
                pb = ps_b[i2 % 2]
                sg = sgt[i2 % 2]
                i2 += 1
                for k in range(8):
                    mm(fw, pa[:, 0:n], wgl[:, ot // 4, k, (ot % 4) * 128:(ot % 4 + 1) * 128], yi_[:, k, 0:n], k == 0, k == 7,
                       [wgl, yi_], [pa])
                for k in range(8):
                    mm(fw, pb[:, 0:n], wgl[:, 2 + ot // 4, k, (ot % 4) * 128:(ot % 4 + 1) * 128], yi_[:, k, 0:n], k == 0, k == 7,
                       [wgl, yi_], [pb])
                act(fw, sg[:, 0:n], pb[:, 0:n], AF.Sigmoid, [pb, gbias], [sg], bias=gbias[:, 8 + ot:9 + ot])
                stt(fw, yo_[:, ot, 0:n], pa[:, 0:n], gbias[:, ot:ot + 1], sg[:, 0:n], ALU.add, ALU.mult, [pa, gbias, sg], [yo_])
            fw.dma("pool", g.y5T[:, t0:t0 + n].rearrange("(k p) t -> p k t", p=128), yo_[:, :, 0:n], reads=[yo_], writes=[g.y5T])
        fw.end_phase(keep=g.keep)
```
